# Optimizing a Trainium2 kernel written in Bass

```python
import jax, jax.numpy as jnp
from jax import lax
import numpy as np

D_MODEL = 1024
BATCH = 8
SEQ = 2048
DEPTH = 2
DEC_BATCH = 128
DEC_SEQ = 8
PAST_LEN = 2048
PAGE_SIZE = 128

N_A = DEPTH // 2
N_B = DEPTH - N_A
MIX_WIDTH = D_MODEL
TOK_WIDTH = MIX_WIDTH // 2
MEM_WIDTH = MIX_WIDTH - TOK_WIDTH
POOL_WINDOWS = (2, 4, 8, 16)
N_POOL_GROUPS = len(POOL_WINDOWS)
POOL_GROUP = TOK_WIDTH // N_POOL_GROUPS
POOL_STATE = max(POOL_WINDOWS) - 1
FOX_HEAD_DIM = 64
FOX_HEADS = TOK_WIDTH // FOX_HEAD_DIM
MEM_HEADS = 4
MEM_HEAD_DIM = MEM_WIDTH // MEM_HEADS
N_MEM = 256
D_FF = -(-8 * D_MODEL // (3 * 256)) * 256
QBLOCK = 128
EPS = 1e-6
FGATE_BIAS = 3.0

kernel_name = "yoco_pool_fox_memory_decoder_step"


def rmsnorm(x, g):
    xf = x.astype(jnp.float32)
    y = xf * lax.rsqrt(jnp.mean(xf * xf, axis=-1, keepdims=True) + EPS)
    return (y * g.astype(jnp.float32)).astype(x.dtype)


def swiglu(x, w_gu, w_down):
    g, u = jnp.split(x @ w_gu, 2, axis=-1)
    return (jax.nn.silu(g) * u) @ w_down


def pool_mix(u_prev, u_new, w_pool, scale):
    B, S, _ = u_new.shape
    P = u_prev.shape[1]
    u_ext = jnp.concatenate([u_prev, u_new], axis=1)
    c = jnp.cumsum(u_ext.astype(jnp.float32), axis=1)
    c = jnp.pad(c, ((0, 0), (1, 0), (0, 0))).reshape(B, P + S + 1, N_POOL_GROUPS, POOL_GROUP)
    hi = P + 1 + jnp.arange(S)
    win = jnp.array(POOL_WINDOWS, dtype=jnp.int32)
    lo = jnp.maximum(hi[:, None] - win[None, :], 0)
    c_hi = c[:, hi]
    c_lo = c[:, lo, jnp.arange(N_POOL_GROUPS)[None, :]]
    count = (hi[:, None] - lo).astype(jnp.float32)[..., None]
    y = (c_hi - c_lo) / count - u_new.reshape(B, S, N_POOL_GROUPS, POOL_GROUP).astype(jnp.float32)
    z = jnp.einsum('bsgc,gcd->bsgd', y.astype(u_new.dtype), w_pool).reshape(B, S, TOK_WIDTH)
    return z * scale, u_ext[:, -POOL_STATE:]


def fox_attention(q, k, v, f_q, f_k, q_pos, k_pos):
    B, Sq, H, Dh = q.shape
    blk = QBLOCK if Sq % QBLOCK == 0 else Sq
    nb = Sq // blk
    scale = Dh ** -0.5
    qb = q.reshape(B, nb, blk, H, Dh).transpose(1, 0, 2, 3, 4)
    fqb = f_q.reshape(B, nb, blk, H).transpose(1, 0, 3, 2)
    pb = q_pos.reshape(nb, blk)
    fk_t = f_k.transpose(0, 2, 1)

    def one_block(args):
        qi, fqi, pi = args
        s = jnp.einsum('bqhd,bkhd->bhqk', qi, k).astype(jnp.float32) * scale
        s = s + (fqi[..., :, None] - fk_t[..., None, :])
        mask = k_pos[None, :] <= pi[:, None]
        p = jax.nn.softmax(jnp.where(mask, s, -jnp.inf), axis=-1)
        return jnp.einsum('bhqk,bkhd->bqhd', p.astype(v.dtype), v)

    out = lax.map(one_block, (qb, fqb, pb))
    return out.transpose(1, 0, 2, 3, 4).reshape(B, Sq, H, Dh)


def mem_attention(q, mk, mv):
    s = jnp.einsum('bqhd,bmhd->bhqm', q, mk).astype(jnp.float32) * (q.shape[-1] ** -0.5)
    p = jax.nn.softmax(s, axis=-1)
    return jnp.einsum('bhqm,bmhd->bqhd', p.astype(mv.dtype), mv)


def memory_kv(mem, g_mem, w_mem_kv, g_mem_k):
    B, M, _ = mem.shape
    k, v = jnp.split(rmsnorm(mem, g_mem) @ w_mem_kv, 2, axis=-1)
    k = rmsnorm(k.reshape(B, M, MEM_HEADS, MEM_HEAD_DIM), g_mem_k)
    return k, v.reshape(B, M, MEM_HEADS, MEM_HEAD_DIM)


def shared_kv(h, g_kv, w_kv, g_fox_k, w_fg, b_fg):
    B, S, _ = h.shape
    hn = rmsnorm(h, g_kv)
    k, v = jnp.split(hn @ w_kv, 2, axis=-1)
    k = rmsnorm(k.reshape(B, S, FOX_HEADS, FOX_HEAD_DIM), g_fox_k)
    v = v.reshape(B, S, FOX_HEADS, FOX_HEAD_DIM)
    logf = jax.nn.log_sigmoid((hn @ w_fg + b_fg).astype(jnp.float32))
    return k, v, logf


def decoder(x, q_pos, pool_prev, mem_k, mem_v, past, g_mix, w_in, w_out, g_ffn, w_gu, w_down,
            g_mem_q, w_pool, pool_scale, g_kv, w_kv, g_fox_k, w_fg, b_fg, g_fox_q):
    B, S, _ = x.shape
    h = x
    pool_states = []
    for l in range(DEPTH):
        z = rmsnorm(h, g_mix[l]) @ w_in[l]
        z_tok, z_mem = z[..., :TOK_WIDTH], z[..., TOK_WIDTH:]
        if l < N_A:
            tok, st = pool_mix(pool_prev[l], z_tok, w_pool[l], pool_scale[l])
            pool_states.append(st)
        else:
            q = rmsnorm(z_tok.reshape(B, S, FOX_HEADS, FOX_HEAD_DIM), g_fox_q[l - N_A])
            tok = fox_attention(q, k_all, v_all, f_q, f_all, q_pos, k_pos).reshape(B, S, TOK_WIDTH)
        qm = rmsnorm(z_mem.reshape(B, S, MEM_HEADS, MEM_HEAD_DIM), g_mem_q[l])
        mo = mem_attention(qm, mem_k[l], mem_v[l]).reshape(B, S, MEM_WIDTH)
        h = h + jnp.concatenate([tok, mo], axis=-1) @ w_out[l]
        h = h + swiglu(rmsnorm(h, g_ffn[l]), w_gu[l], w_down[l])
        if l == N_A - 1:
            k_new, v_new, logf_new = shared_kv(h, g_kv, w_kv, g_fox_k, w_fg, b_fg)
            if past is None:
                k_all, v_all, logf_all = k_new, v_new, logf_new
                k_pos = q_pos
            else:
                k_past, v_past, logf_past = past
                k_all = jnp.concatenate([k_past.astype(k_new.dtype), k_new], axis=1)
                v_all = jnp.concatenate([v_past.astype(v_new.dtype), v_new], axis=1)
                logf_all = jnp.concatenate([logf_past.astype(jnp.float32), logf_new], axis=1)
                k_pos = jnp.arange(k_all.shape[1])
            f_all = jnp.cumsum(logf_all, axis=1)
            f_q = f_all[:, -S:]
    return h, jnp.stack(pool_states), k_new, v_new, logf_new.astype(x.dtype)


def setup_inputs(seed: int = 0) -> dict:
    key = jax.random.key(seed)
    ks = iter(jax.random.split(key, 40))
    nrm = lambda shape, s=1.0: jax.random.normal(next(ks), shape, jnp.float32) * s
    gain = lambda shape: 1.0 + 0.1 * jax.random.normal(next(ks), shape, jnp.float32)
    n_pages = PAST_LEN // PAGE_SIZE
    n_phys = -(-5 * DEC_BATCH * n_pages // 4)
    page_table = jax.random.permutation(next(ks), n_phys)[:DEC_BATCH * n_pages]
    page_table = page_table.reshape(DEC_BATCH, n_pages).astype(jnp.int32)
    return {
        "x_prompt": nrm((BATCH, SEQ, D_MODEL)),
        "x_sample": nrm((DEC_BATCH, DEC_SEQ, D_MODEL)),
        "cache_mem_k": nrm((DEPTH, DEC_BATCH, N_MEM, MEM_HEADS, MEM_HEAD_DIM)),
        "cache_mem_v": nrm((DEPTH, DEC_BATCH, N_MEM, MEM_HEADS, MEM_HEAD_DIM)),
        "state_pool": nrm((N_A, DEC_BATCH, POOL_STATE, TOK_WIDTH)),
        "cache_k": nrm((n_phys, PAGE_SIZE, FOX_HEADS, FOX_HEAD_DIM)),
        "cache_v": nrm((n_phys, PAGE_SIZE, FOX_HEADS, FOX_HEAD_DIM)),
        "cache_logf": jax.nn.log_sigmoid(FGATE_BIAS + nrm((n_phys, PAGE_SIZE, FOX_HEADS))),
        "page_table": page_table,
        "mem_prompt": nrm((BATCH, N_MEM, D_MODEL)),
        "g_mix": gain((DEPTH, D_MODEL)),
        "w_in": nrm((DEPTH, D_MODEL, MIX_WIDTH), D_MODEL ** -0.5),
        "w_out": nrm((DEPTH, MIX_WIDTH, D_MODEL), MIX_WIDTH ** -0.5),
        "g_ffn": gain((DEPTH, D_MODEL)),
        "w_gu": nrm((DEPTH, D_MODEL, 2 * D_FF), D_MODEL ** -0.5),
        "w_down": nrm((DEPTH, D_FF, D_MODEL), D_FF ** -0.5),
        "g_mem": gain((DEPTH, D_MODEL)),
        "w_mem_kv": nrm((DEPTH, D_MODEL, 2 * MEM_WIDTH), D_MODEL ** -0.5),
        "g_mem_q": gain((DEPTH, MEM_HEAD_DIM)),
        "g_mem_k": gain((DEPTH, MEM_HEAD_DIM)),
        "w_pool": nrm((N_A, N_POOL_GROUPS, POOL_GROUP, POOL_GROUP), POOL_GROUP ** -0.5),
        "pool_scale": gain((N_A, TOK_WIDTH)),
        "g_kv": gain((D_MODEL,)),
        "w_kv": nrm((D_MODEL, 2 * TOK_WIDTH), D_MODEL ** -0.5),
        "g_fox_k": gain((FOX_HEAD_DIM,)),
        "w_fg": nrm((D_MODEL, FOX_HEADS), D_MODEL ** -0.5),
        "b_fg": FGATE_BIAS + nrm((FOX_HEADS,), 0.1),
        "g_fox_q": gain((N_B, FOX_HEAD_DIM)),
    }


def reference(x_prompt, x_sample, cache_mem_k, cache_mem_v, state_pool, cache_k, cache_v, cache_logf,
              page_table, mem_prompt, g_mix, w_in, w_out, g_ffn, w_gu, w_down, g_mem, w_mem_kv,
              g_mem_q, g_mem_k, w_pool, pool_scale, g_kv, w_kv, g_fox_k, w_fg, b_fg, g_fox_q):
    weights = (g_mix, w_in, w_out, g_ffn, w_gu, w_down, g_mem_q, w_pool, pool_scale,
               g_kv, w_kv, g_fox_k, w_fg, b_fg, g_fox_q)
    B, S, _ = x_prompt.shape
    mem_kv = [memory_kv(mem_prompt, g_mem[l], w_mem_kv[l], g_mem_k[l]) for l in range(DEPTH)]
    mem_k_prompt = jnp.stack([kv[0] for kv in mem_kv])
    mem_v_prompt = jnp.stack([kv[1] for kv in mem_kv])
    pool_empty = jnp.zeros((N_A, B, 0, TOK_WIDTH), x_prompt.dtype)
    y_prompt, pool_state_prompt, k_p, v_p, logf_p = decoder(
        x_prompt, jnp.arange(S), pool_empty, mem_k_prompt, mem_v_prompt, None, *weights)
    n_pp = S // PAGE_SIZE
    k_rows_prompt = k_p.reshape(B, n_pp, PAGE_SIZE, FOX_HEADS, FOX_HEAD_DIM)
    v_rows_prompt = v_p.reshape(B, n_pp, PAGE_SIZE, FOX_HEADS, FOX_HEAD_DIM)
    logf_rows_prompt = logf_p.reshape(B, n_pp, PAGE_SIZE, FOX_HEADS)
    DB, DS, _ = x_sample.shape
    n_pages = page_table.shape[1]
    past_len = n_pages * PAGE_SIZE

    def gather(c):
        g = c[page_table]
        return g.reshape((DB, past_len) + c.shape[2:])

    past = (gather(cache_k), gather(cache_v), gather(cache_logf))
    y_sample, pool_state_sample, k_rows_sample, v_rows_sample, logf_rows_sample = decoder(
        x_sample, past_len + jnp.arange(DS), state_pool, cache_mem_k, cache_mem_v, past, *weights)
    return (y_prompt, y_sample, k_rows_prompt, v_rows_prompt, logf_rows_prompt, mem_k_prompt,
            mem_v_prompt, pool_state_prompt, k_rows_sample, v_rows_sample, logf_rows_sample,
            pool_state_sample)
```

```python
import os
import numpy as np
import concourse.bass as bass
import concourse.mybir as mybir
from concourse.bass_utils import run_bass_kernel_spmd

F32 = mybir.dt.float32
BF16 = mybir.dt.bfloat16
I32 = mybir.dt.int32
AF = mybir.ActivationFunctionType
ALU = mybir.AluOpType
AX = mybir.AxisListType

D = 1024
SEQ = 2048
NB = 8
DB = 128
DS = 8
BPC = DB // NB
NSAMP = BPC * DS
DFF = 2816
NJ = DFF // 128
NPAGES = 16
PAGE = 128
NPHYS = 2560
NMEM = 256
EPS = 1e-6
TILES = [(0, 512), (512, 512), (1024, 512), (1536, 512), (2048, 128)]
NT = len(TILES)

V_GMIX, V_GFFN, V_GMEM, V_GKV, V_GMQ, V_PSC, V_GFQ, V_GMK, V_GFK, V_BFG, NV = 0, 16, 32, 48, 56, 58, 62, 64, 320, 384, 392
C_ID, C_ONES, C_BD, C_UT, C_E127, C_GE, C_BT, C_INVC, C_MASKS, NCST = 0, 128, 256, 384, 512, 640, 768, 896, 960, 1088


class Buf:
    __slots__ = ("name", "space", "lo", "hi", "w", "r", "ov", "dma")

    def __init__(self, name, space, lo, hi):
        self.name, self.space, self.lo, self.hi = name, space, lo, hi
        self.w = None
        self.r = {}
        self.ov = None
        self.dma = False


class A:
    __slots__ = ("ap", "bufs")

    def __init__(self, ap, bufs):
        self.ap, self.bufs = ap, bufs


class TT:
    def __init__(self, h, buf):
        self.h, self.buf = h, buf

    def __getitem__(self, idx):
        return A(self.h[idx], [self.buf])

    def v(self, fn):
        return A(fn(self.h), [self.buf])


class Op:
    __slots__ = ("eng", "fn", "key", "pos", "signal", "waits", "is_dma", "val", "fill")

    def __init__(self, eng, fn, key, is_dma):
        self.eng, self.fn, self.key, self.is_dma = eng, fn, key, is_dma
        self.pos = 0
        self.signal = is_dma
        self.waits = {}
        self.val = 0
        self.fill = False


class K:
    ENGS = ("pe", "act", "dve", "pool", "sp")

    def __init__(self, nc):
        self.nc = nc
        self.q = {e: [] for e in self.ENGS}
        self.bufs = []
        self.chan_n = {}
        self.out_chans = set()
        self.sb_lo = (nc.sbuf_base + 63) // 64 * 64
        self.sb_hi = nc.sbuf_top
        self.sb_ptr = self.sb_lo
        self.banks = []
        self.free_banks = []
        self.fill = False
        self.filler = None

    def sb(self, name, shape, dt, off=None):
        nbytes = int(np.prod(shape[1:])) * mybir.dt.size(dt)
        if off is None:
            off = self.sb_ptr
            self.sb_ptr = (off + nbytes + 31) // 32 * 32
        assert off + nbytes <= self.sb_hi, (name, off, nbytes, self.sb_hi)
        h = self.nc.alloc_sbuf_tensor_at(name, list(shape), dt, offset=off)
        b = Buf(name, "sb", off, off + nbytes)
        self.bufs.append(b)
        for x in self.bufs:
            x.ov = None
        return TT(h, b)

    def make_banks(self):
        for i in range(8):
            h = self.nc.alloc_psum_tensor("bank%d" % i, [128, 512], F32)
            b = Buf("bank%d" % i, "ps", i, i + 1)
            self.bufs.append(b)
            self.banks.append(TT(h, b))
        self.free_banks = list(self.banks[:7])

    def bank(self):
        assert self.free_banks, "out of PSUM banks"
        return self.free_banks.pop(0)

    def release(self, bk):
        self.free_banks.append(bk)

    def _ov(self, b):
        if b.ov is None:
            b.ov = [x for x in self.bufs if x.space == b.space and x.lo < b.hi and b.lo < x.hi]
        return b.ov

    @staticmethod
    def _dep(op, prod):
        if prod is None or prod is op:
            return
        if prod.eng == "pe" and op.eng == "pe" and not prod.is_dma:
            return
        cur = op.waits.get(prod.key)
        if cur is None or cur.pos < prod.pos:
            op.waits[prod.key] = prod

    def _add(self, eng, fn, R, W, is_dma=False, chan=None):
        if is_dma:
            key = chan
            op = Op(eng, fn, key, True)
            n = self.chan_n.get(key, 0) + 1
            self.chan_n[key] = n
            op.pos = n
        else:
            op = Op(eng, fn, eng, False)
            op.pos = len(self.q[eng]) + 1
        for b in R:
            for x in self._ov(b):
                self._dep(op, x.w)
                if b.space == "ps":
                    for rr in x.r.values():
                        if rr.eng != eng:
                            self._dep(op, rr)
        for b in W:
            for x in self._ov(b):
                self._dep(op, x.w)
                for rr in x.r.values():
                    self._dep(op, rr)
        rkey = op.key if is_dma else eng
        for b in R:
            b.r[rkey] = op
        for b in W:
            b.w = op
            b.r = {}
        op.fill = self.fill
        self.q[eng].append(op)
        return op

    def op(self, eng, fn, R=(), W=()):
        rb = [b for a in R for b in a.bufs]
        wb = [b for a in W for b in a.bufs]
        return self._add(eng, fn, rb, wb)

    def mm(self, out, lhsT, rhs, start=True, stop=True, sgc=False):
        return self.op("pe", lambda e: e.matmul(out.ap, lhsT=lhsT.ap, rhs=rhs.ap, start=start, stop=stop,
                                                skip_group_check=sgc),
                       R=[lhsT, rhs], W=[out])

    def tr(self, out, in_, ident):
        return self.op("pe", lambda e: e.transpose(out.ap, in_.ap, ident.ap), R=[in_, ident], W=[out])

    def act(self, out, in_, func, scale=1.0, bias=None, eng="act"):
        R = [in_]
        kw = {}
        if isinstance(scale, A):
            R.append(scale)
            kw["scale"] = scale.ap
        else:
            kw["scale"] = float(scale)
        if isinstance(bias, A):
            R.append(bias)
            kw["bias"] = bias.ap
        elif bias is not None:
            kw["bias"] = float(bias)
        return self.op("act", lambda e: e.activation(out=out.ap, in_=in_.ap, func=func, **kw), R=R, W=[out])

    def tt(self, eng, out, in0, in1, op):
        return self.op(eng, lambda e: e.tensor_tensor(out=out.ap, in0=in0.ap, in1=in1.ap, op=op),
                       R=[in0, in1], W=[out])

    def ts(self, eng, out, in0, s1, op0, s2=None, op1=None):
        R = [in0]
        a1 = s1
        if isinstance(s1, A):
            R.append(s1)
            a1 = s1.ap
        a2 = s2
        if isinstance(s2, A):
            R.append(s2)
            a2 = s2.ap
        if op1 is None:
            return self.op(eng, lambda e: e.tensor_scalar(out=out.ap, in0=in0.ap, scalar1=a1, scalar2=None, op0=op0),
                           R=R, W=[out])
        return self.op(eng, lambda e: e.tensor_scalar(out=out.ap, in0=in0.ap, scalar1=a1, scalar2=a2, op0=op0, op1=op1),
                       R=R, W=[out])

    def stt(self, out, in0, scalar, in1, op0, op1):
        R = [in0, in1]
        sc = scalar
        if isinstance(scalar, A):
            R.append(scalar)
            sc = scalar.ap
        return self.op("dve", lambda e: e.scalar_tensor_tensor(out=out.ap, in0=in0.ap, scalar=sc, in1=in1.ap,
                                                               op0=op0, op1=op1), R=R, W=[out])

    def copy(self, eng, out, in_):
        if eng == "act":
            return self.op("act", lambda e: e.copy(out=out.ap, in_=in_.ap), R=[in_], W=[out])
        return self.op(eng, lambda e: e.tensor_copy(out=out.ap, in_=in_.ap), R=[in_], W=[out])

    def recip(self, out, in_):
        return self.op("dve", lambda e: e.reciprocal(out=out.ap, in_=in_.ap), R=[in_], W=[out])

    def red(self, out, in_, op=ALU.add, axis=AX.X):
        return self.op("dve", lambda e: e.tensor_reduce(out=out.ap, in_=in_.ap, op=op, axis=axis), R=[in_], W=[out])

    def memset(self, eng, out, val):
        return self.op(eng, lambda e: e.memset(out.ap, val), R=[], W=[out])

    def dma(self, q, out, in_, is_out=False, **kw):
        R, W = [], []
        o_ap, i_ap = out, in_
        chan = None
        if isinstance(out, A):
            W = list(out.bufs)
            o_ap = out.ap
            chan = out.bufs[0].name
        if isinstance(in_, A):
            R = list(in_.bufs)
            i_ap = in_.ap
            if chan is None:
                chan = in_.bufs[0].name
        if is_out:
            self.out_chans.add(chan)
        return self._add(q, lambda e: e.dma_start(out=o_ap, in_=i_ap, **kw), R, W, is_dma=True, chan=chan)

    def gather(self, out, in_ap, idx):
        chan = out.bufs[0].name
        return self._add("pool", lambda e: e.indirect_dma_start(
            out=out.ap, out_offset=None, in_=in_ap,
            in_offset=bass.IndirectOffsetOnAxis(ap=idx.ap, axis=0)),
            list(idx.bufs), list(out.bufs), is_dma=True, chan=chan)

    def emit(self):
        nc = self.nc
        for e in self.ENGS:
            for op in self.q[e]:
                for prod in op.waits.values():
                    prod.signal = True
        sems = {}
        for e in ("pe", "act", "dve", "pool"):
            sems[e] = nc.alloc_semaphore("s_" + e)
        for ch in self.chan_n:
            sems[ch] = nc.alloc_semaphore("d_" + ch)
        for e in self.ENGS:
            n = 0
            for op in self.q[e]:
                if op.is_dma:
                    op.val = 16 * op.pos
                elif op.signal:
                    n += 1
                    op.val = n
        engobj = {"pe": "tensor", "act": "scalar", "dve": "vector", "pool": "gpsimd", "sp": "sync"}
        out_chans = sorted(self.out_chans)
        chan_n = self.chan_n
        q = self.q
        stats = {}
        with nc.Block() as block:
            def body(ename):
                def f(e):
                    known = {}
                    nw = 0
                    for op in q[ename]:
                        filled = False
                        for key, prod in op.waits.items():
                            v = prod.val
                            if known.get(key, 0) >= v:
                                continue
                            if ename == "pe" and op.fill and not filled and self.filler is not None:
                                self.filler(e)
                                filled = True
                            e.wait_ge(sems[key], v)
                            known[key] = v
                            nw += 1
                        inst = op.fn(e)
                        if op.signal:
                            inst.then_inc(sems[op.key], 16 if op.is_dma else 1)
                    if ename == "sp":
                        for ch in out_chans:
                            e.wait_ge(sems[ch], 16 * chan_n[ch])
                    stats[ename] = (len(q[ename]), nw)
                return f
            block.tensor(body("pe"))
            block.scalar(body("act"))
            block.vector(body("dve"))
            block.gpsimd(body("pool"))
            block.sync(body("sp"))
        if os.environ.get("KSTATS"):
            print("KSTATS", stats, "nsem", len(sems))


def build_program(stage=99, nphys=NPHYS):
    nc = bass.Bass("TRN2", target_bir_lowering=False)
    k = K(nc)

    def din(name, shape, dt=F32):
        return nc.dram_tensor(name, list(shape), dt, kind="ExternalInput").ap()

    def dout(name, shape, dt=F32):
        return nc.dram_tensor(name, list(shape), dt, kind="ExternalOutput").ap()

    xp = din("xp", [SEQ, D])
    xs = din("xs", [NSAMP, D])
    memp = din("memp", [NMEM, D])
    cmk = din("cmk", [2, BPC, NMEM, 512])
    cmv = din("cmv", [2, BPC, NMEM, 512])
    spool = din("spool", [BPC * 15, 512])
    ck = din("ck", [nphys * 32, 2048])
    cv = din("cv", [nphys * 32, 2048])
    clf = din("clf", [nphys * 8, 16 * 8])
    ptab = din("ptab", [128, BPC], I32)
    w_in = din("w_in", [2, D, D])
    w_out = din("w_out", [2, D, D])
    w_gu = din("w_gu", [2, D, 2 * DFF])
    w_down = din("w_down", [2, DFF, D])
    w_mkv = din("w_mkv", [2, D, D])
    w_pool = din("w_pool", [4, 128, 128])
    w_kv = din("w_kv", [D, D])
    w_fg = din("w_fg", [D, 8])
    vecs_d = din("vecs", [128, NV])
    cst_d = din("cst", [128, NCST])
    cint_d = din("cint", [128, 1], I32)

    y_p = dout("y_p", [SEQ, D])
    y_s = dout("y_s", [NSAMP, D])
    krp = dout("krp", [SEQ, 512])
    vrp = dout("vrp", [SEQ, 512])
    lfp = dout("lfp", [SEQ, 8])
    mkp = dout("mkp", [2, NMEM, 512])
    mvp = dout("mvp", [2, NMEM, 512])
    psp = dout("psp", [15, 512])
    krs = dout("krs", [NSAMP, 512])
    vrs = dout("vrs", [NSAMP, 512])
    lfs = dout("lfs", [NSAMP, 8])
    pss = dout("pss", [BPC * 15, 512])

    k.make_banks()

    cst = k.sb("cst", [128, NCST], F32)
    vecs = k.sb("vecs", [128, NV], F32)
    cint = k.sb("cint", [128, 1], I32)
    cb = k.sb("cb", [128, 4 * 128], BF16)
    masks_b = k.sb("masks_b", [128, 128], BF16)
    hT = [k.sb("hT%d" % i, [128, 8, n], F32) for i, (_, n) in enumerate(TILES)]
    xn = [k.sb("xn%d" % i, [128, 8, n], BF16) for i, (_, n) in enumerate(TILES)]
    mkT = [k.sb("mkT%d" % l, [128, 4, NMEM], BF16) for l in range(2)]
    mvb = [k.sb("mvb%d" % l, [128, 2, 512], BF16) for l in range(2)]
    rstds = [k.sb("rstd%d" % i, [128, 512], F32) for i in range(2)]
    lnbs = [k.sb("lnb%d" % i, [128, 512], F32) for i in range(2)]
    rs_i = [0]

    def rs_pair():
        rs_i[0] += 1
        return rstds[rs_i[0] % 2], lnbs[rs_i[0] % 2]
    sq = [k.sb("sq%d" % i, [128, 512], BF16) for i in range(2)]
    stg = [k.sb("stg%d" % i, [128, 1024], F32) for i in range(3)]
    small = k.sb("small", [128, 256], F32)
    KTs = k.sb("KTs", [128, 4, NSAMP], BF16)
    Vs = k.sb("Vs", [128, 512], BF16)
    lf_all = k.sb("lf_all", [128, 16, 8], F32)
    fcum = k.sb("fcum", [128, 16, 8], F32)
    fend = k.sb("fend", [128, 4, 8], F32)
    lfsum = k.sb("lfsum", [128, 8], F32)
    lf_s = k.sb("lf_s", [128, 8], F32)
    nfnew = k.sb("nfnew", [128, 8], F32)
    smalls = [k.sb("sm%d" % i, [128, 64], F32) for i in range(4)]
    R0 = k.sb_ptr
    RSIZE = k.sb_hi - R0

    ident = cst[:, C_ID:C_ID + 128]
    ones_f = cst[:, C_ONES:C_ONES + 128]
    ident_b = cb[:, 0:128]
    ones_b = cb[:, 128:256]
    bd_b = cb[:, 256:384]
    ut_b = cb[:, 384:512]

    NFILL = int(os.environ.get("KFILL", "1"))
    if NFILL > 0:
        fb = k.banks[7]

        def _filler(e):
            for _ in range(NFILL):
                e.matmul(fb.h[:, :], lhsT=cb.h[:, 128:256], rhs=cb.h[:, 0:512], start=True, stop=True)
        k.filler = _filler

    stg_i = [0]

    def staging():
        s = stg[stg_i[0] % 3]
        stg_i[0] += 1
        return s

    sq_i = [0]

    def sqbuf():
        s = sq[sq_i[0] % 2]
        sq_i[0] += 1
        return s

    ev_i = [0]

    def ev_eng():
        ev_i[0] += 1
        return "act" if ev_i[0] % 2 else "dve"

    def evac(out, in_, eng=None):
        eng = eng or ev_eng()
        k.copy(eng, out, in_)

    def vcol(c):
        return vecs[:, c:c + 1]

    k.dma("sp", cst[:, :], cst_d[:, :])
    k.dma("sp", vecs[:, :], vecs_d[:, :])
    k.dma("sp", cint[:, :], cint_d[:, :])
    k.copy("dve", cb[:, 0:512], cst[:, 0:512])
    k.copy("dve", masks_b[:, :], cst[:, C_MASKS:C_MASKS + 128])

    def load_tok_major_T(src_rows, nrows, dst_fn):
        s = staging()
        k.dma("sp", s[0:nrows, :], src_rows)
        for half in range(2):
            bk = k.bank()
            for j in range(4):
                kk = half * 4 + j
                k.tr(bk[:, j * 128:j * 128 + nrows], s[0:nrows, kk * 128:(kk + 1) * 128], A(ident.ap[0:nrows, 0:nrows], ident.bufs))
            evac(dst_fn(half * 4), bk.v(lambda h: h[:, :].rearrange("p (j t) -> p j t", j=4)[:, :, 0:nrows]))
            k.release(bk)

    def rms_rstd(src_chunks, nch, n, dim, lhs_ones):
        rstd, lnb = rs_pair()
        bk = k.bank()
        for i in range(nch):
            s = sqbuf()
            k.act(s[:, 0:n], src_chunks(i), AF.Square)
            k.mm(bk[:, 0:n], lhs_ones, s[:, 0:n], start=(i == 0), stop=(i == nch - 1))
        k.act(lnb[:, 0:n], bk[:, 0:n], AF.Ln, scale=1.0 / dim, bias=EPS)
        k.act(rstd[:, 0:n], lnb[:, 0:n], AF.Exp, scale=-0.5)
        k.release(bk)
        return rstd

    def norm_tile(i, gcol, dst):
        n = TILES[i][1]
        rstd = rms_rstd(lambda kk: hT[i][:, kk, :], 8, n, float(D), ones_b)
        for kk in range(8):
            k.stt(dst[:, kk, 0:n], hT[i][:, kk, :], vcol(gcol + kk), rstd[:, 0:n], ALU.mult, ALU.mult)

    for i, (c0, n) in enumerate(TILES):
        for tcn in range(n // 128):
            if i < 4:
                src = xp[c0 + tcn * 128:c0 + (tcn + 1) * 128, :]
            else:
                src = xs[:, :]
            load_tok_major_T(src, 128, lambda k0, i=i, tcn=tcn: hT[i][:, k0:k0 + 4, tcn * 128:(tcn + 1) * 128])

    class Arena:
        def __init__(self, base=None, limit=None):
            self.p = R0 if base is None else base
            self.limit = k.sb_hi if limit is None else limit

        def alloc(self, name, shape, dt):
            nbytes = int(np.prod(shape[1:])) * mybir.dt.size(dt)
            assert self.p + nbytes <= self.limit, (name, self.p, nbytes, self.limit)
            t = k.sb(name, shape, dt, off=self.p)
            self.p = (self.p + nbytes + 31) // 32 * 32
            return t

    uid = [0]

    def uname(s):
        uid[0] += 1
        return "%s_%d" % (s, uid[0])

    def load_wsq(dst, w_ap):
        for half in range(2):
            k.dma("pool", dst[:, half * 4:(half + 1) * 4, :],
                  w_ap[half * 512:(half + 1) * 512, :].rearrange("(k p) n -> p k n", p=128))

    ar = Arena()
    wsq = [ar.alloc(uname("wsq"), [128, 8, D], BF16) for _ in range(2)]
    mT = ar.alloc(uname("mT"), [128, 8, NMEM], F32)
    mn = ar.alloc(uname("mn"), [128, 8, NMEM], BF16)
    ktok = ar.alloc(uname("ktok"), [128, 512], F32)
    ktmp = ar.alloc(uname("ktmp"), [128, 512], F32)
    kb16 = ar.alloc(uname("kb16"), [128, 512], BF16)

    for mc in range(2):
        load_tok_major_T(memp[mc * 128:(mc + 1) * 128, :], 128,
                         lambda k0, mc=mc: mT[:, k0:k0 + 4, mc * 128:(mc + 1) * 128])
    rstd = rms_rstd(lambda kk: mT[:, kk, :], 8, NMEM, float(D), ones_b)
    for l in range(2):
        load_wsq(wsq[l], w_mkv[l])
    for l in range(2):
        for kk in range(8):
            k.stt(mn[:, kk, :], mT[:, kk, :], vcol(V_GMEM + l * 8 + kk), rstd[:, 0:NMEM], ALU.mult, ALU.mult)
        for mc in range(2):
            bkK, bkV = k.bank(), k.bank()
            for kk in range(8):
                k.mm(bkK[:, :], mn[:, kk, mc * 128:(mc + 1) * 128], wsq[l][:, kk, 0:512], start=(kk == 0), stop=(kk == 7))
            for kk in range(8):
                k.mm(bkV[:, :], mn[:, kk, mc * 128:(mc + 1) * 128], wsq[l][:, kk, 512:1024], start=(kk == 0), stop=(kk == 7))
            sv = staging()
            k.copy("act", sv[:, 0:512], bkV[:, :])
            k.release(bkV)
            k.dma("sp", mvp[l, mc * 128:(mc + 1) * 128, :], sv[:, 0:512], is_out=True)
            k.copy("dve", mvb[l][:, mc, :], sv[:, 0:512])
            k.copy("act", ktok[:, :], bkK[:, :])
            k.release(bkK)
            k.tt("dve", ktmp[:, :], ktok[:, :], ktok[:, :], ALU.mult)
            k.red(small[:, 0:4], ktmp.v(lambda h: h[:, :].rearrange("p (h d) -> p h d", h=4)))
            k.act(small[:, 4:8], small[:, 0:4], AF.Ln, scale=1.0 / 128, bias=EPS)
            k.act(small[:, 8:12], small[:, 4:8], AF.Exp, scale=-0.5)
            k.tt("dve", ktmp.v(lambda h: h[:, :].rearrange("p (h d) -> p h d", h=4)),
                 ktok.v(lambda h: h[:, :].rearrange("p (h d) -> p h d", h=4)),
                 small.v(lambda h: h[:, 8:12].unsqueeze(2).to_broadcast([128, 4, 128])), ALU.mult)
            sk = staging()
            k.tt("dve", sk.v(lambda h: h[:, 0:512].rearrange("p (h d) -> p h d", h=4)),
                 ktmp.v(lambda h: h[:, :].rearrange("p (h d) -> p h d", h=4)),
                 vecs.v(lambda h: h[:, V_GMK + l * 128:V_GMK + (l + 1) * 128].unsqueeze(1).to_broadcast([128, 4, 128])), ALU.mult)
            k.dma("sp", mkp[l, mc * 128:(mc + 1) * 128, :], sk[:, 0:512], is_out=True)
            k.copy("act", kb16[:, :], sk[:, 0:512])
            bt = k.bank()
            btb = bt.v(lambda h: h[:, :].bitcast(BF16))
            for hh in range(4):
                k.tr(A(btb.ap[:, hh * 128:(hh + 1) * 128], btb.bufs), kb16[:, hh * 128:(hh + 1) * 128], ident_b)
            evac(mkT[l][:, :, mc * 128:(mc + 1) * 128],
                 A(btb.ap[:, 0:512].rearrange("p (h m) -> p h m", h=4), btb.bufs))
            k.release(bt)

    if stage <= 1:
        k.emit()
        return nc

    rsq128 = 1.0 / float(np.sqrt(128.0))

    def head_norm(bk, n, zm, dim, lhs_ones, gcol, dst):
        k.copy("act", zm[:, 0:n], bk[:, 0:n])
        k.release(bk)
        rstd = rms_rstd(lambda _: zm[:, 0:n], 1, n, float(dim), lhs_ones)
        k.stt(dst, zm[:, 0:n], vcol(gcol), rstd[:, 0:n], ALU.mult, ALU.mult)

    def mem_attn_prompt(l, n, qm, Pm, lnd, rden, cat):
        def stage_a(h):
            P = Pm[h % 2]
            for mc in range(2):
                bkS = k.bank()
                k.mm(bkS[:, 0:n], mkT[l][:, h, mc * 128:(mc + 1) * 128], qm[:, h, 0:n])
                k.act(P[:, mc, 0:n], bkS[:, 0:n], AF.Exp, scale=rsq128)
                k.release(bkS)

        def stage_b(h):
            P = Pm[h % 2]
            rden_, lnd_ = rs_pair()
            bkN, bkD = k.bank(), k.bank()
            for mc in range(2):
                k.mm(bkN[:, 0:n], mvb[l][:, mc, h * 128:(h + 1) * 128], P[:, mc, 0:n], start=(mc == 0), stop=(mc == 1))
            for mc in range(2):
                k.mm(bkD[:, 0:n], ones_b, P[:, mc, 0:n], start=(mc == 0), stop=(mc == 1))
            k.act(lnd_[:, 0:n], bkD[:, 0:n], AF.Ln)
            k.release(bkD)
            k.act(rden_[:, 0:n], lnd_[:, 0:n], AF.Exp, scale=-1.0)
            k.tt("dve", cat[:, 4 + h, 0:n], bkN[:, 0:n], rden_[:, 0:n], ALU.mult)
            k.release(bkN)

        stage_a(0)
        for h in range(4):
            if h + 1 < 4:
                stage_a(h + 1)
            stage_b(h)

    def mem_attn_sample(l, qm, cat, mks, mksT, mvs, Pms):
        for b in range(BPC):
            k.dma("pool", mks[:, :, :], cmk[l, b].rearrange("(c p) f -> p c f", p=128))
            k.dma("pool", mvs[:, :, :], cmv[l, b].rearrange("(c p) f -> p c f", p=128))
            bt = k.bank()
            btb = bt.v(lambda h: h[:, :].bitcast(BF16))
            for h in range(4):
                for mc in range(2):
                    o = h * 256 + mc * 128
                    k.tr(A(btb.ap[:, o:o + 128], btb.bufs), mks[:, mc, h * 128:(h + 1) * 128], ident_b)
            evac(mksT[:, :, :], A(btb.ap[:, 0:1024].rearrange("p (h m) -> p h m", h=4), btb.bufs))
            k.release(bt)
            bkS = k.bank()
            for mc in range(2):
                for h in range(4):
                    o = (mc * 4 + h) * 8
                    k.mm(bkS[:, o:o + 8], mksT[:, h, mc * 128:(mc + 1) * 128], qm[:, h, b * 8:(b + 1) * 8])
            k.act(Pms[:, 0:64], bkS[:, 0:64], AF.Exp, scale=rsq128)
            k.release(bkS)
            bkN = k.bank()
            first = True
            for h in range(4):
                for mc in range(2):
                    o = (mc * 4 + h) * 8
                    k.mm(bkN[:, h * 8:(h + 1) * 8], mvs[:, mc, h * 128:(h + 1) * 128], Pms[:, o:o + 8],
                         start=first, stop=(mc == 1), sgc=True)
                    first = False
            for mc in range(2):
                k.mm(bkN[:, 32:64], ones_b, Pms[:, mc * 32:(mc + 1) * 32], start=False, stop=(mc == 1), sgc=True)
            k.act(small[:, 64:96], bkN[:, 32:64], AF.Ln)
            k.act(small[:, 96:128], small[:, 64:96], AF.Exp, scale=-1.0)
            k.tt("dve", cat[:, 4:8, b * 8:(b + 1) * 8],
                 bkN.v(lambda h: h[:, 0:32].rearrange("p (h t) -> p h t", h=4)),
                 small.v(lambda h: h[:, 96:128].rearrange("p (h t) -> p h t", h=4)), ALU.mult)
            k.release(bkN)

    def out_proj(i, wo, cat):
        n = TILES[i][1]
        for c in range(8):
            bk = k.bank()
            for kk in range(8):
                k.mm(bk[:, 0:n], wo[:, kk, c * 128:(c + 1) * 128], cat[:, kk, 0:n], start=(kk == 0), stop=(kk == 7))
            k.tt("dve", hT[i][:, c, :], bk[:, 0:n], hT[i][:, c, :], ALU.add)
            k.release(bk)

    def mixer0():
        l = 0
        ar = Arena()
        wsq = [ar.alloc(uname("wsq"), [128, 8, D], BF16) for _ in range(2)]
        wpl = ar.alloc(uname("wpl"), [128, 4, 128], BF16)
        zte = ar.alloc(uname("zte"), [128, 4, 528], F32)
        sA = ar.alloc(uname("sA"), [128, 528], F32)
        sB = ar.alloc(uname("sB"), [128, 528], F32)
        yT = ar.alloc(uname("yT"), [128, 4, 512], BF16)
        zm = [ar.alloc(uname("zm"), [128, 512], F32) for _ in range(2)]
        qm = ar.alloc(uname("qm"), [128, 4, 512], BF16)
        Pm = [ar.alloc(uname("Pm"), [128, 2, 512], BF16) for _ in range(2)]
        lnd = rden = None
        ytmp = ar.alloc(uname("ytmp"), [128, 16], F32)
        a0 = Arena(xn[0].buf.lo, xn[0].buf.hi)
        zes = a0.alloc(uname("zes"), [128, 4, BPC, 24], F32)
        mks = a0.alloc(uname("mks"), [128, 2, 512], BF16)
        a1 = Arena(xn[1].buf.lo, xn[1].buf.hi)
        mksT = a1.alloc(uname("mksT"), [128, 4, NMEM], BF16)
        mvs = a1.alloc(uname("mvs"), [128, 2, 512], BF16)
        zst = a1.alloc(uname("zst"), [128, 4, 240], F32)
        Pms = a1.alloc(uname("Pms"), [128, 64], BF16)
        xr = [xn[2], xn[3]]

        load_wsq(wsq[0], w_in[l])
        load_wsq(wsq[1], w_out[l])
        k.dma("pool", wpl[:, :, :], w_pool.rearrange("g c d -> c g d"))
        k.memset("dve", zte[:, :, :], 0.0)
        k.memset("dve", zes[:, :, :, :], 0.0)
        s = staging()
        k.dma("sp", s[0:120, 0:512], spool[0:120, :])
        k.dma("sp", s[0:120, 512:1024], spool[120:240, :])
        for hb in range(2):
            bk = k.bank()
            for g in range(4):
                k.tr(bk[:, g * 128:g * 128 + 120], s[0:120, hb * 512 + g * 128:hb * 512 + (g + 1) * 128],
                     A(ident.ap[0:120, 0:120], ident.bufs))
            for g in range(4):
                evac(zes[:, g, hb * 8:(hb + 1) * 8, 1:16],
                     bk.v(lambda h, g=g: h[:, g * 128:g * 128 + 120].rearrange("p (b j) -> p b j", b=8)))
            k.release(bk)

        ksub = int(os.environ.get('KSUB', '99'))
        if ksub <= 1:
            return
        for ti, i in enumerate((4, 0, 1, 2, 3)):
            n = TILES[i][1]
            samp = (i == 4)
            X = xr[ti % 2]
            norm_tile(i, V_GMIX + l * 8, X)
            def fin0(bk, c):
                if c < 4:
                    if samp:
                        evac(zes[:, c, :, 16:24], bk.v(lambda h: h[:, 0:128].rearrange("p (b t) -> p b t", b=BPC)))
                    else:
                        evac(zte[:, c, 16:528], bk[:, 0:512])
                    k.release(bk)
                else:
                    head_norm(bk, n, zm[c % 2], 128, ones_b, V_GMQ + l, qm[:, c - 4, 0:n])

            pend = None
            for c in range(8):
                bk = k.bank()
                for kk in range(8):
                    k.mm(bk[:, 0:n], wsq[0][:, kk, c * 128:(c + 1) * 128], X[:, kk, 0:n], start=(kk == 0), stop=(kk == 7))
                if pend is not None:
                    fin0(*pend)
                pend = (bk, c)
            fin0(*pend)
            if ksub <= 2:
                return
            cat = X
            for g in range(4):
                w = 2 << g
                eng = "pool" if g < 3 else "dve"
                if samp:
                    zz = lambda a, b_, g=g: zes[:, g, :, a:b_]
                    SA = lambda a, b_: sA.v(lambda h: h[:, 0:384].rearrange("p (b c) -> p b c", b=BPC)[:, :, a:b_])
                    SB = lambda a, b_: sB.v(lambda h: h[:, 0:384].rearrange("p (b c) -> p b c", b=BPC)[:, :, a:b_])
                    W_ = 24
                else:
                    zz = lambda a, b_, g=g: zte[:, g, a:b_]
                    SA = lambda a, b_: sA[:, a:b_]
                    SB = lambda a, b_: sB[:, a:b_]
                    W_ = 528
                k.tt(eng, SA(1, W_), zz(1, W_), zz(0, W_ - 1), ALU.add)
                S = SA
                if g >= 1:
                    k.tt(eng, SB(3, W_), SA(3, W_), SA(1, W_ - 2), ALU.add)
                    S = SB
                if g >= 2:
                    k.tt(eng, SA(7, W_), SB(7, W_), SB(3, W_ - 4), ALU.add)
                    S = SA
                if g >= 3:
                    k.tt(eng, SB(15, W_), SA(15, W_), SA(7, W_ - 8), ALU.add)
                    S = SB
                if samp:
                    k.stt(yT.v(lambda h, g=g: h[:, g, 0:128].rearrange("p (b t) -> p b t", b=BPC)),
                          S(16, 24), 1.0 / w, zz(16, 24), ALU.mult, ALU.subtract)
                else:
                    k.stt(yT[:, g, 0:512], S(16, 528), 1.0 / w, zz(16, 528), ALU.mult, ALU.subtract)
                    if i == 0:
                        k.tt("dve", ytmp[:, :], S(16, 32), cst[:, C_INVC + g * 16:C_INVC + (g + 1) * 16], ALU.mult)
                        k.tt("dve", yT[:, g, 0:16], ytmp[:, :], zz(16, 32), ALU.subtract)
                bk = k.bank()
                k.mm(bk[:, 0:n], wpl[:, g, :], yT[:, g, 0:n])
                k.act(cat[:, g, 0:n], bk[:, 0:n], AF.Identity, scale=vcol(V_PSC + g))
                k.release(bk)
            if ksub <= 3:
                return
            if samp:
                for g in range(4):
                    k.copy("dve", zst.v(lambda h, g=g: h[:, g, :].rearrange("p (b j) -> p b j", b=BPC)), zes[:, g, :, 9:24])
                for hb in range(2):
                    bk = k.bank()
                    for g in range(4):
                        k.tr(bk[0:120, g * 128:(g + 1) * 128], zst[:, g, hb * 120:(hb + 1) * 120], ident)
                    so = staging()
                    k.copy("act", so[0:120, 0:512], bk[0:120, :])
                    k.release(bk)
                    k.dma("sp", pss[hb * 120:(hb + 1) * 120, :], so[0:120, 0:512], is_out=True)
            else:
                if i == 3:
                    bk = k.bank()
                    for g in range(4):
                        k.copy("dve", zst[:, g, 0:15], zte[:, g, 513:528])
                        k.tr(bk[:, g * 128:(g + 1) * 128], zst[:, g, 0:128], ident)
                    so = staging()
                    k.copy("act", so[0:15, 0:512], bk[0:15, :])
                    k.release(bk)
                    k.dma("sp", psp[:, :], so[0:15, 0:512], is_out=True)
                else:
                    k.copy("dve", ytmp.v(lambda h: h[:, 0:16]), zte[:, 0, 512:528])
                    for g in range(4):
                        k.copy("dve", sA[:, 0:16], zte[:, g, 512:528])
                        k.copy("dve", zte[:, g, 0:16], sA[:, 0:16])
            if ksub <= 4:
                return
            if samp:
                mem_attn_sample(l, qm, cat, mks, mksT, mvs, Pms)
            else:
                mem_attn_prompt(l, n, qm, Pm, lnd, rden, cat)
            if ksub <= 5:
                return
            out_proj(i, wsq[1], cat)
            if ksub <= 6:
                return

    k.fill = True
    mixer0()
    k.fill = False
    if stage <= 2:
        k.emit()
        return nc

    sm_i = [0]

    def smbuf():
        sm_i[0] += 1
        return smalls[sm_i[0] % 4]

    def ffn(l):
        ar = Arena()
        aT = [ar.alloc(uname("aT"), [128, 4, n], BF16) for (_, n) in TILES]
        wgu = [ar.alloc(uname("wgu"), [128, 8, 2, 256], BF16) for _ in range(3)]
        wdn = [ar.alloc(uname("wdn"), [128, 4, D], BF16) for _ in range(2)]
        sg = [ar.alloc(uname("sg"), [128, 512], F32) for _ in range(2)]
        for i in range(NT):
            norm_tile(i, V_GFFN + l * 8, xn[i])
        groups = [(0, 4), (4, 4), (8, 4), (12, 4), (16, 4), (20, 2)]
        blk = 0
        cnt = 0
        for gi, (j0, nj) in enumerate(groups):
            wd = wdn[gi % 2]
            k.dma("pool", wd[:, 0:nj, :], w_down[l][j0 * 128:(j0 + nj) * 128, :].rearrange("(j p) n -> p j n", p=128))
            for jb in range(nj // 2):
                j = j0 + 2 * jb
                ws = wgu[blk % 3]
                blk += 1
                k.dma("pool", ws[:, :, 0, :], w_gu[l][:, j * 128:j * 128 + 256].rearrange("(k p) n -> p k n", p=128))
                k.dma("pool", ws[:, :, 1, :], w_gu[l][:, DFF + j * 128:DFF + j * 128 + 256].rearrange("(k p) n -> p k n", p=128))
                for i in range(NT):
                    n = TILES[i][1]
                    for jj in range(2):
                        bkG = k.bank()
                        for kk in range(8):
                            k.mm(bkG[:, 0:n], ws[:, kk, 0, jj * 128:(jj + 1) * 128], xn[i][:, kk, :], start=(kk == 0), stop=(kk == 7))
                        bkU = k.bank()
                        for kk in range(8):
                            k.mm(bkU[:, 0:n], ws[:, kk, 1, jj * 128:(jj + 1) * 128], xn[i][:, kk, :], start=(kk == 0), stop=(kk == 7))
                        s_ = sg[cnt % 2]
                        cnt += 1
                        k.act(s_[:, 0:n], bkG[:, 0:n], AF.Silu)
                        k.release(bkG)
                        k.tt("dve", aT[i][:, 2 * jb + jj, :], bkU[:, 0:n], s_[:, 0:n], ALU.mult)
                        k.release(bkU)
            for i in range(NT):
                n = TILES[i][1]
                for c in range(8):
                    bk = k.bank()
                    for jj in range(nj):
                        k.mm(bk[:, 0:n], wd[:, jj, c * 128:(c + 1) * 128], aT[i][:, jj, :], start=(jj == 0), stop=(jj == nj - 1))
                    k.tt("dve", hT[i][:, c, :], bk[:, 0:n], hT[i][:, c, :], ALU.add)
                    k.release(bk)

    ffn(0)

    axn = Arena(xn[0].buf.lo, xn[4].buf.hi)
    KTp = [axn.alloc(uname("KTp"), [128, 4, 512], BF16) for _ in range(4)]
    Vp = [axn.alloc(uname("Vp"), [128, 4, 8, 65], BF16) for _ in range(4)]

    def shared_kv():
        ar = Arena()
        wkv = ar.alloc(uname("wkv"), [128, 8, D], BF16)
        wfg = ar.alloc(uname("wfg"), [128, 8, 8], BF16)
        xk = ar.alloc(uname("xk"), [128, 8, 512], BF16)
        ktok = ar.alloc(uname("ktok"), [128, 512], F32)
        ktmp = ar.alloc(uname("ktmp"), [128, 512], F32)
        kb16 = ar.alloc(uname("kb16"), [128, 512], BF16)
        load_wsq(wkv, w_kv)
        k.dma("pool", wfg[:, :, :], w_fg.rearrange("(k p) h -> p k h", p=128))
        for i in range(4):
            k.memset("dve", Vp[i][:, :, :, 64:65], 1.0)
        h3 = lambda t, a=0, b_=512: t.v(lambda h: h[:, a:b_].rearrange("p (h d) -> p h d", h=8))
        for i in (4, 0, 1, 2, 3):
            n = TILES[i][1]
            samp = (i == 4)
            norm_tile(i, V_GKV, xk)
            for tcn in range(n // 128):
                gc = i * 4 + tcn
                r0 = TILES[i][0] + tcn * 128
                cs = slice(tcn * 128, (tcn + 1) * 128)
                bkK, bkV, bkF = k.bank(), k.bank(), k.bank()
                for kk in range(8):
                    k.mm(bkK[:, :], xk[:, kk, cs], wkv[:, kk, 0:512], start=(kk == 0), stop=(kk == 7))
                for kk in range(8):
                    k.mm(bkV[:, :], xk[:, kk, cs], wkv[:, kk, 512:1024], start=(kk == 0), stop=(kk == 7))
                for kk in range(8):
                    k.mm(bkF[:, 0:8], xk[:, kk, cs], wfg[:, kk, :], start=(kk == 0), stop=(kk == 7))
                sm = smbuf()
                sv = staging()
                k.copy("act", sv[:, 0:512], bkV[:, :])
                k.release(bkV)
                if samp:
                    k.dma("sp", vrs[:, :], sv[:, 0:512], is_out=True)
                    k.copy("dve", Vs[:, :], sv[:, 0:512])
                else:
                    k.dma("sp", vrp[r0:r0 + 128, :], sv[:, 0:512], is_out=True)
                    k.copy("dve", Vp[i][:, tcn, :, 0:64], h3(sv))
                k.copy("act", ktok[:, :], bkK[:, :])
                k.release(bkK)
                k.tt("dve", ktmp[:, :], ktok[:, :], ktok[:, :], ALU.mult)
                k.red(sm[:, 0:8], h3(ktmp))
                k.act(sm[:, 8:16], sm[:, 0:8], AF.Ln, scale=1.0 / 64, bias=EPS)
                k.act(sm[:, 16:24], sm[:, 8:16], AF.Exp, scale=-0.5)
                k.tt("dve", h3(ktmp), h3(ktok), sm.v(lambda h: h[:, 16:24].unsqueeze(2).to_broadcast([128, 8, 64])), ALU.mult)
                sk = staging()
                k.tt("dve", h3(sk), h3(ktmp), vecs.v(lambda h: h[:, V_GFK:V_GFK + 64].unsqueeze(1).to_broadcast([128, 8, 64])), ALU.mult)
                if samp:
                    k.dma("sp", krs[:, :], sk[:, 0:512], is_out=True)
                else:
                    k.dma("sp", krp[r0:r0 + 128, :], sk[:, 0:512], is_out=True)
                k.copy("act", kb16[:, :], sk[:, 0:512])
                bt = k.bank()
                btb = bt.v(lambda h: h[:, :].bitcast(BF16))
                for c in range(4):
                    k.tr(A(btb.ap[:, c * 128:(c + 1) * 128], btb.bufs), kb16[:, c * 128:(c + 1) * 128], ident_b)
                src = A(btb.ap[:, 0:512].rearrange("p (c t) -> p c t", c=4), btb.bufs)
                if samp:
                    evac(KTs[:, :, :], src)
                else:
                    evac(KTp[i][:, :, cs], src)
                k.release(bt)
                k.tt("dve", sm[:, 24:32], bkF[:, 0:8], vecs[:, V_BFG:V_BFG + 8], ALU.add)
                k.release(bkF)
                k.act(sm[:, 32:40], sm[:, 24:32], AF.Exp, scale=-1.0)
                k.act(sm[:, 40:48], sm[:, 32:40], AF.Ln, bias=1.0)
                if samp:
                    k.ts("dve", lf_s[:, :], sm[:, 40:48], -1.0, ALU.mult)
                    k.dma("sp", lfs[:, :], lf_s[:, :], is_out=True)
                    bkc = k.bank()
                    k.mm(bkc[:, 0:8], cst[:, C_BT:C_BT + 128], lf_s[:, :])
                    k.ts("dve", nfnew[:, :], bkc[:, 0:8], -1.0, ALU.mult)
                    k.release(bkc)
                else:
                    k.ts("dve", lf_all[:, gc, :], sm[:, 40:48], -1.0, ALU.mult)
                    k.dma("sp", lfp[r0:r0 + 128, :], lf_all[:, gc, :], is_out=True)
                    bkc = k.bank()
                    k.mm(bkc[:, 0:8], cst[:, C_UT:C_UT + 128], lf_all[:, gc, :], start=True, stop=(gc == 0))
                    if gc > 0:
                        k.mm(bkc[:, 0:8], ones_f, lfsum[:, :], start=False, stop=True)
                    k.copy("act", fcum[:, gc, :], bkc[:, 0:8])
                    k.release(bkc)
                    if gc == 0:
                        k.copy("dve", lfsum[:, :], lf_all[:, 0, :])
                    else:
                        k.tt("dve", lfsum[:, :], lfsum[:, :], lf_all[:, gc, :], ALU.add)
                    if tcn == 3:
                        bke = k.bank()
                        k.mm(bke[:, 0:8], cst[:, C_E127:C_E127 + 128], fcum[:, gc, :])
                        k.copy("act", fend[:, i, :], bke[:, 0:8])
                        k.release(bke)

    k.fill = True
    shared_kv()
    k.fill = False
    if stage <= 3:
        k.emit()
        return nc

    def mixer1():
        l = 1
        ar = Arena()
        wsq = [ar.alloc(uname("wsq"), [128, 8, D], BF16) for _ in range(2)]
        X = ar.alloc(uname("X1"), [128, 8, 512], BF16)
        zm = [ar.alloc(uname("zm"), [128, 512], F32) for _ in range(2)]
        qm = ar.alloc(uname("qm"), [128, 4, 512], BF16)
        Pm = [ar.alloc(uname("Pm"), [128, 2, 512], BF16) for _ in range(2)]
        lnd = rden = None
        biasJ = ar.alloc(uname("biasJ"), [128, 16, 8], F32)
        p0 = ar.p
        qT = ar.alloc(uname("qT"), [128, 4, 512], BF16)
        PT = [ar.alloc(uname("PT"), [128, 512], BF16) for _ in range(3)]
        otok = ar.alloc(uname("otok"), [128, 4, 8, 64], BF16)
        arw = Arena(wsq[0].buf.lo, wsq[0].buf.hi)
        KTg = arw.alloc(uname("KTg"), [128, 4, 8, 128], BF16)
        mks = arw.alloc(uname("mks"), [128, 2, 512], BF16)
        mksT = arw.alloc(uname("mksT"), [128, 4, NMEM], BF16)
        mvs = arw.alloc(uname("mvs"), [128, 2, 512], BF16)
        qbd = arw.alloc(uname("qbd"), [128, 4, BPC, 16], BF16)
        axs = Arena(xn[0].buf.lo, xn[4].buf.hi)
        Kg = [axs.alloc(uname("Kg"), [128, 4, 512], BF16) for _ in range(4)]
        Vg = [axs.alloc(uname("Vg"), [128, 4, 512], BF16) for _ in range(4)]
        ars = Arena(p0)
        qTs = ars.alloc(uname("qTs"), [128, 4, NSAMP], BF16)
        ptb = ars.alloc(uname("ptb"), [128, BPC], I32)
        itmp = ars.alloc(uname("itmp"), [128, BPC], I32)
        idx8 = ars.alloc(uname("idx8"), [128, BPC], I32)
        idx32 = ars.alloc(uname("idx32"), [128, BPC, 4], I32)
        rg4 = ars.alloc(uname("rg4"), [128, 1], I32)
        lfgs = [ars.alloc(uname("lfg"), [128, 16, 8], F32) for _ in range(2)]
        biasbs = [ars.alloc(uname("biasb"), [128, 16, 8], F32) for _ in range(2)]
        tmpS = ars.alloc(uname("tmpS"), [128, 512], F32)
        Pg = ars.alloc(uname("Pg"), [128, 512], BF16)
        Pn = ars.alloc(uname("Pn"), [128, 64], BF16)
        Pms = ars.alloc(uname("Pms"), [128, 64], BF16)

        load_wsq(wsq[0], w_in[l])
        load_wsq(wsq[1], w_out[l])

        def in_proj(i, n, qdst):
            norm_tile(i, V_GMIX + l * 8, X)
            def fin1(bk, c):
                if c < 4:
                    head_norm(bk, n, zm[c % 2], 64, bd_b, V_GFQ, qdst[:, c, 0:n])
                else:
                    head_norm(bk, n, zm[c % 2], 128, ones_b, V_GMQ + l, qm[:, c - 4, 0:n])

            pend = None
            for c in range(8):
                bk = k.bank()
                for kk in range(8):
                    k.mm(bk[:, 0:n], wsq[0][:, kk, c * 128:(c + 1) * 128], X[:, kk, 0:n], start=(kk == 0), stop=(kk == 7))
                if pend is not None:
                    fin1(*pend)
                pend = (bk, c)
            fin1(*pend)

        ks1 = float(os.environ.get('KSUB1', '99'))
        k.fill = True
        for J in range(4):
            n = 512
            in_proj(J, n, qT)
            if ks1 <= 1:
                return
            cat = X
            nk = 4 * J + 4
            k.tt("dve", biasJ[:, 0:nk, :], fend.v(lambda h: h[:, J, :].unsqueeze(1).to_broadcast([128, nk, 8])),
                 fcum[:, 0:nk, :], ALU.subtract)
            for h in range(8):
                c, hp = h // 2, slice((h % 2) * 64, (h % 2) * 64 + 64)
                acc = k.bank()
                accv = acc.v(lambda hh: hh[:, 0:260].rearrange("p (q e) -> p q e", q=4))
                banks = {}

                def S_E(kc):
                    ti, tcn = kc // 4, kc % 4
                    nq0 = max(0, kc - 4 * J) * 128
                    bkS = k.bank()
                    k.mm(bkS[:, nq0:512], A(KTp[ti].h[hp, c, tcn * 128:(tcn + 1) * 128], [KTp[ti].buf]),
                         A(qT.h[hp, c, nq0:512], [qT.buf]))
                    P = PT[kc % 3]
                    k.act(P[:, nq0:512], bkS[:, nq0:512], AF.Exp, scale=0.125, bias=biasJ[:, kc, h:h + 1])
                    k.release(bkS)
                    if kc >= 4 * J:
                        k.tt("pool", P[:, nq0:nq0 + 128], P[:, nq0:nq0 + 128], ut_b, ALU.mult)

                S_E(0)
                first = True
                for kc in range(nk):
                    if kc + 1 < nk:
                        S_E(kc + 1)
                    ti, tcn = kc // 4, kc % 4
                    nq0 = max(0, kc - 4 * J) * 128
                    P = PT[kc % 3]
                    for tq in range(nq0 // 128, 4):
                        k.mm(A(accv.ap[:, tq, :], accv.bufs), P[:, tq * 128:(tq + 1) * 128], Vp[ti][:, tcn, h, :],
                             start=first, stop=(kc == 4 * J + tq), sgc=True)
                        first = False
                sm = smbuf()
                k.recip(sm.v(lambda hh: hh[:, 0:4].unsqueeze(2)), A(accv.ap[:, :, 64:65], accv.bufs))
                k.tt("dve", otok[:, :, h, :], A(accv.ap[:, :, 0:64], accv.bufs),
                     sm.v(lambda hh: hh[:, 0:4].unsqueeze(2).to_broadcast([128, 4, 64])), ALU.mult)
                k.release(acc)
            if ks1 <= 2:
                return
            for tp in range(2):
                bt = k.bank()
                btb = bt.v(lambda hh: hh[:, :].bitcast(BF16))
                for tq2 in range(2):
                    tq = tp * 2 + tq2
                    for c in range(4):
                        o = (tq2 * 4 + c) * 128
                        k.tr(A(btb.ap[:, o:o + 128], btb.bufs),
                             otok.v(lambda hh, tq=tq, c=c: hh[:, tq, 2 * c:2 * c + 2, :].rearrange("p a d -> p (a d)")), ident_b)
                for tq2 in range(2):
                    tq = tp * 2 + tq2
                    evac(cat[:, 0:4, tq * 128:(tq + 1) * 128],
                         A(btb.ap[:, tq2 * 512:(tq2 + 1) * 512].rearrange("p (c t) -> p c t", c=4), btb.bufs))
                k.release(bt)
            mem_attn_prompt(l, n, qm, Pm, lnd, rden, cat)
            out_proj(J, wsq[1], cat)
            if ks1 <= 3:
                return
        if ks1 <= 4:
            return

        k.fill = False
        n = NSAMP
        in_proj(4, n, qTs)
        cat = X
        k.memset("dve", qbd[:, :, :, :], 0.0)
        for c in range(4):
            k.copy("dve", A(qbd.h[0:64, c, :, 0:8], [qbd.buf]),
                   A(qTs.h[0:64, c, :].rearrange("p (b t) -> p b t", b=BPC), [qTs.buf]))
            k.copy("dve", A(qbd.h[64:128, c, :, 8:16], [qbd.buf]),
                   A(qTs.h[64:128, c, :].rearrange("p (b t) -> p b t", b=BPC), [qTs.buf]))
        k.dma("sp", ptb[:, :], ptab[:, :])
        k.ts("pool", itmp[:, :], ptb[:, :], 8, ALU.mult)
        k.tt("pool", idx8[:, :], itmp[:, :], cint.v(lambda hh: hh[:, 0:1].to_broadcast([128, BPC])), ALU.add)
        k.ts("pool", rg4[:, :], cint[:, :], 4, ALU.mult)
        k.ts("pool", itmp[:, :], ptb[:, :], 32, ALU.mult)
        k.tt("pool", idx32[:, :, 0], itmp[:, :], rg4.v(lambda hh: hh[:, 0:1].to_broadcast([128, BPC])), ALU.add)
        for j in range(1, 4):
            k.ts("pool", idx32[:, :, j], idx32[:, :, 0], j, ALU.add)
        if ks1 <= 5:
            return
        for b in range(BPC):
            lfg, biasb = lfgs[b % 2], biasbs[b % 2]
            k.gather(lfg.v(lambda hh: hh[:, :, :].rearrange("p r h -> p (r h)")), clf, idx8[:, b:b + 1])
            for r in range(1, 16):
                k.tt("dve", lfg[:, r, :], lfg[:, r, :], lfg[:, r - 1, :], ALU.add)
            bkb = k.bank()
            k.mm(bkb[:, 0:8], cst[:, C_GE:C_GE + 128], lfg[:, 15, :])
            k.tt("dve", biasb[:, :, :], bkb.v(lambda hh: hh[:, 0:8].unsqueeze(1).to_broadcast([128, 16, 8])),
                 lfg[:, :, :], ALU.subtract)
            k.release(bkb)
            if ks1 <= 6:
                return
            bkO = k.bank()
            first = [True, True]
            for hf in range(2):
                for jj in range(2):
                    j = hf * 2 + jj
                    k.gather(Kg[j].v(lambda hh: hh[:, :, :].rearrange("p r f -> p (r f)")), ck, idx32[:, b, j:j + 1])
                    k.gather(Vg[j].v(lambda hh: hh[:, :, :].rearrange("p r f -> p (r f)")), cv, idx32[:, b, j:j + 1])
                kgr = lambda r, hf=hf: Kg[hf * 2 + r // 4]
                vgr = lambda r, hf=hf: Vg[hf * 2 + r // 4]
                if ks1 <= 6.1:
                    return
                for c in range(4):
                    bt = k.bank()
                    btb = bt.v(lambda hh: hh[:, :].bitcast(BF16))
                    for r in range(8):
                        k.tr(A(btb.ap[:, r * 128:(r + 1) * 128], btb.bufs), kgr(r)[:, r % 4, c * 128:(c + 1) * 128], ident_b)
                    evac(KTg[:, c, :, :], A(btb.ap[:, 0:1024].rearrange("p (r s) -> p r s", r=8), btb.bufs))
                    k.release(bt)
                if ks1 <= 6.2:
                    return
                bkS = k.bank()
                for r in range(8):
                    for c in range(4):
                        o = (r * 4 + c) * 16
                        k.mm(bkS[:, o:o + 16], KTg[:, c, r, :], qbd[:, c, b, :])
                k.stt(tmpS.v(lambda hh: hh[:, :].rearrange("p (x t) -> p x t", t=8)),
                      bkS.v(lambda hh: hh[:, :].rearrange("p (x t) -> p x t", t=8)), 0.125,
                      biasb.v(lambda hh, hf=hf: hh[:, hf * 8:(hf + 1) * 8, :].rearrange("p r h -> p (r h)").unsqueeze(2).to_broadcast([128, 64, 8])),
                      ALU.mult, ALU.add)
                k.release(bkS)
                k.act(Pg[:, :], tmpS[:, :], AF.Exp)
                if ks1 <= 6.3:
                    return
                Pg5 = lambda r, c, hh_: A(Pg.h[:, (r * 8 + c * 2 + hh_) * 8:(r * 8 + c * 2 + hh_) * 8 + 8], [Pg.buf])
                for r in range(8):
                    for h in range(8):
                        c, hh_ = h // 2, h % 2
                        k.mm(A(bkO.h[hh_ * 64:(hh_ + 1) * 64, c * 8:(c + 1) * 8], [bkO.buf]),
                             vgr(r)[:, r % 4, h * 64:(h + 1) * 64], Pg5(r, c, hh_), start=first[hh_], stop=False, sgc=True)
                        first[hh_] = False
                    for hh_ in range(2):
                        k.mm(A(bkO.h[hh_ * 64:(hh_ + 1) * 64, 32:64].rearrange("p (c t) -> p c t", c=4), [bkO.buf]),
                             A(ones_b.ap[:, 0:64], ones_b.bufs),
                             A(Pg.h[:, r * 64:(r + 1) * 64].rearrange("p (c a t) -> p c a t", c=4, a=2)[:, :, hh_, :], [Pg.buf]),
                             start=False, stop=False, sgc=True)
            if ks1 <= 6.4:
                return
            bkS2 = k.bank()
            for c in range(4):
                k.mm(bkS2[:, c * 16:(c + 1) * 16], KTs[:, c, :], qbd[:, c, b, :])
            k.stt(tmpS.v(lambda hh: hh[:, 0:64].rearrange("p (x t) -> p x t", t=8)),
                  bkS2.v(lambda hh: hh[:, 0:64].rearrange("p (x t) -> p x t", t=8)), 0.125,
                  nfnew.v(lambda hh: hh[:, :].unsqueeze(2).to_broadcast([128, 8, 8])), ALU.mult, ALU.add)
            k.release(bkS2)
            k.act(Pn[:, :], tmpS[:, 0:64], AF.Exp)
            k.tt("dve", Pn.v(lambda hh: hh[:, :].rearrange("p (x t) -> p x t", t=8)),
                 Pn.v(lambda hh: hh[:, :].rearrange("p (x t) -> p x t", t=8)),
                 masks_b.v(lambda hh, b=b: hh[:, b * 8:(b + 1) * 8].unsqueeze(1).to_broadcast([128, 8, 8])), ALU.mult)
            for h in range(8):
                c, hh_ = h // 2, h % 2
                k.mm(A(bkO.h[hh_ * 64:(hh_ + 1) * 64, c * 8:(c + 1) * 8], [bkO.buf]),
                     Vs[:, h * 64:(h + 1) * 64], Pn[:, h * 8:(h + 1) * 8], start=False, stop=True, sgc=True)
            for hh_ in range(2):
                k.mm(A(bkO.h[hh_ * 64:(hh_ + 1) * 64, 32:64].rearrange("p (c t) -> p c t", c=4), [bkO.buf]),
                     A(ones_b.ap[:, 0:64], ones_b.bufs),
                     A(Pn.h[:, :].rearrange("p (c a t) -> p c a t", c=4, a=2)[:, :, hh_, :], [Pn.buf]),
                     start=False, stop=True, sgc=True)
            sm = smbuf()
            k.act(sm[:, 0:32], bkO[:, 32:64], AF.Ln)
            k.act(sm[:, 32:64], sm[:, 0:32], AF.Exp, scale=-1.0)
            k.tt("dve", cat[:, 0:4, b * 8:(b + 1) * 8],
                 bkO.v(lambda hh: hh[:, 0:32].rearrange("p (c t) -> p c t", c=4)),
                 sm.v(lambda hh: hh[:, 32:64].rearrange("p (c t) -> p c t", c=4)), ALU.mult)
            k.release(bkO)
            if ks1 <= 7:
                return
        mem_attn_sample(l, qm, cat, mks, mksT, mvs, Pms)
        out_proj(4, wsq[1], cat)

    mixer1()
    if float(os.environ.get('KSUB1', '99')) < 99:
        k.emit()
        return nc
    ffn(1)

    for i, (c0, n) in enumerate(TILES):
        for tcn in range(n // 128):
            so = staging()
            for half in range(2):
                bk = k.bank()
                for j in range(4):
                    k.tr(bk[:, j * 128:(j + 1) * 128], hT[i][:, half * 4 + j, tcn * 128:(tcn + 1) * 128], ident)
                evac(so[:, half * 512:(half + 1) * 512], bk[:, :])
                k.release(bk)
            if i < 4:
                k.dma("sp", y_p[c0 + tcn * 128:c0 + (tcn + 1) * 128, :], so[:, :], is_out=True)
            else:
                k.dma("sp", y_s[:, :], so[:, :], is_out=True)
    k.emit()
    return nc


def _consts():
    c = np.zeros((128, NCST), np.float32)
    p = np.arange(128)
    c[:, C_ID:C_ID + 128] = np.eye(128, dtype=np.float32)
    c[:, C_ONES:C_ONES + 128] = 1.0
    c[:, C_BD:C_BD + 128] = (p[:, None] // 64 == p[None, :] // 64)
    c[:, C_UT:C_UT + 128] = (p[:, None] <= p[None, :])
    c[127, C_E127:C_E127 + 128] = 1.0
    c[:, C_GE:C_GE + 128] = (p[:, None] >= p[None, :])
    c[:, C_BT:C_BT + 128] = (p[:, None] // 8 == p[None, :] // 8) & (p[:, None] <= p[None, :])
    for g, w in enumerate((2, 4, 8, 16)):
        t = np.arange(16)
        c[:, C_INVC + g * 16:C_INVC + (g + 1) * 16] = 1.0 / np.minimum(w, t + 1)
    m = np.zeros((128, 16, 8), np.float32)
    for b in range(16):
        for t in range(8):
            m[:, b, t] = (p // 8 == b) & (p % 8 <= t)
    c[:, C_MASKS:C_MASKS + 128] = m.reshape(128, 128)
    ci = (p % 8).astype(np.int32).reshape(128, 1)
    return c, ci


def _vecs(g_mix, g_ffn, g_mem, g_kv, g_mem_q, g_mem_k, pool_scale, g_fox_k, g_fox_q, b_fg):
    v = np.zeros((128, NV), np.float32)
    fm = lambda a: np.ascontiguousarray(a.reshape(8, 128).T)
    for l in range(2):
        v[:, V_GMIX + l * 8:V_GMIX + (l + 1) * 8] = fm(g_mix[l])
        v[:, V_GFFN + l * 8:V_GFFN + (l + 1) * 8] = fm(g_ffn[l])
        v[:, V_GMEM + l * 8:V_GMEM + (l + 1) * 8] = fm(g_mem[l])
        v[:, V_GMQ + l] = g_mem_q[l]
        v[:, V_GMK + l * 128:V_GMK + (l + 1) * 128] = np.broadcast_to(g_mem_k[l][None, :], (128, 128))
    v[:, V_GKV:V_GKV + 8] = fm(g_kv)
    v[:, V_PSC:V_PSC + 4] = pool_scale[0].reshape(4, 128).T
    v[:, V_GFQ] = np.tile(g_fox_q[0], 2)
    v[:, V_GFK:V_GFK + 64] = np.broadcast_to(g_fox_k[None, :], (128, 64))
    v[:, V_BFG:V_BFG + 8] = np.broadcast_to(b_fg[None, :], (128, 8))
    return v


_NC_CACHE = {}


def kernel(x_prompt, x_sample, cache_mem_k, cache_mem_v, state_pool, cache_k, cache_v, cache_logf,
           page_table, mem_prompt, g_mix, w_in, w_out, g_ffn, w_gu, w_down, g_mem, w_mem_kv,
           g_mem_q, g_mem_k, w_pool, pool_scale, g_kv, w_kv, g_fox_k, w_fg, b_fg, g_fox_q,
           _stage=99):
    f = lambda a: np.ascontiguousarray(np.asarray(a, dtype=np.float32))
    x_prompt, x_sample, mem_prompt = f(x_prompt), f(x_sample), f(mem_prompt)
    cache_mem_k, cache_mem_v, state_pool = f(cache_mem_k), f(cache_mem_v), f(state_pool)
    cache_k, cache_v, cache_logf = f(cache_k), f(cache_v), f(cache_logf)
    page_table = np.ascontiguousarray(np.asarray(page_table, dtype=np.int32))
    nphys = cache_k.shape[0]
    key = (_stage, nphys)
    if key not in _NC_CACHE:
        _NC_CACHE[key] = build_program(_stage, nphys)
    nc = _NC_CACHE[key]
    cst, cint = _consts()
    vecs = _vecs(f(g_mix), f(g_ffn), f(g_mem), f(g_kv), f(g_mem_q), f(g_mem_k), f(pool_scale),
                 f(g_fox_k), f(g_fox_q), f(b_fg))
    ck = cache_k.reshape(nphys * 32, 2048)
    cv = cache_v.reshape(nphys * 32, 2048)
    clf = cache_logf.reshape(nphys * 8, 16 * 8)
    shared = {
        "ck": ck, "cv": cv, "clf": clf,
        "w_in": f(w_in), "w_out": f(w_out), "w_gu": f(w_gu), "w_down": f(w_down), "w_mkv": f(w_mem_kv),
        "w_pool": f(w_pool).reshape(4, 128, 128), "w_kv": f(w_kv), "w_fg": f(w_fg),
        "vecs": vecs, "cst": cst, "cint": cint,
    }
    in_maps = []
    for c in range(NB):
        m = dict(shared)
        bs = slice(c * BPC, (c + 1) * BPC)
        m["xp"] = x_prompt[c]
        m["xs"] = x_sample[bs].reshape(NSAMP, D)
        m["memp"] = mem_prompt[c]
        m["cmk"] = cache_mem_k[:, bs].reshape(2, BPC, NMEM, 512)
        m["cmv"] = cache_mem_v[:, bs].reshape(2, BPC, NMEM, 512)
        m["spool"] = state_pool[0, bs].reshape(BPC * 15, 512)
        m["ptab"] = np.ascontiguousarray(np.repeat(page_table[bs].T, 8, axis=0))
        in_maps.append(m)
    ncores = int(os.environ.get("KCORES", NB))
    res = run_bass_kernel_spmd(nc, in_maps[:ncores], core_ids=list(range(ncores)))
    r = list(res.results)
    while len(r) < NB:
        r.append({kk: np.zeros_like(np.asarray(v)) for kk, v in r[0].items()})
    cat = lambda name: [np.asarray(r[c][name]) for c in range(NB)]
    y_prompt = np.stack(cat("y_p")).reshape(NB, SEQ, D)
    y_sample = np.concatenate(cat("y_s")).reshape(DB, DS, D)
    k_rows_prompt = np.stack(cat("krp")).reshape(NB, SEQ // PAGE, PAGE, 8, 64)
    v_rows_prompt = np.stack(cat("vrp")).reshape(NB, SEQ // PAGE, PAGE, 8, 64)
    logf_rows_prompt = np.stack(cat("lfp")).reshape(NB, SEQ // PAGE, PAGE, 8)
    mem_k_prompt = np.stack(cat("mkp"), axis=1).reshape(2, NB, NMEM, 4, 128)
    mem_v_prompt = np.stack(cat("mvp"), axis=1).reshape(2, NB, NMEM, 4, 128)
    pool_state_prompt = np.stack(cat("psp")).reshape(1, NB, 15, 512)
    k_rows_sample = np.concatenate(cat("krs")).reshape(DB, DS, 8, 64)
    v_rows_sample = np.concatenate(cat("vrs")).reshape(DB, DS, 8, 64)
    logf_rows_sample = np.concatenate(cat("lfs")).reshape(DB, DS, 8)
    pool_state_sample = np.concatenate(cat("pss")).reshape(1, DB, 15, 512)
    return (y_prompt, y_sample, k_rows_prompt, v_rows_prompt, logf_rows_prompt, mem_k_prompt,
            mem_v_prompt, pool_state_prompt, k_rows_sample, v_rows_sample, logf_rows_sample,
            pool_state_sample)
```

```python
import os
import numpy as np
import concourse.bass as bass
import concourse.mybir as mybir
from concourse.bass_utils import run_bass_kernel_spmd

F32 = mybir.dt.float32
BF16 = mybir.dt.bfloat16
I32 = mybir.dt.int32
AF = mybir.ActivationFunctionType
ALU = mybir.AluOpType
AX = mybir.AxisListType

D = 1024
SEQ = 2048
NB = 8
DB = 128
DS = 8
BPC = DB // NB
NSAMP = BPC * DS
DFF = 2816
NJ = DFF // 128
NPAGES = 16
PAGE = 128
NPHYS = 2560
NMEM = 256
EPS = 1e-6
TILES = [(0, 512), (512, 512), (1024, 512), (1536, 512), (2048, 128)]
NT = len(TILES)

V_GMIX, V_GFFN, V_GMEM, V_GKV, V_GMQ, V_PSC, V_GFQ, V_GMK, V_GFK, V_BFG, NV = 0, 16, 32, 48, 56, 58, 62, 64, 320, 384, 392
C_ID, C_ONES, C_BD, C_UT, C_E127, C_GE, C_BT, C_INVC, C_MASKS, NCST = 0, 128, 256, 384, 512, 640, 768, 896, 960, 1088


class Buf:
    __slots__ = ("name", "space", "lo", "hi", "w", "r", "ov", "dma")

    def __init__(self, name, space, lo, hi):
        self.name, self.space, self.lo, self.hi = name, space, lo, hi
        self.w = None
        self.r = {}
        self.ov = None
        self.dma = False


class A:
    __slots__ = ("ap", "bufs")

    def __init__(self, ap, bufs):
        self.ap, self.bufs = ap, bufs


class TT:
    def __init__(self, h, buf):
        self.h, self.buf = h, buf

    def __getitem__(self, idx):
        return A(self.h[idx], [self.buf])

    def v(self, fn):
        return A(fn(self.h), [self.buf])


class Op:
    __slots__ = ("eng", "fn", "key", "pos", "signal", "waits", "is_dma", "val")

    def __init__(self, eng, fn, key, is_dma):
        self.eng, self.fn, self.key, self.is_dma = eng, fn, key, is_dma
        self.pos = 0
        self.signal = is_dma
        self.waits = {}
        self.val = 0


class K:
    ENGS = ("pe", "act", "dve", "pool", "sp")

    def __init__(self, nc):
        self.nc = nc
        self.q = {e: [] for e in self.ENGS}
        self.bufs = []
        self.chan_n = {}
        self.out_chans = set()
        self.sb_lo = (nc.sbuf_base + 63) // 64 * 64
        self.sb_hi = nc.sbuf_top
        self.sb_ptr = self.sb_lo
        self.banks = []
        self.free_banks = []

    def sb(self, name, shape, dt, off=None):
        nbytes = int(np.prod(shape[1:])) * mybir.dt.size(dt)
        if off is None:
            off = self.sb_ptr
            self.sb_ptr = (off + nbytes + 31) // 32 * 32
        assert off + nbytes <= self.sb_hi, (name, off, nbytes, self.sb_hi)
        h = self.nc.alloc_sbuf_tensor_at(name, list(shape), dt, offset=off)
        b = Buf(name, "sb", off, off + nbytes)
        self.bufs.append(b)
        for x in self.bufs:
            x.ov = None
        return TT(h, b)

    def make_banks(self):
        for i in range(8):
            h = self.nc.alloc_psum_tensor("bank%d" % i, [128, 512], F32)
            b = Buf("bank%d" % i, "ps", i, i + 1)
            self.bufs.append(b)
            self.banks.append(TT(h, b))
        self.free_banks = list(self.banks)

    def bank(self):
        assert self.free_banks, "out of PSUM banks"
        return self.free_banks.pop(0)

    def release(self, bk):
        self.free_banks.append(bk)

    def _ov(self, b):
        if b.ov is None:
            b.ov = [x for x in self.bufs if x.space == b.space and x.lo < b.hi and b.lo < x.hi]
        return b.ov

    @staticmethod
    def _dep(op, prod):
        if prod is None or prod is op:
            return
        if prod.eng == "pe" and op.eng == "pe" and not prod.is_dma:
            return
        cur = op.waits.get(prod.key)
        if cur is None or cur.pos < prod.pos:
            op.waits[prod.key] = prod

    def _add(self, eng, fn, R, W, is_dma=False, chan=None):
        if is_dma:
            key = chan
            op = Op(eng, fn, key, True)
            n = self.chan_n.get(key, 0) + 1
            self.chan_n[key] = n
            op.pos = n
        else:
            op = Op(eng, fn, eng, False)
            op.pos = len(self.q[eng]) + 1
        for b in R:
            for x in self._ov(b):
                self._dep(op, x.w)
                if b.space == "ps":
                    for rr in x.r.values():
                        if rr.eng != eng:
                            self._dep(op, rr)
        for b in W:
            for x in self._ov(b):
                self._dep(op, x.w)
                for rr in x.r.values():
                    self._dep(op, rr)
        rkey = op.key if is_dma else eng
        for b in R:
            b.r[rkey] = op
        for b in W:
            b.w = op
            b.r = {}
        self.q[eng].append(op)
        return op

    def op(self, eng, fn, R=(), W=()):
        rb = [b for a in R for b in a.bufs]
        wb = [b for a in W for b in a.bufs]
        return self._add(eng, fn, rb, wb)

    def mm(self, out, lhsT, rhs, start=True, stop=True, sgc=False):
        return self.op("pe", lambda e: e.matmul(out.ap, lhsT=lhsT.ap, rhs=rhs.ap, start=start, stop=stop,
                                                skip_group_check=sgc),
                       R=[lhsT, rhs], W=[out])

    def tr(self, out, in_, ident):
        return self.op("pe", lambda e: e.transpose(out.ap, in_.ap, ident.ap), R=[in_, ident], W=[out])

    def act(self, out, in_, func, scale=1.0, bias=None, eng="act"):
        R = [in_]
        kw = {}
        if isinstance(scale, A):
            R.append(scale)
            kw["scale"] = scale.ap
        else:
            kw["scale"] = float(scale)
        if isinstance(bias, A):
            R.append(bias)
            kw["bias"] = bias.ap
        elif bias is not None:
            kw["bias"] = float(bias)
        return self.op("act", lambda e: e.activation(out=out.ap, in_=in_.ap, func=func, **kw), R=R, W=[out])

    def tt(self, eng, out, in0, in1, op):
        return self.op(eng, lambda e: e.tensor_tensor(out=out.ap, in0=in0.ap, in1=in1.ap, op=op),
                       R=[in0, in1], W=[out])

    def ts(self, eng, out, in0, s1, op0, s2=None, op1=None):
        R = [in0]
        a1 = s1
        if isinstance(s1, A):
            R.append(s1)
            a1 = s1.ap
        a2 = s2
        if isinstance(s2, A):
            R.append(s2)
            a2 = s2.ap
        if op1 is None:
            return self.op(eng, lambda e: e.tensor_scalar(out=out.ap, in0=in0.ap, scalar1=a1, scalar2=None, op0=op0),
                           R=R, W=[out])
        return self.op(eng, lambda e: e.tensor_scalar(out=out.ap, in0=in0.ap, scalar1=a1, scalar2=a2, op0=op0, op1=op1),
                       R=R, W=[out])

    def stt(self, out, in0, scalar, in1, op0, op1):
        R = [in0, in1]
        sc = scalar
        if isinstance(scalar, A):
            R.append(scalar)
            sc = scalar.ap
        return self.op("dve", lambda e: e.scalar_tensor_tensor(out=out.ap, in0=in0.ap, scalar=sc, in1=in1.ap,
                                                               op0=op0, op1=op1), R=R, W=[out])

    def copy(self, eng, out, in_):
        if eng == "act":
            return self.op("act", lambda e: e.copy(out=out.ap, in_=in_.ap), R=[in_], W=[out])
        return self.op(eng, lambda e: e.tensor_copy(out=out.ap, in_=in_.ap), R=[in_], W=[out])

    def recip(self, out, in_):
        return self.op("dve", lambda e: e.reciprocal(out=out.ap, in_=in_.ap), R=[in_], W=[out])

    def red(self, out, in_, op=ALU.add, axis=AX.X):
        return self.op("dve", lambda e: e.tensor_reduce(out=out.ap, in_=in_.ap, op=op, axis=axis), R=[in_], W=[out])

    def memset(self, eng, out, val):
        return self.op(eng, lambda e: e.memset(out.ap, val), R=[], W=[out])

    def dma(self, q, out, in_, is_out=False, **kw):
        R, W = [], []
        o_ap, i_ap = out, in_
        chan = None
        if isinstance(out, A):
            W = list(out.bufs)
            o_ap = out.ap
            chan = out.bufs[0].name
        if isinstance(in_, A):
            R = list(in_.bufs)
            i_ap = in_.ap
            if chan is None:
                chan = in_.bufs[0].name
        if is_out:
            self.out_chans.add(chan)
        return self._add(q, lambda e: e.dma_start(out=o_ap, in_=i_ap, **kw), R, W, is_dma=True, chan=chan)

    def gather(self, out, in_ap, idx):
        chan = out.bufs[0].name
        return self._add("pool", lambda e: e.indirect_dma_start(
            out=out.ap, out_offset=None, in_=in_ap,
            in_offset=bass.IndirectOffsetOnAxis(ap=idx.ap, axis=0)),
            list(idx.bufs), list(out.bufs), is_dma=True, chan=chan)

    def emit(self):
        nc = self.nc
        for e in self.ENGS:
            for op in self.q[e]:
                for prod in op.waits.values():
                    prod.signal = True
        sems = {}
        for e in ("pe", "act", "dve", "pool"):
            sems[e] = nc.alloc_semaphore("s_" + e)
        for ch in self.chan_n:
            sems[ch] = nc.alloc_semaphore("d_" + ch)
        for e in self.ENGS:
            n = 0
            for op in self.q[e]:
                if op.is_dma:
                    op.val = 16 * op.pos
                elif op.signal:
                    n += 1
                    op.val = n
        engobj = {"pe": "tensor", "act": "scalar", "dve": "vector", "pool": "gpsimd", "sp": "sync"}
        out_chans = sorted(self.out_chans)
        chan_n = self.chan_n
        q = self.q
        stats = {}
        with nc.Block() as block:
            def body(ename):
                def f(e):
                    known = {}
                    nw = 0
                    for op in q[ename]:
                        for key, prod in op.waits.items():
                            v = prod.val
                            if known.get(key, 0) >= v:
                                continue
                            e.wait_ge(sems[key], v)
                            known[key] = v
                            nw += 1
                        inst = op.fn(e)
                        if op.signal:
                            inst.then_inc(sems[op.key], 16 if op.is_dma else 1)
                    if ename == "sp":
                        for ch in out_chans:
                            e.wait_ge(sems[ch], 16 * chan_n[ch])
                    stats[ename] = (len(q[ename]), nw)
                return f
            block.tensor(body("pe"))
            block.scalar(body("act"))
            block.vector(body("dve"))
            block.gpsimd(body("pool"))
            block.sync(body("sp"))
        if os.environ.get("KSTATS"):
            print("KSTATS", stats, "nsem", len(sems))


def build_program(stage=99, nphys=NPHYS):
    nc = bass.Bass("TRN2", target_bir_lowering=False)
    k = K(nc)

    def din(name, shape, dt=F32):
        return nc.dram_tensor(name, list(shape), dt, kind="ExternalInput").ap()

    def dout(name, shape, dt=F32):
        return nc.dram_tensor(name, list(shape), dt, kind="ExternalOutput").ap()

    xp = din("xp", [SEQ, D])
    xs = din("xs", [NSAMP, D])
    memp = din("memp", [NMEM, D])
    cmk = din("cmk", [2, BPC, NMEM, 512])
    cmv = din("cmv", [2, BPC, NMEM, 512])
    spool = din("spool", [BPC * 15, 512])
    ck = din("ck", [nphys * 32, 2048])
    cv = din("cv", [nphys * 32, 2048])
    clf = din("clf", [nphys * 8, 16 * 8])
    ptab = din("ptab", [128, BPC], I32)
    w_in = din("w_in", [2, D, D])
    w_out = din("w_out", [2, D, D])
    w_gu = din("w_gu", [2, D, 2 * DFF])
    w_down = din("w_down", [2, DFF, D])
    w_mkv = din("w_mkv", [2, D, D])
    w_pool = din("w_pool", [4, 128, 128])
    w_kv = din("w_kv", [D, D])
    w_fg = din("w_fg", [D, 8])
    vecs_d = din("vecs", [128, NV])
    cst_d = din("cst", [128, NCST])
    cint_d = din("cint", [128, 1], I32)

    y_p = dout("y_p", [SEQ, D])
    y_s = dout("y_s", [NSAMP, D])
    krp = dout("krp", [SEQ, 512])
    vrp = dout("vrp", [SEQ, 512])
    lfp = dout("lfp", [SEQ, 8])
    mkp = dout("mkp", [2, NMEM, 512])
    mvp = dout("mvp", [2, NMEM, 512])
    psp = dout("psp", [15, 512])
    krs = dout("krs", [NSAMP, 512])
    vrs = dout("vrs", [NSAMP, 512])
    lfs = dout("lfs", [NSAMP, 8])
    pss = dout("pss", [BPC * 15, 512])

    k.make_banks()

    cst = k.sb("cst", [128, NCST], F32)
    vecs = k.sb("vecs", [128, NV], F32)
    cint = k.sb("cint", [128, 1], I32)
    cb = k.sb("cb", [128, 4 * 128], BF16)
    masks_b = k.sb("masks_b", [128, 128], BF16)
    hT = [k.sb("hT%d" % i, [128, 8, n], F32) for i, (_, n) in enumerate(TILES)]
    xn = [k.sb("xn%d" % i, [128, 8, n], BF16) for i, (_, n) in enumerate(TILES)]
    mkT = [k.sb("mkT%d" % l, [128, 4, NMEM], BF16) for l in range(2)]
    mvb = [k.sb("mvb%d" % l, [128, 2, 512], BF16) for l in range(2)]
    rstds = [k.sb("rstd%d" % i, [128, 512], F32) for i in range(2)]
    lnbs = [k.sb("lnb%d" % i, [128, 512], F32) for i in range(2)]
    rs_i = [0]

    def rs_pair():
        rs_i[0] += 1
        return rstds[rs_i[0] % 2], lnbs[rs_i[0] % 2]
    sq = [k.sb("sq%d" % i, [128, 512], BF16) for i in range(2)]
    stg = [k.sb("stg%d" % i, [128, 1024], F32) for i in range(3)]
    small = k.sb("small", [128, 256], F32)
    KTs = k.sb("KTs", [128, 4, NSAMP], BF16)
    Vs = k.sb("Vs", [128, 512], BF16)
    lf_all = k.sb("lf_all", [128, 16, 8], F32)
    fcum = k.sb("fcum", [128, 16, 8], F32)
    fend = k.sb("fend", [128, 4, 8], F32)
    lfsum = k.sb("lfsum", [128, 8], F32)
    lf_s = k.sb("lf_s", [128, 8], F32)
    nfnew = k.sb("nfnew", [128, 8], F32)
    smalls = [k.sb("sm%d" % i, [128, 64], F32) for i in range(4)]
    R0 = k.sb_ptr
    RSIZE = k.sb_hi - R0

    ident = cst[:, C_ID:C_ID + 128]
    ones_f = cst[:, C_ONES:C_ONES + 128]
    ident_b = cb[:, 0:128]
    ones_b = cb[:, 128:256]
    bd_b = cb[:, 256:384]
    ut_b = cb[:, 384:512]

    stg_i = [0]

    def staging():
        s = stg[stg_i[0] % 3]
        stg_i[0] += 1
        return s

    sq_i = [0]

    def sqbuf():
        s = sq[sq_i[0] % 2]
        sq_i[0] += 1
        return s

    ev_i = [0]

    def ev_eng():
        ev_i[0] += 1
        return "act" if ev_i[0] % 2 else "dve"

    def evac(out, in_, eng=None):
        eng = eng or ev_eng()
        k.copy(eng, out, in_)

    def vcol(c):
        return vecs[:, c:c + 1]

    k.dma("sp", cst[:, :], cst_d[:, :])
    k.dma("sp", vecs[:, :], vecs_d[:, :])
    k.dma("sp", cint[:, :], cint_d[:, :])
    k.copy("dve", cb[:, 0:512], cst[:, 0:512])
    k.copy("dve", masks_b[:, :], cst[:, C_MASKS:C_MASKS + 128])

    def load_tok_major_T(src_rows, nrows, dst_fn):
        s = staging()
        k.dma("sp", s[0:nrows, :], src_rows)
        for half in range(2):
            bk = k.bank()
            for j in range(4):
                kk = half * 4 + j
                k.tr(bk[:, j * 128:j * 128 + nrows], s[0:nrows, kk * 128:(kk + 1) * 128], A(ident.ap[0:nrows, 0:nrows], ident.bufs))
            evac(dst_fn(half * 4), bk.v(lambda h: h[:, :].rearrange("p (j t) -> p j t", j=4)[:, :, 0:nrows]))
            k.release(bk)

    def rms_rstd(src_chunks, nch, n, dim, lhs_ones):
        rstd, lnb = rs_pair()
        bk = k.bank()
        for i in range(nch):
            s = sqbuf()
            k.act(s[:, 0:n], src_chunks(i), AF.Square)
            k.mm(bk[:, 0:n], lhs_ones, s[:, 0:n], start=(i == 0), stop=(i == nch - 1))
        k.act(lnb[:, 0:n], bk[:, 0:n], AF.Ln, scale=1.0 / dim, bias=EPS)
        k.act(rstd[:, 0:n], lnb[:, 0:n], AF.Exp, scale=-0.5)
        k.release(bk)
        return rstd

    def norm_tile(i, gcol, dst):
        n = TILES[i][1]
        rstd = rms_rstd(lambda kk: hT[i][:, kk, :], 8, n, float(D), ones_b)
        for kk in range(8):
            k.stt(dst[:, kk, 0:n], hT[i][:, kk, :], vcol(gcol + kk), rstd[:, 0:n], ALU.mult, ALU.mult)

    for i, (c0, n) in enumerate(TILES):
        for tcn in range(n // 128):
            if i < 4:
                src = xp[c0 + tcn * 128:c0 + (tcn + 1) * 128, :]
            else:
                src = xs[:, :]
            load_tok_major_T(src, 128, lambda k0, i=i, tcn=tcn: hT[i][:, k0:k0 + 4, tcn * 128:(tcn + 1) * 128])

    class Arena:
        def __init__(self, base=None, limit=None):
            self.p = R0 if base is None else base
            self.limit = k.sb_hi if limit is None else limit

        def alloc(self, name, shape, dt):
            nbytes = int(np.prod(shape[1:])) * mybir.dt.size(dt)
            assert self.p + nbytes <= self.limit, (name, self.p, nbytes, self.limit)
            t = k.sb(name, shape, dt, off=self.p)
            self.p = (self.p + nbytes + 31) // 32 * 32
            return t

    uid = [0]

    def uname(s):
        uid[0] += 1
        return "%s_%d" % (s, uid[0])

    def load_wsq(dst, w_ap):
        for half in range(2):
            k.dma("pool", dst[:, half * 4:(half + 1) * 4, :],
                  w_ap[half * 512:(half + 1) * 512, :].rearrange("(k p) n -> p k n", p=128))

    ar = Arena()
    wsq = [ar.alloc(uname("wsq"), [128, 8, D], BF16) for _ in range(2)]
    mT = ar.alloc(uname("mT"), [128, 8, NMEM], F32)
    mn = ar.alloc(uname("mn"), [128, 8, NMEM], BF16)
    ktok = ar.alloc(uname("ktok"), [128, 512], F32)
    ktmp = ar.alloc(uname("ktmp"), [128, 512], F32)
    kb16 = ar.alloc(uname("kb16"), [128, 512], BF16)

    for mc in range(2):
        load_tok_major_T(memp[mc * 128:(mc + 1) * 128, :], 128,
                         lambda k0, mc=mc: mT[:, k0:k0 + 4, mc * 128:(mc + 1) * 128])
    rstd = rms_rstd(lambda kk: mT[:, kk, :], 8, NMEM, float(D), ones_b)
    for l in range(2):
        load_wsq(wsq[l], w_mkv[l])
    for l in range(2):
        for kk in range(8):
            k.stt(mn[:, kk, :], mT[:, kk, :], vcol(V_GMEM + l * 8 + kk), rstd[:, 0:NMEM], ALU.mult, ALU.mult)
        for mc in range(2):
            bkK, bkV = k.bank(), k.bank()
            for kk in range(8):
                k.mm(bkK[:, :], mn[:, kk, mc * 128:(mc + 1) * 128], wsq[l][:, kk, 0:512], start=(kk == 0), stop=(kk == 7))
            for kk in range(8):
                k.mm(bkV[:, :], mn[:, kk, mc * 128:(mc + 1) * 128], wsq[l][:, kk, 512:1024], start=(kk == 0), stop=(kk == 7))
            sv = staging()
            k.copy("act", sv[:, 0:512], bkV[:, :])
            k.release(bkV)
            k.dma("sp", mvp[l, mc * 128:(mc + 1) * 128, :], sv[:, 0:512], is_out=True)
            k.copy("dve", mvb[l][:, mc, :], sv[:, 0:512])
            k.copy("act", ktok[:, :], bkK[:, :])
            k.release(bkK)
            k.tt("dve", ktmp[:, :], ktok[:, :], ktok[:, :], ALU.mult)
            k.red(small[:, 0:4], ktmp.v(lambda h: h[:, :].rearrange("p (h d) -> p h d", h=4)))
            k.act(small[:, 4:8], small[:, 0:4], AF.Ln, scale=1.0 / 128, bias=EPS)
            k.act(small[:, 8:12], small[:, 4:8], AF.Exp, scale=-0.5)
            k.tt("dve", ktmp.v(lambda h: h[:, :].rearrange("p (h d) -> p h d", h=4)),
                 ktok.v(lambda h: h[:, :].rearrange("p (h d) -> p h d", h=4)),
                 small.v(lambda h: h[:, 8:12].unsqueeze(2).to_broadcast([128, 4, 128])), ALU.mult)
            sk = staging()
            k.tt("dve", sk.v(lambda h: h[:, 0:512].rearrange("p (h d) -> p h d", h=4)),
                 ktmp.v(lambda h: h[:, :].rearrange("p (h d) -> p h d", h=4)),
                 vecs.v(lambda h: h[:, V_GMK + l * 128:V_GMK + (l + 1) * 128].unsqueeze(1).to_broadcast([128, 4, 128])), ALU.mult)
            k.dma("sp", mkp[l, mc * 128:(mc + 1) * 128, :], sk[:, 0:512], is_out=True)
            k.copy("act", kb16[:, :], sk[:, 0:512])
            bt = k.bank()
            btb = bt.v(lambda h: h[:, :].bitcast(BF16))
            for hh in range(4):
                k.tr(A(btb.ap[:, hh * 128:(hh + 1) * 128], btb.bufs), kb16[:, hh * 128:(hh + 1) * 128], ident_b)
            evac(mkT[l][:, :, mc * 128:(mc + 1) * 128],
                 A(btb.ap[:, 0:512].rearrange("p (h m) -> p h m", h=4), btb.bufs))
            k.release(bt)

    if stage <= 1:
        k.emit()
        return nc

    rsq128 = 1.0 / float(np.sqrt(128.0))

    def head_norm(bk, n, zm, dim, lhs_ones, gcol, dst):
        k.copy("act", zm[:, 0:n], bk[:, 0:n])
        k.release(bk)
        rstd = rms_rstd(lambda _: zm[:, 0:n], 1, n, float(dim), lhs_ones)
        k.stt(dst, zm[:, 0:n], vcol(gcol), rstd[:, 0:n], ALU.mult, ALU.mult)

    def mem_attn_prompt(l, n, qm, Pm, lnd, rden, cat):
        def stage_a(h):
            P = Pm[h % 2]
            for mc in range(2):
                bkS = k.bank()
                k.mm(bkS[:, 0:n], mkT[l][:, h, mc * 128:(mc + 1) * 128], qm[:, h, 0:n])
                k.act(P[:, mc, 0:n], bkS[:, 0:n], AF.Exp, scale=rsq128)
                k.release(bkS)

        def stage_b(h):
            P = Pm[h % 2]
            rden_, lnd_ = rs_pair()
            bkN, bkD = k.bank(), k.bank()
            for mc in range(2):
                k.mm(bkN[:, 0:n], mvb[l][:, mc, h * 128:(h + 1) * 128], P[:, mc, 0:n], start=(mc == 0), stop=(mc == 1))
            for mc in range(2):
                k.mm(bkD[:, 0:n], ones_b, P[:, mc, 0:n], start=(mc == 0), stop=(mc == 1))
            k.act(lnd_[:, 0:n], bkD[:, 0:n], AF.Ln)
            k.release(bkD)
            k.act(rden_[:, 0:n], lnd_[:, 0:n], AF.Exp, scale=-1.0)
            k.tt("dve", cat[:, 4 + h, 0:n], bkN[:, 0:n], rden_[:, 0:n], ALU.mult)
            k.release(bkN)

        stage_a(0)
        for h in range(4):
            if h + 1 < 4:
                stage_a(h + 1)
            stage_b(h)

    def mem_attn_sample(l, qm, cat, mks2, mksT, mvs2, Pms):
        def loads(b):
            k.dma("pool", mks2[b % 2][:, :, :], cmk[l, b].rearrange("(c p) f -> p c f", p=128))
            k.dma("pool", mvs2[b % 2][:, :, :], cmv[l, b].rearrange("(c p) f -> p c f", p=128))

        loads(0)
        for b in range(BPC):
            if b + 1 < BPC:
                loads(b + 1)
            mks, mvs = mks2[b % 2], mvs2[b % 2]
            bt = k.bank()
            btb = bt.v(lambda h: h[:, :].bitcast(BF16))
            for h in range(4):
                for mc in range(2):
                    o = h * 256 + mc * 128
                    k.tr(A(btb.ap[:, o:o + 128], btb.bufs), mks[:, mc, h * 128:(h + 1) * 128], ident_b)
            evac(mksT[:, :, :], A(btb.ap[:, 0:1024].rearrange("p (h m) -> p h m", h=4), btb.bufs))
            k.release(bt)
            bkS = k.bank()
            for mc in range(2):
                for h in range(4):
                    o = (mc * 4 + h) * 8
                    k.mm(bkS[:, o:o + 8], mksT[:, h, mc * 128:(mc + 1) * 128], qm[:, h, b * 8:(b + 1) * 8])
            k.act(Pms[:, 0:64], bkS[:, 0:64], AF.Exp, scale=rsq128)
            k.release(bkS)
            bkN = k.bank()
            first = True
            for h in range(4):
                for mc in range(2):
                    o = (mc * 4 + h) * 8
                    k.mm(bkN[:, h * 8:(h + 1) * 8], mvs[:, mc, h * 128:(h + 1) * 128], Pms[:, o:o + 8],
                         start=first, stop=(mc == 1), sgc=True)
                    first = False
            for mc in range(2):
                k.mm(bkN[:, 32:64], ones_b, Pms[:, mc * 32:(mc + 1) * 32], start=False, stop=(mc == 1), sgc=True)
            k.act(small[:, 64:96], bkN[:, 32:64], AF.Ln)
            k.act(small[:, 96:128], small[:, 64:96], AF.Exp, scale=-1.0)
            k.tt("dve", cat[:, 4:8, b * 8:(b + 1) * 8],
                 bkN.v(lambda h: h[:, 0:32].rearrange("p (h t) -> p h t", h=4)),
                 small.v(lambda h: h[:, 96:128].rearrange("p (h t) -> p h t", h=4)), ALU.mult)
            k.release(bkN)

    def out_proj(i, wo, cat):
        n = TILES[i][1]
        for c in range(8):
            bk = k.bank()
            for kk in range(8):
                k.mm(bk[:, 0:n], wo[:, kk, c * 128:(c + 1) * 128], cat[:, kk, 0:n], start=(kk == 0), stop=(kk == 7))
            k.tt("dve", hT[i][:, c, :], bk[:, 0:n], hT[i][:, c, :], ALU.add)
            k.release(bk)

    def mixer0():
        l = 0
        ar = Arena()
        wsq = [ar.alloc(uname("wsq"), [128, 8, D], BF16) for _ in range(2)]
        wpl = ar.alloc(uname("wpl"), [128, 4, 128], BF16)
        zte = ar.alloc(uname("zte"), [128, 4, 528], F32)
        sA = ar.alloc(uname("sA"), [128, 528], F32)
        sB = ar.alloc(uname("sB"), [128, 528], F32)
        yT = ar.alloc(uname("yT"), [128, 4, 512], BF16)
        zm = [ar.alloc(uname("zm"), [128, 512], F32) for _ in range(2)]
        qm = ar.alloc(uname("qm"), [128, 4, 512], BF16)
        Pm = [ar.alloc(uname("Pm"), [128, 2, 512], BF16) for _ in range(2)]
        lnd = rden = None
        ytmp = ar.alloc(uname("ytmp"), [128, 16], F32)
        a0 = Arena(xn[0].buf.lo, xn[0].buf.hi)
        zes = a0.alloc(uname("zes"), [128, 4, BPC, 24], F32)
        mks = a0.alloc(uname("mks"), [128, 2, 512], BF16)
        a1 = Arena(xn[1].buf.lo, xn[1].buf.hi)
        mksT = a1.alloc(uname("mksT"), [128, 4, NMEM], BF16)
        mvs = a1.alloc(uname("mvs"), [128, 2, 512], BF16)
        zst = a1.alloc(uname("zst"), [128, 4, 240], F32)
        Pms = a1.alloc(uname("Pms"), [128, 64], BF16)
        xr = [xn[2], xn[3]]
        mks2 = [mks, ar.alloc(uname("mksb"), [128, 2, 512], BF16)]
        mvs2 = [mvs, Arena(xn[4].buf.lo, xn[4].buf.hi).alloc(uname("mvsb"), [128, 2, 512], BF16)]

        load_wsq(wsq[0], w_in[l])
        load_wsq(wsq[1], w_out[l])
        k.dma("pool", wpl[:, :, :], w_pool.rearrange("g c d -> c g d"))
        k.memset("dve", zte[:, :, :], 0.0)
        k.memset("dve", zes[:, :, :, :], 0.0)
        s = staging()
        k.dma("sp", s[0:120, 0:512], spool[0:120, :])
        k.dma("sp", s[0:120, 512:1024], spool[120:240, :])
        for hb in range(2):
            bk = k.bank()
            for g in range(4):
                k.tr(bk[:, g * 128:g * 128 + 120], s[0:120, hb * 512 + g * 128:hb * 512 + (g + 1) * 128],
                     A(ident.ap[0:120, 0:120], ident.bufs))
            for g in range(4):
                evac(zes[:, g, hb * 8:(hb + 1) * 8, 1:16],
                     bk.v(lambda h, g=g: h[:, g * 128:g * 128 + 120].rearrange("p (b j) -> p b j", b=8)))
            k.release(bk)

        ksub = int(os.environ.get('KSUB', '99'))
        if ksub <= 1:
            return
        for ti, i in enumerate((4, 0, 1, 2, 3)):
            n = TILES[i][1]
            samp = (i == 4)
            X = xr[ti % 2]
            norm_tile(i, V_GMIX + l * 8, X)
            def fin0(bk, c):
                if c < 4:
                    if samp:
                        evac(zes[:, c, :, 16:24], bk.v(lambda h: h[:, 0:128].rearrange("p (b t) -> p b t", b=BPC)))
                    else:
                        evac(zte[:, c, 16:528], bk[:, 0:512])
                    k.release(bk)
                else:
                    head_norm(bk, n, zm[c % 2], 128, ones_b, V_GMQ + l, qm[:, c - 4, 0:n])

            pend = None
            for c in range(8):
                bk = k.bank()
                for kk in range(8):
                    k.mm(bk[:, 0:n], wsq[0][:, kk, c * 128:(c + 1) * 128], X[:, kk, 0:n], start=(kk == 0), stop=(kk == 7))
                if pend is not None:
                    fin0(*pend)
                pend = (bk, c)
            fin0(*pend)
            if ksub <= 2:
                return
            cat = X
            for g in range(4):
                w = 2 << g
                eng = "pool" if g < 3 else "dve"
                if samp:
                    zz = lambda a, b_, g=g: zes[:, g, :, a:b_]
                    SA = lambda a, b_: sA.v(lambda h: h[:, 0:384].rearrange("p (b c) -> p b c", b=BPC)[:, :, a:b_])
                    SB = lambda a, b_: sB.v(lambda h: h[:, 0:384].rearrange("p (b c) -> p b c", b=BPC)[:, :, a:b_])
                    W_ = 24
                else:
                    zz = lambda a, b_, g=g: zte[:, g, a:b_]
                    SA = lambda a, b_: sA[:, a:b_]
                    SB = lambda a, b_: sB[:, a:b_]
                    W_ = 528
                k.tt(eng, SA(1, W_), zz(1, W_), zz(0, W_ - 1), ALU.add)
                S = SA
                if g >= 1:
                    k.tt(eng, SB(3, W_), SA(3, W_), SA(1, W_ - 2), ALU.add)
                    S = SB
                if g >= 2:
                    k.tt(eng, SA(7, W_), SB(7, W_), SB(3, W_ - 4), ALU.add)
                    S = SA
                if g >= 3:
                    k.tt(eng, SB(15, W_), SA(15, W_), SA(7, W_ - 8), ALU.add)
                    S = SB
                if samp:
                    k.stt(yT.v(lambda h, g=g: h[:, g, 0:128].rearrange("p (b t) -> p b t", b=BPC)),
                          S(16, 24), 1.0 / w, zz(16, 24), ALU.mult, ALU.subtract)
                else:
                    k.stt(yT[:, g, 0:512], S(16, 528), 1.0 / w, zz(16, 528), ALU.mult, ALU.subtract)
                    if i == 0:
                        k.tt("dve", ytmp[:, :], S(16, 32), cst[:, C_INVC + g * 16:C_INVC + (g + 1) * 16], ALU.mult)
                        k.tt("dve", yT[:, g, 0:16], ytmp[:, :], zz(16, 32), ALU.subtract)
                bk = k.bank()
                k.mm(bk[:, 0:n], wpl[:, g, :], yT[:, g, 0:n])
                k.act(cat[:, g, 0:n], bk[:, 0:n], AF.Identity, scale=vcol(V_PSC + g))
                k.release(bk)
            if ksub <= 3:
                return
            if samp:
                for g in range(4):
                    k.copy("dve", zst.v(lambda h, g=g: h[:, g, :].rearrange("p (b j) -> p b j", b=BPC)), zes[:, g, :, 9:24])
                for hb in range(2):
                    bk = k.bank()
                    for g in range(4):
                        k.tr(bk[0:120, g * 128:(g + 1) * 128], zst[:, g, hb * 120:(hb + 1) * 120], ident)
                    so = staging()
                    k.copy("act", so[0:120, 0:512], bk[0:120, :])
                    k.release(bk)
                    k.dma("sp", pss[hb * 120:(hb + 1) * 120, :], so[0:120, 0:512], is_out=True)
            else:
                if i == 3:
                    bk = k.bank()
                    for g in range(4):
                        k.copy("dve", zst[:, g, 0:15], zte[:, g, 513:528])
                        k.tr(bk[:, g * 128:(g + 1) * 128], zst[:, g, 0:128], ident)
                    so = staging()
                    k.copy("act", so[0:15, 0:512], bk[0:15, :])
                    k.release(bk)
                    k.dma("sp", psp[:, :], so[0:15, 0:512], is_out=True)
                else:
                    k.copy("dve", ytmp.v(lambda h: h[:, 0:16]), zte[:, 0, 512:528])
                    for g in range(4):
                        k.copy("dve", sA[:, 0:16], zte[:, g, 512:528])
                        k.copy("dve", zte[:, g, 0:16], sA[:, 0:16])
            if ksub <= 4:
                return
            if samp:
                mem_attn_sample(l, qm, cat, mks2, mksT, mvs2, Pms)
            else:
                mem_attn_prompt(l, n, qm, Pm, lnd, rden, cat)
            if ksub <= 5:
                return
            out_proj(i, wsq[1], cat)
            if ksub <= 6:
                return

    mixer0()
    if stage <= 2:
        k.emit()
        return nc

    sm_i = [0]

    def smbuf():
        sm_i[0] += 1
        return smalls[sm_i[0] % 4]

    def ffn(l):
        ar = Arena()
        aT = [ar.alloc(uname("aT"), [128, 4, n], BF16) for (_, n) in TILES]
        wgu = [ar.alloc(uname("wgu"), [128, 8, 2, 256], BF16) for _ in range(3)]
        wdn = [ar.alloc(uname("wdn"), [128, 4, D], BF16) for _ in range(2)]
        sg = [ar.alloc(uname("sg"), [128, 512], F32) for _ in range(2)]
        for i in range(NT):
            norm_tile(i, V_GFFN + l * 8, xn[i])
        groups = [(0, 4), (4, 4), (8, 4), (12, 4), (16, 4), (20, 2)]
        blk = 0
        cnt = 0
        for gi, (j0, nj) in enumerate(groups):
            wd = wdn[gi % 2]
            k.dma("pool", wd[:, 0:nj, :], w_down[l][j0 * 128:(j0 + nj) * 128, :].rearrange("(j p) n -> p j n", p=128))
            for jb in range(nj // 2):
                j = j0 + 2 * jb
                ws = wgu[blk % 3]
                blk += 1
                k.dma("pool", ws[:, :, 0, :], w_gu[l][:, j * 128:j * 128 + 256].rearrange("(k p) n -> p k n", p=128))
                k.dma("pool", ws[:, :, 1, :], w_gu[l][:, DFF + j * 128:DFF + j * 128 + 256].rearrange("(k p) n -> p k n", p=128))
                for i in range(NT):
                    n = TILES[i][1]
                    for jj in range(2):
                        bkG = k.bank()
                        for kk in range(8):
                            k.mm(bkG[:, 0:n], ws[:, kk, 0, jj * 128:(jj + 1) * 128], xn[i][:, kk, :], start=(kk == 0), stop=(kk == 7))
                        bkU = k.bank()
                        for kk in range(8):
                            k.mm(bkU[:, 0:n], ws[:, kk, 1, jj * 128:(jj + 1) * 128], xn[i][:, kk, :], start=(kk == 0), stop=(kk == 7))
                        s_ = sg[cnt % 2]
                        cnt += 1
                        k.act(s_[:, 0:n], bkG[:, 0:n], AF.Silu)
                        k.release(bkG)
                        k.tt("dve", aT[i][:, 2 * jb + jj, :], bkU[:, 0:n], s_[:, 0:n], ALU.mult)
                        k.release(bkU)
            for i in range(NT):
                n = TILES[i][1]
                for c in range(8):
                    bk = k.bank()
                    for jj in range(nj):
                        k.mm(bk[:, 0:n], wd[:, jj, c * 128:(c + 1) * 128], aT[i][:, jj, :], start=(jj == 0), stop=(jj == nj - 1))
                    k.tt("dve", hT[i][:, c, :], bk[:, 0:n], hT[i][:, c, :], ALU.add)
                    k.release(bk)

    ffn(0)

    axn = Arena(xn[0].buf.lo, xn[4].buf.hi)
    KTp = [axn.alloc(uname("KTp"), [128, 4, 512], BF16) for _ in range(4)]
    Vp = [axn.alloc(uname("Vp"), [128, 4, 8, 65], BF16) for _ in range(4)]

    def shared_kv():
        ar = Arena()
        wkv = ar.alloc(uname("wkv"), [128, 8, D], BF16)
        wfg = ar.alloc(uname("wfg"), [128, 8, 8], BF16)
        xk = ar.alloc(uname("xk"), [128, 8, 512], BF16)
        ktoks = [ar.alloc(uname("ktok"), [128, 512], F32) for _ in range(2)]
        ktmps = [ar.alloc(uname("ktmp"), [128, 512], F32) for _ in range(2)]
        kb16s = [ar.alloc(uname("kb16"), [128, 512], BF16) for _ in range(2)]
        chunk_i = [0]
        load_wsq(wkv, w_kv)
        k.dma("pool", wfg[:, :, :], w_fg.rearrange("(k p) h -> p k h", p=128))
        for i in range(4):
            k.memset("dve", Vp[i][:, :, :, 64:65], 1.0)
        h3 = lambda t, a=0, b_=512: t.v(lambda h: h[:, a:b_].rearrange("p (h d) -> p h d", h=8))
        for i in (4, 0, 1, 2, 3):
            n = TILES[i][1]
            samp = (i == 4)
            norm_tile(i, V_GKV, xk)
            for tcn in range(n // 128):
                gc = i * 4 + tcn
                r0 = TILES[i][0] + tcn * 128
                cs = slice(tcn * 128, (tcn + 1) * 128)
                chunk_i[0] += 1
                ktok, ktmp, kb16 = ktoks[chunk_i[0] % 2], ktmps[chunk_i[0] % 2], kb16s[chunk_i[0] % 2]
                bkK, bkV, bkF = k.bank(), k.bank(), k.bank()
                for kk in range(8):
                    k.mm(bkK[:, :], xk[:, kk, cs], wkv[:, kk, 0:512], start=(kk == 0), stop=(kk == 7))
                for kk in range(8):
                    k.mm(bkV[:, :], xk[:, kk, cs], wkv[:, kk, 512:1024], start=(kk == 0), stop=(kk == 7))
                for kk in range(8):
                    k.mm(bkF[:, 0:8], xk[:, kk, cs], wfg[:, kk, :], start=(kk == 0), stop=(kk == 7))
                sm = smbuf()
                sv = staging()
                k.copy("act", sv[:, 0:512], bkV[:, :])
                k.release(bkV)
                if samp:
                    k.dma("sp", vrs[:, :], sv[:, 0:512], is_out=True)
                    k.copy("dve", Vs[:, :], sv[:, 0:512])
                else:
                    k.dma("sp", vrp[r0:r0 + 128, :], sv[:, 0:512], is_out=True)
                    k.copy("dve", Vp[i][:, tcn, :, 0:64], h3(sv))
                k.copy("act", ktok[:, :], bkK[:, :])
                k.release(bkK)
                k.tt("dve", ktmp[:, :], ktok[:, :], ktok[:, :], ALU.mult)
                k.red(sm[:, 0:8], h3(ktmp))
                k.act(sm[:, 8:16], sm[:, 0:8], AF.Ln, scale=1.0 / 64, bias=EPS)
                k.act(sm[:, 16:24], sm[:, 8:16], AF.Exp, scale=-0.5)
                k.tt("dve", h3(ktmp), h3(ktok), sm.v(lambda h: h[:, 16:24].unsqueeze(2).to_broadcast([128, 8, 64])), ALU.mult)
                sk = staging()
                k.tt("dve", h3(sk), h3(ktmp), vecs.v(lambda h: h[:, V_GFK:V_GFK + 64].unsqueeze(1).to_broadcast([128, 8, 64])), ALU.mult)
                if samp:
                    k.dma("sp", krs[:, :], sk[:, 0:512], is_out=True)
                else:
                    k.dma("sp", krp[r0:r0 + 128, :], sk[:, 0:512], is_out=True)
                k.copy("act", kb16[:, :], sk[:, 0:512])
                bt = k.bank()
                btb = bt.v(lambda h: h[:, :].bitcast(BF16))
                for c in range(4):
                    k.tr(A(btb.ap[:, c * 128:(c + 1) * 128], btb.bufs), kb16[:, c * 128:(c + 1) * 128], ident_b)
                src = A(btb.ap[:, 0:512].rearrange("p (c t) -> p c t", c=4), btb.bufs)
                if samp:
                    evac(KTs[:, :, :], src)
                else:
                    evac(KTp[i][:, :, cs], src)
                k.release(bt)
                k.tt("dve", sm[:, 24:32], bkF[:, 0:8], vecs[:, V_BFG:V_BFG + 8], ALU.add)
                k.release(bkF)
                k.act(sm[:, 32:40], sm[:, 24:32], AF.Exp, scale=-1.0)
                k.act(sm[:, 40:48], sm[:, 32:40], AF.Ln, bias=1.0)
                if samp:
                    k.ts("dve", lf_s[:, :], sm[:, 40:48], -1.0, ALU.mult)
                    k.dma("sp", lfs[:, :], lf_s[:, :], is_out=True)
                    bkc = k.bank()
                    k.mm(bkc[:, 0:8], cst[:, C_BT:C_BT + 128], lf_s[:, :])
                    k.ts("dve", nfnew[:, :], bkc[:, 0:8], -1.0, ALU.mult)
                    k.release(bkc)
                else:
                    k.ts("dve", lf_all[:, gc, :], sm[:, 40:48], -1.0, ALU.mult)
                    k.dma("sp", lfp[r0:r0 + 128, :], lf_all[:, gc, :], is_out=True)
                    bkc = k.bank()
                    k.mm(bkc[:, 0:8], cst[:, C_UT:C_UT + 128], lf_all[:, gc, :], start=True, stop=(gc == 0))
                    if gc > 0:
                        k.mm(bkc[:, 0:8], ones_f, lfsum[:, :], start=False, stop=True)
                    k.copy("act", fcum[:, gc, :], bkc[:, 0:8])
                    k.release(bkc)
                    if gc == 0:
                        k.copy("dve", lfsum[:, :], lf_all[:, 0, :])
                    else:
                        k.tt("dve", lfsum[:, :], lfsum[:, :], lf_all[:, gc, :], ALU.add)
                    if tcn == 3:
                        bke = k.bank()
                        k.mm(bke[:, 0:8], cst[:, C_E127:C_E127 + 128], fcum[:, gc, :])
                        k.copy("act", fend[:, i, :], bke[:, 0:8])
                        k.release(bke)

    shared_kv()
    if stage <= 3:
        k.emit()
        return nc

    def mixer1():
        l = 1
        ar = Arena()
        wsq = [ar.alloc(uname("wsq"), [128, 8, D], BF16) for _ in range(2)]
        X = ar.alloc(uname("X1"), [128, 8, 512], BF16)
        zm = [ar.alloc(uname("zm"), [128, 512], F32) for _ in range(2)]
        qm = ar.alloc(uname("qm"), [128, 4, 512], BF16)
        Pm = [ar.alloc(uname("Pm"), [128, 2, 512], BF16) for _ in range(2)]
        lnd = rden = None
        biasJ = ar.alloc(uname("biasJ"), [128, 16, 8], F32)
        p0 = ar.p
        qT = ar.alloc(uname("qT"), [128, 4, 512], BF16)
        PT = [ar.alloc(uname("PT"), [128, 512], BF16) for _ in range(3)]
        otok = ar.alloc(uname("otok"), [128, 4, 8, 64], BF16)
        arw = Arena(wsq[0].buf.lo, wsq[0].buf.hi)
        KTg = arw.alloc(uname("KTg"), [128, 4, 8, 128], BF16)
        mks = arw.alloc(uname("mks"), [128, 2, 512], BF16)
        mksT = arw.alloc(uname("mksT"), [128, 4, NMEM], BF16)
        mvs = arw.alloc(uname("mvs"), [128, 2, 512], BF16)
        qbd = arw.alloc(uname("qbd"), [128, 4, BPC, 16], BF16)
        axs = Arena(xn[0].buf.lo, xn[4].buf.hi)
        Kg = [axs.alloc(uname("Kg"), [128, 4, 512], BF16) for _ in range(4)]
        Vg = [axs.alloc(uname("Vg"), [128, 4, 512], BF16) for _ in range(4)]
        ars = Arena(p0)
        qTs = ars.alloc(uname("qTs"), [128, 4, NSAMP], BF16)
        ptb = ars.alloc(uname("ptb"), [128, BPC], I32)
        itmp = ars.alloc(uname("itmp"), [128, BPC], I32)
        idx8 = ars.alloc(uname("idx8"), [128, BPC], I32)
        idx32 = ars.alloc(uname("idx32"), [128, BPC, 4], I32)
        rg4 = ars.alloc(uname("rg4"), [128, 1], I32)
        lfgs = [ars.alloc(uname("lfg"), [128, 16, 8], F32) for _ in range(2)]
        biasbs = [ars.alloc(uname("biasb"), [128, 16, 8], F32) for _ in range(2)]
        tmpS = ars.alloc(uname("tmpS"), [128, 512], F32)
        Pg = ars.alloc(uname("Pg"), [128, 512], BF16)
        Pn = ars.alloc(uname("Pn"), [128, 64], BF16)
        Pms = ars.alloc(uname("Pms"), [128, 64], BF16)
        mks2 = [mks, ars.alloc(uname("mksb"), [128, 2, 512], BF16)]
        mvs2 = [mvs, ars.alloc(uname("mvsb"), [128, 2, 512], BF16)]

        load_wsq(wsq[0], w_in[l])
        load_wsq(wsq[1], w_out[l])

        def in_proj(i, n, qdst):
            norm_tile(i, V_GMIX + l * 8, X)
            def fin1(bk, c):
                if c < 4:
                    head_norm(bk, n, zm[c % 2], 64, bd_b, V_GFQ, qdst[:, c, 0:n])
                else:
                    head_norm(bk, n, zm[c % 2], 128, ones_b, V_GMQ + l, qm[:, c - 4, 0:n])

            pend = None
            for c in range(8):
                bk = k.bank()
                for kk in range(8):
                    k.mm(bk[:, 0:n], wsq[0][:, kk, c * 128:(c + 1) * 128], X[:, kk, 0:n], start=(kk == 0), stop=(kk == 7))
                if pend is not None:
                    fin1(*pend)
                pend = (bk, c)
            fin1(*pend)

        ks1 = float(os.environ.get('KSUB1', '99'))
        for J in range(4):
            n = 512
            in_proj(J, n, qT)
            if ks1 <= 1:
                return
            cat = X
            nk = 4 * J + 4
            k.tt("dve", biasJ[:, 0:nk, :], fend.v(lambda h: h[:, J, :].unsqueeze(1).to_broadcast([128, nk, 8])),
                 fcum[:, 0:nk, :], ALU.subtract)
            for h in range(8):
                c, hp = h // 2, slice((h % 2) * 64, (h % 2) * 64 + 64)
                acc = k.bank()
                accv = acc.v(lambda hh: hh[:, 0:260].rearrange("p (q e) -> p q e", q=4))
                banks = {}

                def S_E(kc):
                    ti, tcn = kc // 4, kc % 4
                    nq0 = max(0, kc - 4 * J) * 128
                    bkS = k.bank()
                    k.mm(bkS[:, nq0:512], A(KTp[ti].h[hp, c, tcn * 128:(tcn + 1) * 128], [KTp[ti].buf]),
                         A(qT.h[hp, c, nq0:512], [qT.buf]))
                    P = PT[kc % 3]
                    k.act(P[:, nq0:512], bkS[:, nq0:512], AF.Exp, scale=0.125, bias=biasJ[:, kc, h:h + 1])
                    k.release(bkS)
                    if kc >= 4 * J:
                        k.tt("pool", P[:, nq0:nq0 + 128], P[:, nq0:nq0 + 128], ut_b, ALU.mult)

                S_E(0)
                first = True
                for kc in range(nk):
                    if kc + 1 < nk:
                        S_E(kc + 1)
                    ti, tcn = kc // 4, kc % 4
                    nq0 = max(0, kc - 4 * J) * 128
                    P = PT[kc % 3]
                    for tq in range(nq0 // 128, 4):
                        k.mm(A(accv.ap[:, tq, :], accv.bufs), P[:, tq * 128:(tq + 1) * 128], Vp[ti][:, tcn, h, :],
                             start=first, stop=(kc == 4 * J + tq), sgc=True)
                        first = False
                sm = smbuf()
                k.recip(sm.v(lambda hh: hh[:, 0:4].unsqueeze(2)), A(accv.ap[:, :, 64:65], accv.bufs))
                k.tt("dve", otok[:, :, h, :], A(accv.ap[:, :, 0:64], accv.bufs),
                     sm.v(lambda hh: hh[:, 0:4].unsqueeze(2).to_broadcast([128, 4, 64])), ALU.mult)
                k.release(acc)
            if ks1 <= 2:
                return
            for tp in range(2):
                bt = k.bank()
                btb = bt.v(lambda hh: hh[:, :].bitcast(BF16))
                for tq2 in range(2):
                    tq = tp * 2 + tq2
                    for c in range(4):
                        o = (tq2 * 4 + c) * 128
                        k.tr(A(btb.ap[:, o:o + 128], btb.bufs),
                             otok.v(lambda hh, tq=tq, c=c: hh[:, tq, 2 * c:2 * c + 2, :].rearrange("p a d -> p (a d)")), ident_b)
                for tq2 in range(2):
                    tq = tp * 2 + tq2
                    evac(cat[:, 0:4, tq * 128:(tq + 1) * 128],
                         A(btb.ap[:, tq2 * 512:(tq2 + 1) * 512].rearrange("p (c t) -> p c t", c=4), btb.bufs))
                k.release(bt)
            mem_attn_prompt(l, n, qm, Pm, lnd, rden, cat)
            out_proj(J, wsq[1], cat)
            if ks1 <= 3:
                return
        if ks1 <= 4:
            return

        n = NSAMP
        in_proj(4, n, qTs)
        cat = X
        k.memset("dve", qbd[:, :, :, :], 0.0)
        for c in range(4):
            k.copy("dve", A(qbd.h[0:64, c, :, 0:8], [qbd.buf]),
                   A(qTs.h[0:64, c, :].rearrange("p (b t) -> p b t", b=BPC), [qTs.buf]))
            k.copy("dve", A(qbd.h[64:128, c, :, 8:16], [qbd.buf]),
                   A(qTs.h[64:128, c, :].rearrange("p (b t) -> p b t", b=BPC), [qTs.buf]))
        k.dma("sp", ptb[:, :], ptab[:, :])
        k.ts("pool", itmp[:, :], ptb[:, :], 8, ALU.mult)
        k.tt("pool", idx8[:, :], itmp[:, :], cint.v(lambda hh: hh[:, 0:1].to_broadcast([128, BPC])), ALU.add)
        k.ts("pool", rg4[:, :], cint[:, :], 4, ALU.mult)
        k.ts("pool", itmp[:, :], ptb[:, :], 32, ALU.mult)
        k.tt("pool", idx32[:, :, 0], itmp[:, :], rg4.v(lambda hh: hh[:, 0:1].to_broadcast([128, BPC])), ALU.add)
        for j in range(1, 4):
            k.ts("pool", idx32[:, :, j], idx32[:, :, 0], j, ALU.add)
        if ks1 <= 5:
            return
        for b in range(BPC):
            lfg, biasb = lfgs[b % 2], biasbs[b % 2]
            k.gather(lfg.v(lambda hh: hh[:, :, :].rearrange("p r h -> p (r h)")), clf, idx8[:, b:b + 1])
            for r in range(1, 16):
                k.tt("dve", lfg[:, r, :], lfg[:, r, :], lfg[:, r - 1, :], ALU.add)
            bkb = k.bank()
            k.mm(bkb[:, 0:8], cst[:, C_GE:C_GE + 128], lfg[:, 15, :])
            k.tt("dve", biasb[:, :, :], bkb.v(lambda hh: hh[:, 0:8].unsqueeze(1).to_broadcast([128, 16, 8])),
                 lfg[:, :, :], ALU.subtract)
            k.release(bkb)
            if ks1 <= 6:
                return
            bkO = k.bank()
            first = [True, True]
            for hf in range(2):
                for jj in range(2):
                    j = hf * 2 + jj
                    k.gather(Kg[j].v(lambda hh: hh[:, :, :].rearrange("p r f -> p (r f)")), ck, idx32[:, b, j:j + 1])
                    k.gather(Vg[j].v(lambda hh: hh[:, :, :].rearrange("p r f -> p (r f)")), cv, idx32[:, b, j:j + 1])
                kgr = lambda r, hf=hf: Kg[hf * 2 + r // 4]
                vgr = lambda r, hf=hf: Vg[hf * 2 + r // 4]
                if ks1 <= 6.1:
                    return
                for c in range(4):
                    bt = k.bank()
                    btb = bt.v(lambda hh: hh[:, :].bitcast(BF16))
                    for r in range(8):
                        k.tr(A(btb.ap[:, r * 128:(r + 1) * 128], btb.bufs), kgr(r)[:, r % 4, c * 128:(c + 1) * 128], ident_b)
                    evac(KTg[:, c, :, :], A(btb.ap[:, 0:1024].rearrange("p (r s) -> p r s", r=8), btb.bufs))
                    k.release(bt)
                if ks1 <= 6.2:
                    return
                bkS = k.bank()
                for r in range(8):
                    for c in range(4):
                        o = (r * 4 + c) * 16
                        k.mm(bkS[:, o:o + 16], KTg[:, c, r, :], qbd[:, c, b, :])
                k.stt(tmpS.v(lambda hh: hh[:, :].rearrange("p (x t) -> p x t", t=8)),
                      bkS.v(lambda hh: hh[:, :].rearrange("p (x t) -> p x t", t=8)), 0.125,
                      biasb.v(lambda hh, hf=hf: hh[:, hf * 8:(hf + 1) * 8, :].rearrange("p r h -> p (r h)").unsqueeze(2).to_broadcast([128, 64, 8])),
                      ALU.mult, ALU.add)
                k.release(bkS)
                k.act(Pg[:, :], tmpS[:, :], AF.Exp)
                if ks1 <= 6.3:
                    return
                Pg5 = lambda r, c, hh_: A(Pg.h[:, (r * 8 + c * 2 + hh_) * 8:(r * 8 + c * 2 + hh_) * 8 + 8], [Pg.buf])
                for r in range(8):
                    for h in range(8):
                        c, hh_ = h // 2, h % 2
                        k.mm(A(bkO.h[hh_ * 64:(hh_ + 1) * 64, c * 8:(c + 1) * 8], [bkO.buf]),
                             vgr(r)[:, r % 4, h * 64:(h + 1) * 64], Pg5(r, c, hh_), start=first[hh_], stop=False, sgc=True)
                        first[hh_] = False
                    for hh_ in range(2):
                        k.mm(A(bkO.h[hh_ * 64:(hh_ + 1) * 64, 32:64].rearrange("p (c t) -> p c t", c=4), [bkO.buf]),
                             A(ones_b.ap[:, 0:64], ones_b.bufs),
                             A(Pg.h[:, r * 64:(r + 1) * 64].rearrange("p (c a t) -> p c a t", c=4, a=2)[:, :, hh_, :], [Pg.buf]),
                             start=False, stop=False, sgc=True)
            if ks1 <= 6.4:
                return
            bkS2 = k.bank()
            for c in range(4):
                k.mm(bkS2[:, c * 16:(c + 1) * 16], KTs[:, c, :], qbd[:, c, b, :])
            k.stt(tmpS.v(lambda hh: hh[:, 0:64].rearrange("p (x t) -> p x t", t=8)),
                  bkS2.v(lambda hh: hh[:, 0:64].rearrange("p (x t) -> p x t", t=8)), 0.125,
                  nfnew.v(lambda hh: hh[:, :].unsqueeze(2).to_broadcast([128, 8, 8])), ALU.mult, ALU.add)
            k.release(bkS2)
            k.act(Pn[:, :], tmpS[:, 0:64], AF.Exp)
            k.tt("dve", Pn.v(lambda hh: hh[:, :].rearrange("p (x t) -> p x t", t=8)),
                 Pn.v(lambda hh: hh[:, :].rearrange("p (x t) -> p x t", t=8)),
                 masks_b.v(lambda hh, b=b: hh[:, b * 8:(b + 1) * 8].unsqueeze(1).to_broadcast([128, 8, 8])), ALU.mult)
            for h in range(8):
                c, hh_ = h // 2, h % 2
                k.mm(A(bkO.h[hh_ * 64:(hh_ + 1) * 64, c * 8:(c + 1) * 8], [bkO.buf]),
                     Vs[:, h * 64:(h + 1) * 64], Pn[:, h * 8:(h + 1) * 8], start=False, stop=True, sgc=True)
            for hh_ in range(2):
                k.mm(A(bkO.h[hh_ * 64:(hh_ + 1) * 64, 32:64].rearrange("p (c t) -> p c t", c=4), [bkO.buf]),
                     A(ones_b.ap[:, 0:64], ones_b.bufs),
                     A(Pn.h[:, :].rearrange("p (c a t) -> p c a t", c=4, a=2)[:, :, hh_, :], [Pn.buf]),
                     start=False, stop=True, sgc=True)
            sm = smbuf()
            k.act(sm[:, 0:32], bkO[:, 32:64], AF.Ln)
            k.act(sm[:, 32:64], sm[:, 0:32], AF.Exp, scale=-1.0)
            k.tt("dve", cat[:, 0:4, b * 8:(b + 1) * 8],
                 bkO.v(lambda hh: hh[:, 0:32].rearrange("p (c t) -> p c t", c=4)),
                 sm.v(lambda hh: hh[:, 32:64].rearrange("p (c t) -> p c t", c=4)), ALU.mult)
            k.release(bkO)
            if ks1 <= 7:
                return
        mem_attn_sample(l, qm, cat, mks2, mksT, mvs2, Pms)
        out_proj(4, wsq[1], cat)

    mixer1()
    if float(os.environ.get('KSUB1', '99')) < 99:
        k.emit()
        return nc
    ffn(1)

    for i, (c0, n) in enumerate(TILES):
        for tcn in range(n // 128):
            so = staging()
            for half in range(2):
                bk = k.bank()
                for j in range(4):
                    k.tr(bk[:, j * 128:(j + 1) * 128], hT[i][:, half * 4 + j, tcn * 128:(tcn + 1) * 128], ident)
                evac(so[:, half * 512:(half + 1) * 512], bk[:, :])
                k.release(bk)
            if i < 4:
                k.dma("sp", y_p[c0 + tcn * 128:c0 + (tcn + 1) * 128, :], so[:, :], is_out=True)
            else:
                k.dma("sp", y_s[:, :], so[:, :], is_out=True)
    k.emit()
    return nc


def _consts():
    c = np.zeros((128, NCST), np.float32)
    p = np.arange(128)
    c[:, C_ID:C_ID + 128] = np.eye(128, dtype=np.float32)
    c[:, C_ONES:C_ONES + 128] = 1.0
    c[:, C_BD:C_BD + 128] = (p[:, None] // 64 == p[None, :] // 64)
    c[:, C_UT:C_UT + 128] = (p[:, None] <= p[None, :])
    c[127, C_E127:C_E127 + 128] = 1.0
    c[:, C_GE:C_GE + 128] = (p[:, None] >= p[None, :])
    c[:, C_BT:C_BT + 128] = (p[:, None] // 8 == p[None, :] // 8) & (p[:, None] <= p[None, :])
    for g, w in enumerate((2, 4, 8, 16)):
        t = np.arange(16)
        c[:, C_INVC + g * 16:C_INVC + (g + 1) * 16] = 1.0 / np.minimum(w, t + 1)
    m = np.zeros((128, 16, 8), np.float32)
    for b in range(16):
        for t in range(8):
            m[:, b, t] = (p // 8 == b) & (p % 8 <= t)
    c[:, C_MASKS:C_MASKS + 128] = m.reshape(128, 128)
    ci = (p % 8).astype(np.int32).reshape(128, 1)
    return c, ci


def _vecs(g_mix, g_ffn, g_mem, g_kv, g_mem_q, g_mem_k, pool_scale, g_fox_k, g_fox_q, b_fg):
    v = np.zeros((128, NV), np.float32)
    fm = lambda a: np.ascontiguousarray(a.reshape(8, 128).T)
    for l in range(2):
        v[:, V_GMIX + l * 8:V_GMIX + (l + 1) * 8] = fm(g_mix[l])
        v[:, V_GFFN + l * 8:V_GFFN + (l + 1) * 8] = fm(g_ffn[l])
        v[:, V_GMEM + l * 8:V_GMEM + (l + 1) * 8] = fm(g_mem[l])
        v[:, V_GMQ + l] = g_mem_q[l]
        v[:, V_GMK + l * 128:V_GMK + (l + 1) * 128] = np.broadcast_to(g_mem_k[l][None, :], (128, 128))
    v[:, V_GKV:V_GKV + 8] = fm(g_kv)
    v[:, V_PSC:V_PSC + 4] = pool_scale[0].reshape(4, 128).T
    v[:, V_GFQ] = np.tile(g_fox_q[0], 2)
    v[:, V_GFK:V_GFK + 64] = np.broadcast_to(g_fox_k[None, :], (128, 64))
    v[:, V_BFG:V_BFG + 8] = np.broadcast_to(b_fg[None, :], (128, 8))
    return v


_NC_CACHE = {}


def kernel(x_prompt, x_sample, cache_mem_k, cache_mem_v, state_pool, cache_k, cache_v, cache_logf,
           page_table, mem_prompt, g_mix, w_in, w_out, g_ffn, w_gu, w_down, g_mem, w_mem_kv,
           g_mem_q, g_mem_k, w_pool, pool_scale, g_kv, w_kv, g_fox_k, w_fg, b_fg, g_fox_q,
           _stage=99):
    f = lambda a: np.ascontiguousarray(np.asarray(a, dtype=np.float32))
    x_prompt, x_sample, mem_prompt = f(x_prompt), f(x_sample), f(mem_prompt)
    cache_mem_k, cache_mem_v, state_pool = f(cache_mem_k), f(cache_mem_v), f(state_pool)
    cache_k, cache_v, cache_logf = f(cache_k), f(cache_v), f(cache_logf)
    page_table = np.ascontiguousarray(np.asarray(page_table, dtype=np.int32))
    nphys = cache_k.shape[0]
    key = (_stage, nphys)
    if key not in _NC_CACHE:
        _NC_CACHE[key] = build_program(_stage, nphys)
    nc = _NC_CACHE[key]
    cst, cint = _consts()
    vecs = _vecs(f(g_mix), f(g_ffn), f(g_mem), f(g_kv), f(g_mem_q), f(g_mem_k), f(pool_scale),
                 f(g_fox_k), f(g_fox_q), f(b_fg))
    ck = cache_k.reshape(nphys * 32, 2048)
    cv = cache_v.reshape(nphys * 32, 2048)
    clf = cache_logf.reshape(nphys * 8, 16 * 8)
    shared = {
        "ck": ck, "cv": cv, "clf": clf,
        "w_in": f(w_in), "w_out": f(w_out), "w_gu": f(w_gu), "w_down": f(w_down), "w_mkv": f(w_mem_kv),
        "w_pool": f(w_pool).reshape(4, 128, 128), "w_kv": f(w_kv), "w_fg": f(w_fg),
        "vecs": vecs, "cst": cst, "cint": cint,
    }
    in_maps = []
    for c in range(NB):
        m = dict(shared)
        bs = slice(c * BPC, (c + 1) * BPC)
        m["xp"] = x_prompt[c]
        m["xs"] = x_sample[bs].reshape(NSAMP, D)
        m["memp"] = mem_prompt[c]
        m["cmk"] = cache_mem_k[:, bs].reshape(2, BPC, NMEM, 512)
        m["cmv"] = cache_mem_v[:, bs].reshape(2, BPC, NMEM, 512)
        m["spool"] = state_pool[0, bs].reshape(BPC * 15, 512)
        m["ptab"] = np.ascontiguousarray(np.repeat(page_table[bs].T, 8, axis=0))
        in_maps.append(m)
    ncores = int(os.environ.get("KCORES", NB))
    res = run_bass_kernel_spmd(nc, in_maps[:ncores], core_ids=list(range(ncores)))
    r = list(res.results)
    while len(r) < NB:
        r.append({kk: np.zeros_like(np.asarray(v)) for kk, v in r[0].items()})
    cat = lambda name: [np.asarray(r[c][name]) for c in range(NB)]
    y_prompt = np.stack(cat("y_p")).reshape(NB, SEQ, D)
    y_sample = np.concatenate(cat("y_s")).reshape(DB, DS, D)
    k_rows_prompt = np.stack(cat("krp")).reshape(NB, SEQ // PAGE, PAGE, 8, 64)
    v_rows_prompt = np.stack(cat("vrp")).reshape(NB, SEQ // PAGE, PAGE, 8, 64)
    logf_rows_prompt = np.stack(cat("lfp")).reshape(NB, SEQ // PAGE, PAGE, 8)
    mem_k_prompt = np.stack(cat("mkp"), axis=1).reshape(2, NB, NMEM, 4, 128)
    mem_v_prompt = np.stack(cat("mvp"), axis=1).reshape(2, NB, NMEM, 4, 128)
    pool_state_prompt = np.stack(cat("psp")).reshape(1, NB, 15, 512)
    k_rows_sample = np.concatenate(cat("krs")).reshape(DB, DS, 8, 64)
    v_rows_sample = np.concatenate(cat("vrs")).reshape(DB, DS, 8, 64)
    logf_rows_sample = np.concatenate(cat("lfs")).reshape(DB, DS, 8)
    pool_state_sample = np.concatenate(cat("pss")).reshape(1, DB, 15, 512)
    return (y_prompt, y_sample, k_rows_prompt, v_rows_prompt, logf_rows_prompt, mem_k_prompt,
            mem_v_prompt, pool_state_prompt, k_rows_sample, v_rows_sample, logf_rows_sample,
            pool_state_sample)
```

```python
import os
import numpy as np
import concourse.bass as bass
import concourse.mybir as mybir
from concourse.bass_utils import run_bass_kernel_spmd

F32 = mybir.dt.float32
BF16 = mybir.dt.bfloat16
I32 = mybir.dt.int32
AF = mybir.ActivationFunctionType
ALU = mybir.AluOpType
AX = mybir.AxisListType

D = 1024
SEQ = 2048
NB = 8
DB = 128
DS = 8
BPC = DB // NB
NSAMP = BPC * DS
DFF = 2816
NJ = DFF // 128
NPAGES = 16
PAGE = 128
NPHYS = 2560
NMEM = 256
EPS = 1e-6
TILES = [(0, 512), (512, 512), (1024, 512), (1536, 512), (2048, 128)]
NT = len(TILES)

V_GMIX, V_GFFN, V_GMEM, V_GKV, V_GMQ, V_PSC, V_GFQ, V_GMK, V_GFK, V_BFG, NV = 0, 16, 32, 48, 56, 58, 62, 64, 320, 384, 392
C_ID, C_ONES, C_BD, C_UT, C_E127, C_GE, C_BT, C_INVC, C_MASKS, NCST = 0, 128, 256, 384, 512, 640, 768, 896, 960, 1088


class Buf:
    __slots__ = ("name", "space", "lo", "hi", "w", "r", "ov", "dma")

    def __init__(self, name, space, lo, hi):
        self.name, self.space, self.lo, self.hi = name, space, lo, hi
        self.w = None
        self.r = {}
        self.ov = None
        self.dma = False


class A:
    __slots__ = ("ap", "bufs")

    def __init__(self, ap, bufs):
        self.ap, self.bufs = ap, bufs


class TT:
    def __init__(self, h, buf):
        self.h, self.buf = h, buf

    def __getitem__(self, idx):
        return A(self.h[idx], [self.buf])

    def v(self, fn):
        return A(fn(self.h), [self.buf])


class Op:
    __slots__ = ("eng", "fn", "key", "pos", "signal", "waits", "is_dma", "val")

    def __init__(self, eng, fn, key, is_dma):
        self.eng, self.fn, self.key, self.is_dma = eng, fn, key, is_dma
        self.pos = 0
        self.signal = is_dma
        self.waits = {}
        self.val = 0


class K:
    ENGS = ("pe", "act", "dve", "pool", "sp")

    def __init__(self, nc):
        self.nc = nc
        self.q = {e: [] for e in self.ENGS}
        self.bufs = []
        self.chan_n = {}
        self.out_chans = set()
        self.sb_lo = (nc.sbuf_base + 63) // 64 * 64
        self.sb_hi = nc.sbuf_top
        self.sb_ptr = self.sb_lo
        self.banks = []
        self.free_banks = []

    def sb(self, name, shape, dt, off=None):
        nbytes = int(np.prod(shape[1:])) * mybir.dt.size(dt)
        if off is None:
            off = self.sb_ptr
            self.sb_ptr = (off + nbytes + 31) // 32 * 32
        assert off + nbytes <= self.sb_hi, (name, off, nbytes, self.sb_hi)
        h = self.nc.alloc_sbuf_tensor_at(name, list(shape), dt, offset=off)
        b = Buf(name, "sb", off, off + nbytes)
        self.bufs.append(b)
        for x in self.bufs:
            x.ov = None
        return TT(h, b)

    def make_banks(self):
        for i in range(8):
            h = self.nc.alloc_psum_tensor("bank%d" % i, [128, 512], F32)
            b = Buf("bank%d" % i, "ps", i, i + 1)
            self.bufs.append(b)
            self.banks.append(TT(h, b))
        self.free_banks = list(self.banks)

    def bank(self):
        assert self.free_banks, "out of PSUM banks"
        return self.free_banks.pop(0)

    def release(self, bk):
        self.free_banks.append(bk)

    def _ov(self, b):
        if b.ov is None:
            b.ov = [x for x in self.bufs if x.space == b.space and x.lo < b.hi and b.lo < x.hi]
        return b.ov

    @staticmethod
    def _dep(op, prod):
        if prod is None or prod is op:
            return
        if prod.eng == "pe" and op.eng == "pe" and not prod.is_dma:
            return
        cur = op.waits.get(prod.key)
        if cur is None or cur.pos < prod.pos:
            op.waits[prod.key] = prod

    def _add(self, eng, fn, R, W, is_dma=False, chan=None):
        if is_dma:
            key = chan
            op = Op(eng, fn, key, True)
            n = self.chan_n.get(key, 0) + 1
            self.chan_n[key] = n
            op.pos = n
        else:
            op = Op(eng, fn, eng, False)
            op.pos = len(self.q[eng]) + 1
        for b in R:
            for x in self._ov(b):
                self._dep(op, x.w)
                if b.space == "ps":
                    for rr in x.r.values():
                        if rr.eng != eng:
                            self._dep(op, rr)
        for b in W:
            for x in self._ov(b):
                self._dep(op, x.w)
                for rr in x.r.values():
                    self._dep(op, rr)
        rkey = op.key if is_dma else eng
        for b in R:
            b.r[rkey] = op
        for b in W:
            b.w = op
            b.r = {}
        self.q[eng].append(op)
        return op

    def op(self, eng, fn, R=(), W=()):
        rb = [b for a in R for b in a.bufs]
        wb = [b for a in W for b in a.bufs]
        return self._add(eng, fn, rb, wb)

    def mm(self, out, lhsT, rhs, start=True, stop=True, sgc=False):
        return self.op("pe", lambda e: e.matmul(out.ap, lhsT=lhsT.ap, rhs=rhs.ap, start=start, stop=stop,
                                                skip_group_check=sgc),
                       R=[lhsT, rhs], W=[out])

    def tr(self, out, in_, ident):
        return self.op("pe", lambda e: e.transpose(out.ap, in_.ap, ident.ap), R=[in_, ident], W=[out])

    def act(self, out, in_, func, scale=1.0, bias=None, eng="act"):
        R = [in_]
        kw = {}
        if isinstance(scale, A):
            R.append(scale)
            kw["scale"] = scale.ap
        else:
            kw["scale"] = float(scale)
        if isinstance(bias, A):
            R.append(bias)
            kw["bias"] = bias.ap
        elif bias is not None:
            kw["bias"] = float(bias)
        return self.op("act", lambda e: e.activation(out=out.ap, in_=in_.ap, func=func, **kw), R=R, W=[out])

    def tt(self, eng, out, in0, in1, op):
        return self.op(eng, lambda e: e.tensor_tensor(out=out.ap, in0=in0.ap, in1=in1.ap, op=op),
                       R=[in0, in1], W=[out])

    def ts(self, eng, out, in0, s1, op0, s2=None, op1=None):
        R = [in0]
        a1 = s1
        if isinstance(s1, A):
            R.append(s1)
            a1 = s1.ap
        a2 = s2
        if isinstance(s2, A):
            R.append(s2)
            a2 = s2.ap
        if op1 is None:
            return self.op(eng, lambda e: e.tensor_scalar(out=out.ap, in0=in0.ap, scalar1=a1, scalar2=None, op0=op0),
                           R=R, W=[out])
        return self.op(eng, lambda e: e.tensor_scalar(out=out.ap, in0=in0.ap, scalar1=a1, scalar2=a2, op0=op0, op1=op1),
                       R=R, W=[out])

    def stt(self, out, in0, scalar, in1, op0, op1):
        R = [in0, in1]
        sc = scalar
        if isinstance(scalar, A):
            R.append(scalar)
            sc = scalar.ap
        return self.op("dve", lambda e: e.scalar_tensor_tensor(out=out.ap, in0=in0.ap, scalar=sc, in1=in1.ap,
                                                               op0=op0, op1=op1), R=R, W=[out])

    def copy(self, eng, out, in_):
        if eng == "act":
            return self.op("act", lambda e: e.copy(out=out.ap, in_=in_.ap), R=[in_], W=[out])
        return self.op(eng, lambda e: e.tensor_copy(out=out.ap, in_=in_.ap), R=[in_], W=[out])

    def recip(self, out, in_):
        return self.op("dve", lambda e: e.reciprocal(out=out.ap, in_=in_.ap), R=[in_], W=[out])

    def red(self, out, in_, op=ALU.add, axis=AX.X):
        return self.op("dve", lambda e: e.tensor_reduce(out=out.ap, in_=in_.ap, op=op, axis=axis), R=[in_], W=[out])

    def memset(self, eng, out, val):
        return self.op(eng, lambda e: e.memset(out.ap, val), R=[], W=[out])

    def dma(self, q, out, in_, is_out=False, **kw):
        R, W = [], []
        o_ap, i_ap = out, in_
        chan = None
        if isinstance(out, A):
            W = list(out.bufs)
            o_ap = out.ap
            chan = out.bufs[0].name
        if isinstance(in_, A):
            R = list(in_.bufs)
            i_ap = in_.ap
            if chan is None:
                chan = in_.bufs[0].name
        if is_out:
            self.out_chans.add(chan)
        return self._add(q, lambda e: e.dma_start(out=o_ap, in_=i_ap, **kw), R, W, is_dma=True, chan=chan)

    def gather(self, out, in_ap, idx):
        chan = out.bufs[0].name
        return self._add("pool", lambda e: e.indirect_dma_start(
            out=out.ap, out_offset=None, in_=in_ap,
            in_offset=bass.IndirectOffsetOnAxis(ap=idx.ap, axis=0)),
            list(idx.bufs), list(out.bufs), is_dma=True, chan=chan)

    def emit(self):
        nc = self.nc
        for e in self.ENGS:
            for op in self.q[e]:
                for prod in op.waits.values():
                    prod.signal = True
        sems = {}
        for e in ("pe", "act", "dve", "pool"):
            sems[e] = nc.alloc_semaphore("s_" + e)
        for ch in self.chan_n:
            sems[ch] = nc.alloc_semaphore("d_" + ch)
        for e in self.ENGS:
            n = 0
            for op in self.q[e]:
                if op.is_dma:
                    op.val = 16 * op.pos
                elif op.signal:
                    n += 1
                    op.val = n
        engobj = {"pe": "tensor", "act": "scalar", "dve": "vector", "pool": "gpsimd", "sp": "sync"}
        out_chans = sorted(self.out_chans)
        chan_n = self.chan_n
        q = self.q
        stats = {}
        with nc.Block() as block:
            def body(ename):
                def f(e):
                    known = {}
                    nw = 0
                    for op in q[ename]:
                        for key, prod in op.waits.items():
                            v = prod.val
                            if known.get(key, 0) >= v:
                                continue
                            e.wait_ge(sems[key], v)
                            known[key] = v
                            nw += 1
                        inst = op.fn(e)
                        if op.signal:
                            inst.then_inc(sems[op.key], 16 if op.is_dma else 1)
                    if ename == "sp":
                        for ch in out_chans:
                            e.wait_ge(sems[ch], 16 * chan_n[ch])
                    stats[ename] = (len(q[ename]), nw)
                return f
            block.tensor(body("pe"))
            block.scalar(body("act"))
            block.vector(body("dve"))
            block.gpsimd(body("pool"))
            block.sync(body("sp"))
        if os.environ.get("KSTATS"):
            print("KSTATS", stats, "nsem", len(sems))


def build_program(stage=99, nphys=NPHYS):
    nc = bass.Bass("TRN2", target_bir_lowering=False)
    k = K(nc)

    def din(name, shape, dt=F32):
        return nc.dram_tensor(name, list(shape), dt, kind="ExternalInput").ap()

    def dout(name, shape, dt=F32):
        return nc.dram_tensor(name, list(shape), dt, kind="ExternalOutput").ap()

    xp = din("xp", [SEQ, D])
    xs = din("xs", [NSAMP, D])
    memp = din("memp", [NMEM, D])
    cmk = din("cmk", [2, BPC, NMEM, 512])
    cmv = din("cmv", [2, BPC, NMEM, 512])
    spool = din("spool", [BPC * 15, 512])
    ck = din("ck", [nphys * 32, 2048])
    cv = din("cv", [nphys * 32, 2048])
    clf = din("clf", [nphys * 8, 16 * 8])
    ptab = din("ptab", [128, BPC], I32)
    w_in = din("w_in", [2, D, D])
    w_out = din("w_out", [2, D, D])
    w_gu = din("w_gu", [2, D, 2 * DFF])
    w_down = din("w_down", [2, DFF, D])
    w_mkv = din("w_mkv", [2, D, D])
    w_pool = din("w_pool", [4, 128, 128])
    w_kv = din("w_kv", [D, D])
    w_fg = din("w_fg", [D, 8])
    vecs_d = din("vecs", [128, NV])
    cst_d = din("cst", [128, NCST])
    cint_d = din("cint", [128, 1], I32)

    y_p = dout("y_p", [SEQ, D])
    y_s = dout("y_s", [NSAMP, D])
    krp = dout("krp", [SEQ, 512])
    vrp = dout("vrp", [SEQ, 512])
    lfp = dout("lfp", [SEQ, 8])
    mkp = dout("mkp", [2, NMEM, 512])
    mvp = dout("mvp", [2, NMEM, 512])
    psp = dout("psp", [15, 512])
    krs = dout("krs", [NSAMP, 512])
    vrs = dout("vrs", [NSAMP, 512])
    lfs = dout("lfs", [NSAMP, 8])
    pss = dout("pss", [BPC * 15, 512])

    k.make_banks()

    cst = k.sb("cst", [128, NCST], F32)
    vecs = k.sb("vecs", [128, NV], F32)
    cint = k.sb("cint", [128, 1], I32)
    cb = k.sb("cb", [128, 4 * 128], BF16)
    masks_b = k.sb("masks_b", [128, 128], BF16)
    hT = [k.sb("hT%d" % i, [128, 8, n], F32) for i, (_, n) in enumerate(TILES)]
    xn = [k.sb("xn%d" % i, [128, 8, n], BF16) for i, (_, n) in enumerate(TILES)]
    mkT = [k.sb("mkT%d" % l, [128, 4, NMEM], BF16) for l in range(2)]
    mvb = [k.sb("mvb%d" % l, [128, 2, 512], BF16) for l in range(2)]
    rstds = [k.sb("rstd%d" % i, [128, 512], F32) for i in range(2)]
    lnbs = [k.sb("lnb%d" % i, [128, 512], F32) for i in range(2)]
    rs_i = [0]

    def rs_pair():
        rs_i[0] += 1
        return rstds[rs_i[0] % 2], lnbs[rs_i[0] % 2]
    sq = [k.sb("sq%d" % i, [128, 512], BF16) for i in range(2)]
    stg = [k.sb("stg%d" % i, [128, 1024], F32) for i in range(3)]
    small = k.sb("small", [128, 256], F32)
    KTs = k.sb("KTs", [128, 4, NSAMP], BF16)
    Vs = k.sb("Vs", [128, 512], BF16)
    lf_all = k.sb("lf_all", [128, 16, 8], F32)
    fcum = k.sb("fcum", [128, 16, 8], F32)
    fend = k.sb("fend", [128, 4, 8], F32)
    lfsum = k.sb("lfsum", [128, 8], F32)
    lf_s = k.sb("lf_s", [128, 8], F32)
    nfnew = k.sb("nfnew", [128, 8], F32)
    smalls = [k.sb("sm%d" % i, [128, 64], F32) for i in range(4)]
    R0 = k.sb_ptr
    RSIZE = k.sb_hi - R0

    ident = cst[:, C_ID:C_ID + 128]
    ones_f = cst[:, C_ONES:C_ONES + 128]
    ident_b = cb[:, 0:128]
    ones_b = cb[:, 128:256]
    bd_b = cb[:, 256:384]
    ut_b = cb[:, 384:512]

    stg_i = [0]

    def staging():
        s = stg[stg_i[0] % 3]
        stg_i[0] += 1
        return s

    sq_i = [0]

    def sqbuf():
        s = sq[sq_i[0] % 2]
        sq_i[0] += 1
        return s

    ev_i = [0]

    def ev_eng():
        ev_i[0] += 1
        return "act" if ev_i[0] % 2 else "dve"

    def evac(out, in_, eng=None):
        eng = eng or ev_eng()
        k.copy(eng, out, in_)

    def vcol(c):
        return vecs[:, c:c + 1]

    k.dma("sp", cst[:, :], cst_d[:, :])
    k.dma("sp", vecs[:, :], vecs_d[:, :])
    k.dma("sp", cint[:, :], cint_d[:, :])
    k.copy("dve", cb[:, 0:512], cst[:, 0:512])
    k.copy("dve", masks_b[:, :], cst[:, C_MASKS:C_MASKS + 128])

    def load_tok_major_T(src_rows, nrows, dst_fn):
        s = staging()
        k.dma("sp", s[0:nrows, :], src_rows)
        for half in range(2):
            bk = k.bank()
            for j in range(4):
                kk = half * 4 + j
                k.tr(bk[:, j * 128:j * 128 + nrows], s[0:nrows, kk * 128:(kk + 1) * 128], A(ident.ap[0:nrows, 0:nrows], ident.bufs))
            evac(dst_fn(half * 4), bk.v(lambda h: h[:, :].rearrange("p (j t) -> p j t", j=4)[:, :, 0:nrows]))
            k.release(bk)

    def rms_rstd(src_chunks, nch, n, dim, lhs_ones):
        rstd, lnb = rs_pair()
        bk = k.bank()
        for i in range(nch):
            s = sqbuf()
            k.act(s[:, 0:n], src_chunks(i), AF.Square)
            k.mm(bk[:, 0:n], lhs_ones, s[:, 0:n], start=(i == 0), stop=(i == nch - 1))
        k.act(lnb[:, 0:n], bk[:, 0:n], AF.Ln, scale=1.0 / dim, bias=EPS)
        k.act(rstd[:, 0:n], lnb[:, 0:n], AF.Exp, scale=-0.5)
        k.release(bk)
        return rstd

    def norm_tile(i, gcol, dst):
        n = TILES[i][1]
        rstd = rms_rstd(lambda kk: hT[i][:, kk, :], 8, n, float(D), ones_b)
        for kk in range(8):
            k.stt(dst[:, kk, 0:n], hT[i][:, kk, :], vcol(gcol + kk), rstd[:, 0:n], ALU.mult, ALU.mult)

    for i, (c0, n) in enumerate(TILES):
        for tcn in range(n // 128):
            if i < 4:
                src = xp[c0 + tcn * 128:c0 + (tcn + 1) * 128, :]
            else:
                src = xs[:, :]
            load_tok_major_T(src, 128, lambda k0, i=i, tcn=tcn: hT[i][:, k0:k0 + 4, tcn * 128:(tcn + 1) * 128])

    class Arena:
        def __init__(self, base=None, limit=None):
            self.p = R0 if base is None else base
            self.limit = k.sb_hi if limit is None else limit

        def alloc(self, name, shape, dt):
            nbytes = int(np.prod(shape[1:])) * mybir.dt.size(dt)
            assert self.p + nbytes <= self.limit, (name, self.p, nbytes, self.limit)
            t = k.sb(name, shape, dt, off=self.p)
            self.p = (self.p + nbytes + 31) // 32 * 32
            return t

    uid = [0]

    def uname(s):
        uid[0] += 1
        return "%s_%d" % (s, uid[0])

    def load_wsq(dst, w_ap):
        for half in range(2):
            k.dma("pool", dst[:, half * 4:(half + 1) * 4, :],
                  w_ap[half * 512:(half + 1) * 512, :].rearrange("(k p) n -> p k n", p=128))

    ar = Arena()
    wsq = [ar.alloc(uname("wsq"), [128, 8, D], BF16) for _ in range(2)]
    mT = ar.alloc(uname("mT"), [128, 8, NMEM], F32)
    mn = ar.alloc(uname("mn"), [128, 8, NMEM], BF16)
    ktok = ar.alloc(uname("ktok"), [128, 512], F32)
    ktmp = ar.alloc(uname("ktmp"), [128, 512], F32)
    kb16 = ar.alloc(uname("kb16"), [128, 512], BF16)

    for mc in range(2):
        load_tok_major_T(memp[mc * 128:(mc + 1) * 128, :], 128,
                         lambda k0, mc=mc: mT[:, k0:k0 + 4, mc * 128:(mc + 1) * 128])
    rstd = rms_rstd(lambda kk: mT[:, kk, :], 8, NMEM, float(D), ones_b)
    for l in range(2):
        load_wsq(wsq[l], w_mkv[l])
    for l in range(2):
        for kk in range(8):
            k.stt(mn[:, kk, :], mT[:, kk, :], vcol(V_GMEM + l * 8 + kk), rstd[:, 0:NMEM], ALU.mult, ALU.mult)
        for mc in range(2):
            bkK, bkV = k.bank(), k.bank()
            for kk in range(8):
                k.mm(bkK[:, :], mn[:, kk, mc * 128:(mc + 1) * 128], wsq[l][:, kk, 0:512], start=(kk == 0), stop=(kk == 7))
            for kk in range(8):
                k.mm(bkV[:, :], mn[:, kk, mc * 128:(mc + 1) * 128], wsq[l][:, kk, 512:1024], start=(kk == 0), stop=(kk == 7))
            sv = staging()
            k.copy("act", sv[:, 0:512], bkV[:, :])
            k.release(bkV)
            k.dma("sp", mvp[l, mc * 128:(mc + 1) * 128, :], sv[:, 0:512], is_out=True)
            k.copy("dve", mvb[l][:, mc, :], sv[:, 0:512])
            k.copy("act", ktok[:, :], bkK[:, :])
            k.release(bkK)
            k.tt("dve", ktmp[:, :], ktok[:, :], ktok[:, :], ALU.mult)
            k.red(small[:, 0:4], ktmp.v(lambda h: h[:, :].rearrange("p (h d) -> p h d", h=4)))
            k.act(small[:, 4:8], small[:, 0:4], AF.Ln, scale=1.0 / 128, bias=EPS)
            k.act(small[:, 8:12], small[:, 4:8], AF.Exp, scale=-0.5)
            k.tt("dve", ktmp.v(lambda h: h[:, :].rearrange("p (h d) -> p h d", h=4)),
                 ktok.v(lambda h: h[:, :].rearrange("p (h d) -> p h d", h=4)),
                 small.v(lambda h: h[:, 8:12].unsqueeze(2).to_broadcast([128, 4, 128])), ALU.mult)
            sk = staging()
            k.tt("dve", sk.v(lambda h: h[:, 0:512].rearrange("p (h d) -> p h d", h=4)),
                 ktmp.v(lambda h: h[:, :].rearrange("p (h d) -> p h d", h=4)),
                 vecs.v(lambda h: h[:, V_GMK + l * 128:V_GMK + (l + 1) * 128].unsqueeze(1).to_broadcast([128, 4, 128])), ALU.mult)
            k.dma("sp", mkp[l, mc * 128:(mc + 1) * 128, :], sk[:, 0:512], is_out=True)
            k.copy("act", kb16[:, :], sk[:, 0:512])
            bt = k.bank()
            btb = bt.v(lambda h: h[:, :].bitcast(BF16))
            for hh in range(4):
                k.tr(A(btb.ap[:, hh * 128:(hh + 1) * 128], btb.bufs), kb16[:, hh * 128:(hh + 1) * 128], ident_b)
            evac(mkT[l][:, :, mc * 128:(mc + 1) * 128],
                 A(btb.ap[:, 0:512].rearrange("p (h m) -> p h m", h=4), btb.bufs))
            k.release(bt)

    if stage <= 1:
        k.emit()
        return nc

    rsq128 = 1.0 / float(np.sqrt(128.0))

    def head_norm(bk, n, zm, dim, lhs_ones, gcol, dst):
        k.copy("act", zm[:, 0:n], bk[:, 0:n])
        k.release(bk)
        rstd = rms_rstd(lambda _: zm[:, 0:n], 1, n, float(dim), lhs_ones)
        k.stt(dst, zm[:, 0:n], vcol(gcol), rstd[:, 0:n], ALU.mult, ALU.mult)

    def mem_attn_prompt(l, n, qm, Pm, lnd, rden, cat):
        def stage_a(h):
            P = Pm[h % 2]
            for mc in range(2):
                bkS = k.bank()
                k.mm(bkS[:, 0:n], mkT[l][:, h, mc * 128:(mc + 1) * 128], qm[:, h, 0:n])
                k.act(P[:, mc, 0:n], bkS[:, 0:n], AF.Exp, scale=rsq128)
                k.release(bkS)

        def stage_b(h):
            P = Pm[h % 2]
            rden_, lnd_ = rs_pair()
            bkN, bkD = k.bank(), k.bank()
            for mc in range(2):
                k.mm(bkN[:, 0:n], mvb[l][:, mc, h * 128:(h + 1) * 128], P[:, mc, 0:n], start=(mc == 0), stop=(mc == 1))
            for mc in range(2):
                k.mm(bkD[:, 0:n], ones_b, P[:, mc, 0:n], start=(mc == 0), stop=(mc == 1))
            k.act(lnd_[:, 0:n], bkD[:, 0:n], AF.Ln)
            k.release(bkD)
            k.act(rden_[:, 0:n], lnd_[:, 0:n], AF.Exp, scale=-1.0)
            k.tt("dve", cat[:, 4 + h, 0:n], bkN[:, 0:n], rden_[:, 0:n], ALU.mult)
            k.release(bkN)

        stage_a(0)
        for h in range(4):
            if h + 1 < 4:
                stage_a(h + 1)
            stage_b(h)

    def mem_attn_sample(l, qm, cat, mks2, mksT, mvs2, Pms):
        def loads(b):
            k.dma("pool", mks2[b % 2][:, :, :], cmk[l, b].rearrange("(c p) f -> p c f", p=128))
            k.dma("pool", mvs2[b % 2][:, :, :], cmv[l, b].rearrange("(c p) f -> p c f", p=128))

        loads(0)
        for b in range(BPC):
            if b + 1 < BPC:
                loads(b + 1)
            mks, mvs = mks2[b % 2], mvs2[b % 2]
            bt = k.bank()
            btb = bt.v(lambda h: h[:, :].bitcast(BF16))
            for h in range(4):
                for mc in range(2):
                    o = h * 256 + mc * 128
                    k.tr(A(btb.ap[:, o:o + 128], btb.bufs), mks[:, mc, h * 128:(h + 1) * 128], ident_b)
            evac(mksT[:, :, :], A(btb.ap[:, 0:1024].rearrange("p (h m) -> p h m", h=4), btb.bufs))
            k.release(bt)
            bkS = k.bank()
            for mc in range(2):
                for h in range(4):
                    o = (mc * 4 + h) * 8
                    k.mm(bkS[:, o:o + 8], mksT[:, h, mc * 128:(mc + 1) * 128], qm[:, h, b * 8:(b + 1) * 8])
            k.act(Pms[:, 0:64], bkS[:, 0:64], AF.Exp, scale=rsq128)
            k.release(bkS)
            bkN = k.bank()
            first = True
            for h in range(4):
                for mc in range(2):
                    o = (mc * 4 + h) * 8
                    k.mm(bkN[:, h * 8:(h + 1) * 8], mvs[:, mc, h * 128:(h + 1) * 128], Pms[:, o:o + 8],
                         start=first, stop=(mc == 1), sgc=True)
                    first = False
            for mc in range(2):
                k.mm(bkN[:, 32:64], ones_b, Pms[:, mc * 32:(mc + 1) * 32], start=False, stop=(mc == 1), sgc=True)
            k.act(small[:, 64:96], bkN[:, 32:64], AF.Ln)
            k.act(small[:, 96:128], small[:, 64:96], AF.Exp, scale=-1.0)
            k.tt("dve", cat[:, 4:8, b * 8:(b + 1) * 8],
                 bkN.v(lambda h: h[:, 0:32].rearrange("p (h t) -> p h t", h=4)),
                 small.v(lambda h: h[:, 96:128].rearrange("p (h t) -> p h t", h=4)), ALU.mult)
            k.release(bkN)

    def out_proj(i, wo, cat):
        n = TILES[i][1]
        for c in range(8):
            bk = k.bank()
            for kk in range(8):
                k.mm(bk[:, 0:n], wo[:, kk, c * 128:(c + 1) * 128], cat[:, kk, 0:n], start=(kk == 0), stop=(kk == 7))
            k.tt("dve", hT[i][:, c, :], bk[:, 0:n], hT[i][:, c, :], ALU.add)
            k.release(bk)

    def mixer0():
        l = 0
        ar = Arena()
        wsq = [ar.alloc(uname("wsq"), [128, 8, D], BF16) for _ in range(2)]
        wpl = ar.alloc(uname("wpl"), [128, 4, 128], BF16)
        zte = ar.alloc(uname("zte"), [128, 4, 528], F32)
        sA = ar.alloc(uname("sA"), [128, 528], F32)
        sB = ar.alloc(uname("sB"), [128, 528], F32)
        yT = ar.alloc(uname("yT"), [128, 4, 512], BF16)
        zm = [ar.alloc(uname("zm"), [128, 512], F32) for _ in range(2)]
        qm = ar.alloc(uname("qm"), [128, 4, 512], BF16)
        Pm = [ar.alloc(uname("Pm"), [128, 2, 512], BF16) for _ in range(2)]
        lnd = rden = None
        ytmp = ar.alloc(uname("ytmp"), [128, 16], F32)
        a0 = Arena(xn[0].buf.lo, xn[0].buf.hi)
        zes = a0.alloc(uname("zes"), [128, 4, BPC, 24], F32)
        mks = a0.alloc(uname("mks"), [128, 2, 512], BF16)
        a1 = Arena(xn[1].buf.lo, xn[1].buf.hi)
        mksT = a1.alloc(uname("mksT"), [128, 4, NMEM], BF16)
        mvs = a1.alloc(uname("mvs"), [128, 2, 512], BF16)
        zst = a1.alloc(uname("zst"), [128, 4, 240], F32)
        Pms = a1.alloc(uname("Pms"), [128, 64], BF16)
        xr = [xn[2], xn[3]]
        mks2 = [mks, ar.alloc(uname("mksb"), [128, 2, 512], BF16)]
        mvs2 = [mvs, Arena(xn[4].buf.lo, xn[4].buf.hi).alloc(uname("mvsb"), [128, 2, 512], BF16)]

        load_wsq(wsq[0], w_in[l])
        load_wsq(wsq[1], w_out[l])
        k.dma("pool", wpl[:, :, :], w_pool.rearrange("g c d -> c g d"))
        k.memset("dve", zte[:, :, :], 0.0)
        k.memset("dve", zes[:, :, :, :], 0.0)
        s = staging()
        k.dma("sp", s[0:120, 0:512], spool[0:120, :])
        k.dma("sp", s[0:120, 512:1024], spool[120:240, :])
        for hb in range(2):
            bk = k.bank()
            for g in range(4):
                k.tr(bk[:, g * 128:g * 128 + 120], s[0:120, hb * 512 + g * 128:hb * 512 + (g + 1) * 128],
                     A(ident.ap[0:120, 0:120], ident.bufs))
            for g in range(4):
                evac(zes[:, g, hb * 8:(hb + 1) * 8, 1:16],
                     bk.v(lambda h, g=g: h[:, g * 128:g * 128 + 120].rearrange("p (b j) -> p b j", b=8)))
            k.release(bk)

        ksub = int(os.environ.get('KSUB', '99'))
        if ksub <= 1:
            return
        for ti, i in enumerate((4, 0, 1, 2, 3)):
            n = TILES[i][1]
            samp = (i == 4)
            X = xr[ti % 2]
            norm_tile(i, V_GMIX + l * 8, X)
            def fin0(bk, c):
                if c < 4:
                    if samp:
                        evac(zes[:, c, :, 16:24], bk.v(lambda h: h[:, 0:128].rearrange("p (b t) -> p b t", b=BPC)))
                    else:
                        evac(zte[:, c, 16:528], bk[:, 0:512])
                    k.release(bk)
                else:
                    head_norm(bk, n, zm[c % 2], 128, ones_b, V_GMQ + l, qm[:, c - 4, 0:n])

            pend = None
            for c in range(8):
                bk = k.bank()
                for kk in range(8):
                    k.mm(bk[:, 0:n], wsq[0][:, kk, c * 128:(c + 1) * 128], X[:, kk, 0:n], start=(kk == 0), stop=(kk == 7))
                if pend is not None:
                    fin0(*pend)
                pend = (bk, c)
            fin0(*pend)
            if ksub <= 2:
                return
            cat = X
            for g in range(4):
                w = 2 << g
                eng = "pool" if g < 3 else "dve"
                if samp:
                    zz = lambda a, b_, g=g: zes[:, g, :, a:b_]
                    SA = lambda a, b_: sA.v(lambda h: h[:, 0:384].rearrange("p (b c) -> p b c", b=BPC)[:, :, a:b_])
                    SB = lambda a, b_: sB.v(lambda h: h[:, 0:384].rearrange("p (b c) -> p b c", b=BPC)[:, :, a:b_])
                    W_ = 24
                else:
                    zz = lambda a, b_, g=g: zte[:, g, a:b_]
                    SA = lambda a, b_: sA[:, a:b_]
                    SB = lambda a, b_: sB[:, a:b_]
                    W_ = 528
                k.tt(eng, SA(1, W_), zz(1, W_), zz(0, W_ - 1), ALU.add)
                S = SA
                if g >= 1:
                    k.tt(eng, SB(3, W_), SA(3, W_), SA(1, W_ - 2), ALU.add)
                    S = SB
                if g >= 2:
                    k.tt(eng, SA(7, W_), SB(7, W_), SB(3, W_ - 4), ALU.add)
                    S = SA
                if g >= 3:
                    k.tt(eng, SB(15, W_), SA(15, W_), SA(7, W_ - 8), ALU.add)
                    S = SB
                if samp:
                    k.stt(yT.v(lambda h, g=g: h[:, g, 0:128].rearrange("p (b t) -> p b t", b=BPC)),
                          S(16, 24), 1.0 / w, zz(16, 24), ALU.mult, ALU.subtract)
                else:
                    k.stt(yT[:, g, 0:512], S(16, 528), 1.0 / w, zz(16, 528), ALU.mult, ALU.subtract)
                    if i == 0:
                        k.tt("dve", ytmp[:, :], S(16, 32), cst[:, C_INVC + g * 16:C_INVC + (g + 1) * 16], ALU.mult)
                        k.tt("dve", yT[:, g, 0:16], ytmp[:, :], zz(16, 32), ALU.subtract)
                bk = k.bank()
                k.mm(bk[:, 0:n], wpl[:, g, :], yT[:, g, 0:n])
                k.act(cat[:, g, 0:n], bk[:, 0:n], AF.Identity, scale=vcol(V_PSC + g))
                k.release(bk)
            if ksub <= 3:
                return
            if samp:
                for g in range(4):
                    k.copy("dve", zst.v(lambda h, g=g: h[:, g, :].rearrange("p (b j) -> p b j", b=BPC)), zes[:, g, :, 9:24])
                for hb in range(2):
                    bk = k.bank()
                    for g in range(4):
                        k.tr(bk[0:120, g * 128:(g + 1) * 128], zst[:, g, hb * 120:(hb + 1) * 120], ident)
                    so = staging()
                    k.copy("act", so[0:120, 0:512], bk[0:120, :])
                    k.release(bk)
                    k.dma("sp", pss[hb * 120:(hb + 1) * 120, :], so[0:120, 0:512], is_out=True)
            else:
                if i == 3:
                    bk = k.bank()
                    for g in range(4):
                        k.copy("dve", zst[:, g, 0:15], zte[:, g, 513:528])
                        k.tr(bk[:, g * 128:(g + 1) * 128], zst[:, g, 0:128], ident)
                    so = staging()
                    k.copy("act", so[0:15, 0:512], bk[0:15, :])
                    k.release(bk)
                    k.dma("sp", psp[:, :], so[0:15, 0:512], is_out=True)
                else:
                    k.copy("dve", ytmp.v(lambda h: h[:, 0:16]), zte[:, 0, 512:528])
                    for g in range(4):
                        k.copy("dve", sA[:, 0:16], zte[:, g, 512:528])
                        k.copy("dve", zte[:, g, 0:16], sA[:, 0:16])
            if ksub <= 4:
                return
            if samp:
                mem_attn_sample(l, qm, cat, mks2, mksT, mvs2, Pms)
            else:
                mem_attn_prompt(l, n, qm, Pm, lnd, rden, cat)
            if ksub <= 5:
                return
            out_proj(i, wsq[1], cat)
            if ksub <= 6:
                return

    mixer0()
    if stage <= 2:
        k.emit()
        return nc

    sm_i = [0]

    def smbuf():
        sm_i[0] += 1
        return smalls[sm_i[0] % 4]

    def ffn(l):
        ar = Arena()
        aT = [ar.alloc(uname("aT"), [128, 4, n], BF16) for (_, n) in TILES]
        wgu = [ar.alloc(uname("wgu"), [128, 8, 2, 256], BF16) for _ in range(3)]
        wdn = [ar.alloc(uname("wdn"), [128, 4, D], BF16) for _ in range(2)]
        sg = [ar.alloc(uname("sg"), [128, 512], F32) for _ in range(2)]
        for i in range(NT):
            norm_tile(i, V_GFFN + l * 8, xn[i])
        groups = [(0, 4), (4, 4), (8, 4), (12, 4), (16, 4), (20, 2)]
        blk = 0
        cnt = 0
        for gi, (j0, nj) in enumerate(groups):
            wd = wdn[gi % 2]
            k.dma("pool", wd[:, 0:nj, :], w_down[l][j0 * 128:(j0 + nj) * 128, :].rearrange("(j p) n -> p j n", p=128))
            for jb in range(nj // 2):
                j = j0 + 2 * jb
                ws = wgu[blk % 3]
                blk += 1
                k.dma("pool", ws[:, :, 0, :], w_gu[l][:, j * 128:j * 128 + 256].rearrange("(k p) n -> p k n", p=128))
                k.dma("pool", ws[:, :, 1, :], w_gu[l][:, DFF + j * 128:DFF + j * 128 + 256].rearrange("(k p) n -> p k n", p=128))
                for i in range(NT):
                    n = TILES[i][1]
                    for jj in range(2):
                        bkG = k.bank()
                        for kk in range(8):
                            k.mm(bkG[:, 0:n], ws[:, kk, 0, jj * 128:(jj + 1) * 128], xn[i][:, kk, :], start=(kk == 0), stop=(kk == 7))
                        bkU = k.bank()
                        for kk in range(8):
                            k.mm(bkU[:, 0:n], ws[:, kk, 1, jj * 128:(jj + 1) * 128], xn[i][:, kk, :], start=(kk == 0), stop=(kk == 7))
                        s_ = sg[cnt % 2]
                        cnt += 1
                        k.act(s_[:, 0:n], bkG[:, 0:n], AF.Silu)
                        k.release(bkG)
                        k.tt("dve", aT[i][:, 2 * jb + jj, :], bkU[:, 0:n], s_[:, 0:n], ALU.mult)
                        k.release(bkU)
            for i in range(NT):
                n = TILES[i][1]
                for c in range(8):
                    bk = k.bank()
                    for jj in range(nj):
                        k.mm(bk[:, 0:n], wd[:, jj, c * 128:(c + 1) * 128], aT[i][:, jj, :], start=(jj == 0), stop=(jj == nj - 1))
                    k.tt("dve", hT[i][:, c, :], bk[:, 0:n], hT[i][:, c, :], ALU.add)
                    k.release(bk)

    ffn(0)

    axn = Arena(xn[0].buf.lo, xn[4].buf.hi)
    KTp = [axn.alloc(uname("KTp"), [128, 4, 512], BF16) for _ in range(4)]
    Vp = [axn.alloc(uname("Vp"), [128, 4, 8, 65], BF16) for _ in range(4)]

    def shared_kv():
        ar = Arena()
        wkv = ar.alloc(uname("wkv"), [128, 8, D], BF16)
        wfg = ar.alloc(uname("wfg"), [128, 8, 8], BF16)
        xk = ar.alloc(uname("xk"), [128, 8, 512], BF16)
        ktoks = [ar.alloc(uname("ktok"), [128, 512], F32) for _ in range(2)]
        ktmps = [ar.alloc(uname("ktmp"), [128, 512], F32) for _ in range(2)]
        kb16s = [ar.alloc(uname("kb16"), [128, 512], BF16) for _ in range(2)]
        chunk_i = [0]
        load_wsq(wkv, w_kv)
        k.dma("pool", wfg[:, :, :], w_fg.rearrange("(k p) h -> p k h", p=128))
        for i in range(4):
            k.memset("dve", Vp[i][:, :, :, 64:65], 1.0)
        h3 = lambda t, a=0, b_=512: t.v(lambda h: h[:, a:b_].rearrange("p (h d) -> p h d", h=8))
        for i in (4, 0, 1, 2, 3):
            n = TILES[i][1]
            samp = (i == 4)
            norm_tile(i, V_GKV, xk)
            for tcn in range(n // 128):
                gc = i * 4 + tcn
                r0 = TILES[i][0] + tcn * 128
                cs = slice(tcn * 128, (tcn + 1) * 128)
                chunk_i[0] += 1
                ktok, ktmp, kb16 = ktoks[chunk_i[0] % 2], ktmps[chunk_i[0] % 2], kb16s[chunk_i[0] % 2]
                bkK, bkV, bkF = k.bank(), k.bank(), k.bank()
                for kk in range(8):
                    k.mm(bkK[:, :], xk[:, kk, cs], wkv[:, kk, 0:512], start=(kk == 0), stop=(kk == 7))
                for kk in range(8):
                    k.mm(bkV[:, :], xk[:, kk, cs], wkv[:, kk, 512:1024], start=(kk == 0), stop=(kk == 7))
                for kk in range(8):
                    k.mm(bkF[:, 0:8], xk[:, kk, cs], wfg[:, kk, :], start=(kk == 0), stop=(kk == 7))
                sm = smbuf()
                sv = staging()
                k.copy("act", sv[:, 0:512], bkV[:, :])
                k.release(bkV)
                if samp:
                    k.dma("sp", vrs[:, :], sv[:, 0:512], is_out=True)
                    k.copy("dve", Vs[:, :], sv[:, 0:512])
                else:
                    k.dma("sp", vrp[r0:r0 + 128, :], sv[:, 0:512], is_out=True)
                    k.copy("dve", Vp[i][:, tcn, :, 0:64], h3(sv))
                k.copy("act", ktok[:, :], bkK[:, :])
                k.release(bkK)
                k.tt("dve", ktmp[:, :], ktok[:, :], ktok[:, :], ALU.mult)
                k.red(sm[:, 0:8], h3(ktmp))
                k.act(sm[:, 8:16], sm[:, 0:8], AF.Ln, scale=1.0 / 64, bias=EPS)
                k.act(sm[:, 16:24], sm[:, 8:16], AF.Exp, scale=-0.5)
                k.tt("dve", h3(ktmp), h3(ktok), sm.v(lambda h: h[:, 16:24].unsqueeze(2).to_broadcast([128, 8, 64])), ALU.mult)
                sk = staging()
                k.tt("dve", h3(sk), h3(ktmp), vecs.v(lambda h: h[:, V_GFK:V_GFK + 64].unsqueeze(1).to_broadcast([128, 8, 64])), ALU.mult)
                if samp:
                    k.dma("sp", krs[:, :], sk[:, 0:512], is_out=True)
                else:
                    k.dma("sp", krp[r0:r0 + 128, :], sk[:, 0:512], is_out=True)
                k.copy("act", kb16[:, :], sk[:, 0:512])
                bt = k.bank()
                btb = bt.v(lambda h: h[:, :].bitcast(BF16))
                for c in range(4):
                    k.tr(A(btb.ap[:, c * 128:(c + 1) * 128], btb.bufs), kb16[:, c * 128:(c + 1) * 128], ident_b)
                src = A(btb.ap[:, 0:512].rearrange("p (c t) -> p c t", c=4), btb.bufs)
                if samp:
                    evac(KTs[:, :, :], src)
                else:
                    evac(KTp[i][:, :, cs], src)
                k.release(bt)
                k.tt("dve", sm[:, 24:32], bkF[:, 0:8], vecs[:, V_BFG:V_BFG + 8], ALU.add)
                k.release(bkF)
                k.act(sm[:, 32:40], sm[:, 24:32], AF.Exp, scale=-1.0)
                k.act(sm[:, 40:48], sm[:, 32:40], AF.Ln, bias=1.0)
                if samp:
                    k.ts("dve", lf_s[:, :], sm[:, 40:48], -1.0, ALU.mult)
                    k.dma("sp", lfs[:, :], lf_s[:, :], is_out=True)
                    bkc = k.bank()
                    k.mm(bkc[:, 0:8], cst[:, C_BT:C_BT + 128], lf_s[:, :])
                    k.ts("dve", nfnew[:, :], bkc[:, 0:8], -1.0, ALU.mult)
                    k.release(bkc)
                else:
                    k.ts("dve", lf_all[:, gc, :], sm[:, 40:48], -1.0, ALU.mult)
                    k.dma("sp", lfp[r0:r0 + 128, :], lf_all[:, gc, :], is_out=True)
                    bkc = k.bank()
                    k.mm(bkc[:, 0:8], cst[:, C_UT:C_UT + 128], lf_all[:, gc, :], start=True, stop=(gc == 0))
                    if gc > 0:
                        k.mm(bkc[:, 0:8], ones_f, lfsum[:, :], start=False, stop=True)
                    k.copy("act", fcum[:, gc, :], bkc[:, 0:8])
                    k.release(bkc)
                    if gc == 0:
                        k.copy("dve", lfsum[:, :], lf_all[:, 0, :])
                    else:
                        k.tt("dve", lfsum[:, :], lfsum[:, :], lf_all[:, gc, :], ALU.add)
                    if tcn == 3:
                        bke = k.bank()
                        k.mm(bke[:, 0:8], cst[:, C_E127:C_E127 + 128], fcum[:, gc, :])
                        k.copy("act", fend[:, i, :], bke[:, 0:8])
                        k.release(bke)

    shared_kv()
    if stage <= 3:
        k.emit()
        return nc

    def mixer1():
        l = 1
        ar = Arena()
        wsq = [ar.alloc(uname("wsq"), [128, 8, D], BF16) for _ in range(2)]
        X = ar.alloc(uname("X1"), [128, 8, 512], BF16)
        zm = [ar.alloc(uname("zm"), [128, 512], F32) for _ in range(2)]
        qm = ar.alloc(uname("qm"), [128, 4, 512], BF16)
        Pm = [ar.alloc(uname("Pm"), [128, 2, 512], BF16) for _ in range(2)]
        lnd = rden = None
        biasJ = ar.alloc(uname("biasJ"), [128, 16, 8], F32)
        p0 = ar.p
        qT = ar.alloc(uname("qT"), [128, 4, 512], BF16)
        PT = [ar.alloc(uname("PT"), [128, 512], BF16) for _ in range(3)]
        otok = ar.alloc(uname("otok"), [128, 4, 8, 64], BF16)
        arw = Arena(wsq[0].buf.lo, wsq[0].buf.hi)
        KTg = arw.alloc(uname("KTg"), [128, 4, 8, 128], BF16)
        mks = arw.alloc(uname("mks"), [128, 2, 512], BF16)
        mksT = arw.alloc(uname("mksT"), [128, 4, NMEM], BF16)
        mvs = arw.alloc(uname("mvs"), [128, 2, 512], BF16)
        qbd = arw.alloc(uname("qbd"), [128, 4, BPC, 16], BF16)
        axs = Arena(xn[0].buf.lo, xn[4].buf.hi)
        Kg = [axs.alloc(uname("Kg"), [128, 4, 512], BF16) for _ in range(4)]
        Vg = [axs.alloc(uname("Vg"), [128, 4, 512], BF16) for _ in range(4)]
        ars = Arena(p0)
        qTs = ars.alloc(uname("qTs"), [128, 4, NSAMP], BF16)
        ptb = ars.alloc(uname("ptb"), [128, BPC], I32)
        itmp = ars.alloc(uname("itmp"), [128, BPC], I32)
        idx8 = ars.alloc(uname("idx8"), [128, BPC], I32)
        idx32 = ars.alloc(uname("idx32"), [128, BPC, 4], I32)
        rg4 = ars.alloc(uname("rg4"), [128, 1], I32)
        lfgs = [ars.alloc(uname("lfg"), [128, 16, 8], F32) for _ in range(2)]
        biasbs = [ars.alloc(uname("biasb"), [128, 16, 8], F32) for _ in range(2)]
        tmpS = ars.alloc(uname("tmpS"), [128, 512], F32)
        Pg = ars.alloc(uname("Pg"), [128, 512], BF16)
        Pn = ars.alloc(uname("Pn"), [128, 64], BF16)
        Pms = ars.alloc(uname("Pms"), [128, 64], BF16)
        mks2 = [mks, ars.alloc(uname("mksb"), [128, 2, 512], BF16)]
        mvs2 = [mvs, ars.alloc(uname("mvsb"), [128, 2, 512], BF16)]

        load_wsq(wsq[0], w_in[l])
        load_wsq(wsq[1], w_out[l])

        def in_proj(i, n, qdst):
            norm_tile(i, V_GMIX + l * 8, X)
            def fin1(bk, c):
                if c < 4:
                    head_norm(bk, n, zm[c % 2], 64, bd_b, V_GFQ, qdst[:, c, 0:n])
                else:
                    head_norm(bk, n, zm[c % 2], 128, ones_b, V_GMQ + l, qm[:, c - 4, 0:n])

            pend = None
            for c in range(8):
                bk = k.bank()
                for kk in range(8):
                    k.mm(bk[:, 0:n], wsq[0][:, kk, c * 128:(c + 1) * 128], X[:, kk, 0:n], start=(kk == 0), stop=(kk == 7))
                if pend is not None:
                    fin1(*pend)
                pend = (bk, c)
            fin1(*pend)

        ks1 = float(os.environ.get('KSUB1', '99'))
        for J in range(4):
            n = 512
            in_proj(J, n, qT)
            if ks1 <= 1:
                return
            cat = X
            nk = 4 * J + 4
            k.tt("dve", biasJ[:, 0:nk, :], fend.v(lambda h: h[:, J, :].unsqueeze(1).to_broadcast([128, nk, 8])),
                 fcum[:, 0:nk, :], ALU.subtract)
            for h in range(8):
                c, hp = h // 2, slice((h % 2) * 64, (h % 2) * 64 + 64)
                acc = k.bank()
                accv = acc.v(lambda hh: hh[:, 0:260].rearrange("p (q e) -> p q e", q=4))
                banks = {}

                def S_E(kc):
                    ti, tcn = kc // 4, kc % 4
                    nq0 = max(0, kc - 4 * J) * 128
                    bkS = k.bank()
                    k.mm(bkS[:, nq0:512], A(KTp[ti].h[hp, c, tcn * 128:(tcn + 1) * 128], [KTp[ti].buf]),
                         A(qT.h[hp, c, nq0:512], [qT.buf]))
                    P = PT[kc % 3]
                    k.act(P[:, nq0:512], bkS[:, nq0:512], AF.Exp, scale=0.125, bias=biasJ[:, kc, h:h + 1])
                    k.release(bkS)
                    if kc >= 4 * J:
                        k.tt("dve", P[:, nq0:nq0 + 128], P[:, nq0:nq0 + 128], ut_b, ALU.mult)

                S_E(0)
                if nk > 1:
                    S_E(1)
                first = True
                for kc in range(nk):
                    if kc + 2 < nk:
                        S_E(kc + 2)
                    ti, tcn = kc // 4, kc % 4
                    nq0 = max(0, kc - 4 * J) * 128
                    P = PT[kc % 3]
                    for tq in range(nq0 // 128, 4):
                        k.mm(A(accv.ap[:, tq, :], accv.bufs), P[:, tq * 128:(tq + 1) * 128], Vp[ti][:, tcn, h, :],
                             start=first, stop=(kc == 4 * J + tq), sgc=True)
                        first = False
                sm = smbuf()
                k.recip(sm.v(lambda hh: hh[:, 0:4].unsqueeze(2)), A(accv.ap[:, :, 64:65], accv.bufs))
                k.tt("dve", otok[:, :, h, :], A(accv.ap[:, :, 0:64], accv.bufs),
                     sm.v(lambda hh: hh[:, 0:4].unsqueeze(2).to_broadcast([128, 4, 64])), ALU.mult)
                k.release(acc)
            if ks1 <= 2:
                return
            for tp in range(2):
                bt = k.bank()
                btb = bt.v(lambda hh: hh[:, :].bitcast(BF16))
                for tq2 in range(2):
                    tq = tp * 2 + tq2
                    for c in range(4):
                        o = (tq2 * 4 + c) * 128
                        k.tr(A(btb.ap[:, o:o + 128], btb.bufs),
                             otok.v(lambda hh, tq=tq, c=c: hh[:, tq, 2 * c:2 * c + 2, :].rearrange("p a d -> p (a d)")), ident_b)
                for tq2 in range(2):
                    tq = tp * 2 + tq2
                    evac(cat[:, 0:4, tq * 128:(tq + 1) * 128],
                         A(btb.ap[:, tq2 * 512:(tq2 + 1) * 512].rearrange("p (c t) -> p c t", c=4), btb.bufs))
                k.release(bt)
            mem_attn_prompt(l, n, qm, Pm, lnd, rden, cat)
            out_proj(J, wsq[1], cat)
            if ks1 <= 3:
                return
        if ks1 <= 4:
            return

        n = NSAMP
        in_proj(4, n, qTs)
        cat = X
        k.memset("dve", qbd[:, :, :, :], 0.0)
        for c in range(4):
            k.copy("dve", A(qbd.h[0:64, c, :, 0:8], [qbd.buf]),
                   A(qTs.h[0:64, c, :].rearrange("p (b t) -> p b t", b=BPC), [qTs.buf]))
            k.copy("dve", A(qbd.h[64:128, c, :, 8:16], [qbd.buf]),
                   A(qTs.h[64:128, c, :].rearrange("p (b t) -> p b t", b=BPC), [qTs.buf]))
        k.dma("sp", ptb[:, :], ptab[:, :])
        k.ts("pool", itmp[:, :], ptb[:, :], 8, ALU.mult)
        k.tt("pool", idx8[:, :], itmp[:, :], cint.v(lambda hh: hh[:, 0:1].to_broadcast([128, BPC])), ALU.add)
        k.ts("pool", rg4[:, :], cint[:, :], 4, ALU.mult)
        k.ts("pool", itmp[:, :], ptb[:, :], 32, ALU.mult)
        k.tt("pool", idx32[:, :, 0], itmp[:, :], rg4.v(lambda hh: hh[:, 0:1].to_broadcast([128, BPC])), ALU.add)
        for j in range(1, 4):
            k.ts("pool", idx32[:, :, j], idx32[:, :, 0], j, ALU.add)
        if ks1 <= 5:
            return
        for b in range(BPC):
            lfg, biasb = lfgs[b % 2], biasbs[b % 2]
            k.gather(lfg.v(lambda hh: hh[:, :, :].rearrange("p r h -> p (r h)")), clf, idx8[:, b:b + 1])
            for r in range(1, 16):
                k.tt("dve", lfg[:, r, :], lfg[:, r, :], lfg[:, r - 1, :], ALU.add)
            bkb = k.bank()
            k.mm(bkb[:, 0:8], cst[:, C_GE:C_GE + 128], lfg[:, 15, :])
            k.tt("dve", biasb[:, :, :], bkb.v(lambda hh: hh[:, 0:8].unsqueeze(1).to_broadcast([128, 16, 8])),
                 lfg[:, :, :], ALU.subtract)
            k.release(bkb)
            if ks1 <= 6:
                return
            bkO = k.bank()
            first = [True, True]
            for hf in range(2):
                for jj in range(2):
                    j = hf * 2 + jj
                    k.gather(Kg[j].v(lambda hh: hh[:, :, :].rearrange("p r f -> p (r f)")), ck, idx32[:, b, j:j + 1])
                    k.gather(Vg[j].v(lambda hh: hh[:, :, :].rearrange("p r f -> p (r f)")), cv, idx32[:, b, j:j + 1])
                kgr = lambda r, hf=hf: Kg[hf * 2 + r // 4]
                vgr = lambda r, hf=hf: Vg[hf * 2 + r // 4]
                if ks1 <= 6.1:
                    return
                for c in range(4):
                    bt = k.bank()
                    btb = bt.v(lambda hh: hh[:, :].bitcast(BF16))
                    for r in range(8):
                        k.tr(A(btb.ap[:, r * 128:(r + 1) * 128], btb.bufs), kgr(r)[:, r % 4, c * 128:(c + 1) * 128], ident_b)
                    evac(KTg[:, c, :, :], A(btb.ap[:, 0:1024].rearrange("p (r s) -> p r s", r=8), btb.bufs))
                    k.release(bt)
                if ks1 <= 6.2:
                    return
                bkS = k.bank()
                for r in range(8):
                    for c in range(4):
                        o = (r * 4 + c) * 16
                        k.mm(bkS[:, o:o + 16], KTg[:, c, r, :], qbd[:, c, b, :])
                k.stt(tmpS.v(lambda hh: hh[:, :].rearrange("p (x t) -> p x t", t=8)),
                      bkS.v(lambda hh: hh[:, :].rearrange("p (x t) -> p x t", t=8)), 0.125,
                      biasb.v(lambda hh, hf=hf: hh[:, hf * 8:(hf + 1) * 8, :].rearrange("p r h -> p (r h)").unsqueeze(2).to_broadcast([128, 64, 8])),
                      ALU.mult, ALU.add)
                k.release(bkS)
                k.act(Pg[:, :], tmpS[:, :], AF.Exp)
                if ks1 <= 6.3:
                    return
                Pg5 = lambda r, c, hh_: A(Pg.h[:, (r * 8 + c * 2 + hh_) * 8:(r * 8 + c * 2 + hh_) * 8 + 8], [Pg.buf])
                for r in range(8):
                    for h in range(8):
                        c, hh_ = h // 2, h % 2
                        k.mm(A(bkO.h[hh_ * 64:(hh_ + 1) * 64, c * 8:(c + 1) * 8], [bkO.buf]),
                             vgr(r)[:, r % 4, h * 64:(h + 1) * 64], Pg5(r, c, hh_), start=first[hh_], stop=False, sgc=True)
                        first[hh_] = False
                    for hh_ in range(2):
                        k.mm(A(bkO.h[hh_ * 64:(hh_ + 1) * 64, 32:64].rearrange("p (c t) -> p c t", c=4), [bkO.buf]),
                             A(ones_b.ap[:, 0:64], ones_b.bufs),
                             A(Pg.h[:, r * 64:(r + 1) * 64].rearrange("p (c a t) -> p c a t", c=4, a=2)[:, :, hh_, :], [Pg.buf]),
                             start=False, stop=False, sgc=True)
            if ks1 <= 6.4:
                return
            bkS2 = k.bank()
            for c in range(4):
                k.mm(bkS2[:, c * 16:(c + 1) * 16], KTs[:, c, :], qbd[:, c, b, :])
            k.stt(tmpS.v(lambda hh: hh[:, 0:64].rearrange("p (x t) -> p x t", t=8)),
                  bkS2.v(lambda hh: hh[:, 0:64].rearrange("p (x t) -> p x t", t=8)), 0.125,
                  nfnew.v(lambda hh: hh[:, :].unsqueeze(2).to_broadcast([128, 8, 8])), ALU.mult, ALU.add)
            k.release(bkS2)
            k.act(Pn[:, :], tmpS[:, 0:64], AF.Exp)
            k.tt("dve", Pn.v(lambda hh: hh[:, :].rearrange("p (x t) -> p x t", t=8)),
                 Pn.v(lambda hh: hh[:, :].rearrange("p (x t) -> p x t", t=8)),
                 masks_b.v(lambda hh, b=b: hh[:, b * 8:(b + 1) * 8].unsqueeze(1).to_broadcast([128, 8, 8])), ALU.mult)
            for h in range(8):
                c, hh_ = h // 2, h % 2
                k.mm(A(bkO.h[hh_ * 64:(hh_ + 1) * 64, c * 8:(c + 1) * 8], [bkO.buf]),
                     Vs[:, h * 64:(h + 1) * 64], Pn[:, h * 8:(h + 1) * 8], start=False, stop=True, sgc=True)
            for hh_ in range(2):
                k.mm(A(bkO.h[hh_ * 64:(hh_ + 1) * 64, 32:64].rearrange("p (c t) -> p c t", c=4), [bkO.buf]),
                     A(ones_b.ap[:, 0:64], ones_b.bufs),
                     A(Pn.h[:, :].rearrange("p (c a t) -> p c a t", c=4, a=2)[:, :, hh_, :], [Pn.buf]),
                     start=False, stop=True, sgc=True)
            sm = smbuf()
            k.act(sm[:, 0:32], bkO[:, 32:64], AF.Ln)
            k.act(sm[:, 32:64], sm[:, 0:32], AF.Exp, scale=-1.0)
            k.tt("dve", cat[:, 0:4, b * 8:(b + 1) * 8],
                 bkO.v(lambda hh: hh[:, 0:32].rearrange("p (c t) -> p c t", c=4)),
                 sm.v(lambda hh: hh[:, 32:64].rearrange("p (c t) -> p c t", c=4)), ALU.mult)
            k.release(bkO)
            if ks1 <= 7:
                return
        mem_attn_sample(l, qm, cat, mks2, mksT, mvs2, Pms)
        out_proj(4, wsq[1], cat)

    mixer1()
    if float(os.environ.get('KSUB1', '99')) < 99:
        k.emit()
        return nc
    ffn(1)

    for i, (c0, n) in enumerate(TILES):
        for tcn in range(n // 128):
            so = staging()
            for half in range(2):
                bk = k.bank()
                for j in range(4):
                    k.tr(bk[:, j * 128:(j + 1) * 128], hT[i][:, half * 4 + j, tcn * 128:(tcn + 1) * 128], ident)
                evac(so[:, half * 512:(half + 1) * 512], bk[:, :])
                k.release(bk)
            if i < 4:
                k.dma("sp", y_p[c0 + tcn * 128:c0 + (tcn + 1) * 128, :], so[:, :], is_out=True)
            else:
                k.dma("sp", y_s[:, :], so[:, :], is_out=True)
    k.emit()
    return nc


def _consts():
    c = np.zeros((128, NCST), np.float32)
    p = np.arange(128)
    c[:, C_ID:C_ID + 128] = np.eye(128, dtype=np.float32)
    c[:, C_ONES:C_ONES + 128] = 1.0
    c[:, C_BD:C_BD + 128] = (p[:, None] // 64 == p[None, :] // 64)
    c[:, C_UT:C_UT + 128] = (p[:, None] <= p[None, :])
    c[127, C_E127:C_E127 + 128] = 1.0
    c[:, C_GE:C_GE + 128] = (p[:, None] >= p[None, :])
    c[:, C_BT:C_BT + 128] = (p[:, None] // 8 == p[None, :] // 8) & (p[:, None] <= p[None, :])
    for g, w in enumerate((2, 4, 8, 16)):
        t = np.arange(16)
        c[:, C_INVC + g * 16:C_INVC + (g + 1) * 16] = 1.0 / np.minimum(w, t + 1)
    m = np.zeros((128, 16, 8), np.float32)
    for b in range(16):
        for t in range(8):
            m[:, b, t] = (p // 8 == b) & (p % 8 <= t)
    c[:, C_MASKS:C_MASKS + 128] = m.reshape(128, 128)
    ci = (p % 8).astype(np.int32).reshape(128, 1)
    return c, ci


def _vecs(g_mix, g_ffn, g_mem, g_kv, g_mem_q, g_mem_k, pool_scale, g_fox_k, g_fox_q, b_fg):
    v = np.zeros((128, NV), np.float32)
    fm = lambda a: np.ascontiguousarray(a.reshape(8, 128).T)
    for l in range(2):
        v[:, V_GMIX + l * 8:V_GMIX + (l + 1) * 8] = fm(g_mix[l])
        v[:, V_GFFN + l * 8:V_GFFN + (l + 1) * 8] = fm(g_ffn[l])
        v[:, V_GMEM + l * 8:V_GMEM + (l + 1) * 8] = fm(g_mem[l])
        v[:, V_GMQ + l] = g_mem_q[l]
        v[:, V_GMK + l * 128:V_GMK + (l + 1) * 128] = np.broadcast_to(g_mem_k[l][None, :], (128, 128))
    v[:, V_GKV:V_GKV + 8] = fm(g_kv)
    v[:, V_PSC:V_PSC + 4] = pool_scale[0].reshape(4, 128).T
    v[:, V_GFQ] = np.tile(g_fox_q[0], 2)
    v[:, V_GFK:V_GFK + 64] = np.broadcast_to(g_fox_k[None, :], (128, 64))
    v[:, V_BFG:V_BFG + 8] = np.broadcast_to(b_fg[None, :], (128, 8))
    return v


_NC_CACHE = {}


def kernel(x_prompt, x_sample, cache_mem_k, cache_mem_v, state_pool, cache_k, cache_v, cache_logf,
           page_table, mem_prompt, g_mix, w_in, w_out, g_ffn, w_gu, w_down, g_mem, w_mem_kv,
           g_mem_q, g_mem_k, w_pool, pool_scale, g_kv, w_kv, g_fox_k, w_fg, b_fg, g_fox_q,
           _stage=99):
    f = lambda a: np.ascontiguousarray(np.asarray(a, dtype=np.float32))
    x_prompt, x_sample, mem_prompt = f(x_prompt), f(x_sample), f(mem_prompt)
    cache_mem_k, cache_mem_v, state_pool = f(cache_mem_k), f(cache_mem_v), f(state_pool)
    cache_k, cache_v, cache_logf = f(cache_k), f(cache_v), f(cache_logf)
    page_table = np.ascontiguousarray(np.asarray(page_table, dtype=np.int32))
    nphys = cache_k.shape[0]
    key = (_stage, nphys)
    if key not in _NC_CACHE:
        _NC_CACHE[key] = build_program(_stage, nphys)
    nc = _NC_CACHE[key]
    cst, cint = _consts()
    vecs = _vecs(f(g_mix), f(g_ffn), f(g_mem), f(g_kv), f(g_mem_q), f(g_mem_k), f(pool_scale),
                 f(g_fox_k), f(g_fox_q), f(b_fg))
    ck = cache_k.reshape(nphys * 32, 2048)
    cv = cache_v.reshape(nphys * 32, 2048)
    clf = cache_logf.reshape(nphys * 8, 16 * 8)
    shared = {
        "ck": ck, "cv": cv, "clf": clf,
        "w_in": f(w_in), "w_out": f(w_out), "w_gu": f(w_gu), "w_down": f(w_down), "w_mkv": f(w_mem_kv),
        "w_pool": f(w_pool).reshape(4, 128, 128), "w_kv": f(w_kv), "w_fg": f(w_fg),
        "vecs": vecs, "cst": cst, "cint": cint,
    }
    in_maps = []
    for c in range(NB):
        m = dict(shared)
        bs = slice(c * BPC, (c + 1) * BPC)
        m["xp"] = x_prompt[c]
        m["xs"] = x_sample[bs].reshape(NSAMP, D)
        m["memp"] = mem_prompt[c]
        m["cmk"] = cache_mem_k[:, bs].reshape(2, BPC, NMEM, 512)
        m["cmv"] = cache_mem_v[:, bs].reshape(2, BPC, NMEM, 512)
        m["spool"] = state_pool[0, bs].reshape(BPC * 15, 512)
        m["ptab"] = np.ascontiguousarray(np.repeat(page_table[bs].T, 8, axis=0))
        in_maps.append(m)
    ncores = int(os.environ.get("KCORES", NB))
    res = run_bass_kernel_spmd(nc, in_maps[:ncores], core_ids=list(range(ncores)))
    r = list(res.results)
    while len(r) < NB:
        r.append({kk: np.zeros_like(np.asarray(v)) for kk, v in r[0].items()})
    cat = lambda name: [np.asarray(r[c][name]) for c in range(NB)]
    y_prompt = np.stack(cat("y_p")).reshape(NB, SEQ, D)
    y_sample = np.concatenate(cat("y_s")).reshape(DB, DS, D)
    k_rows_prompt = np.stack(cat("krp")).reshape(NB, SEQ // PAGE, PAGE, 8, 64)
    v_rows_prompt = np.stack(cat("vrp")).reshape(NB, SEQ // PAGE, PAGE, 8, 64)
    logf_rows_prompt = np.stack(cat("lfp")).reshape(NB, SEQ // PAGE, PAGE, 8)
    mem_k_prompt = np.stack(cat("mkp"), axis=1).reshape(2, NB, NMEM, 4, 128)
    mem_v_prompt = np.stack(cat("mvp"), axis=1).reshape(2, NB, NMEM, 4, 128)
    pool_state_prompt = np.stack(cat("psp")).reshape(1, NB, 15, 512)
    k_rows_sample = np.concatenate(cat("krs")).reshape(DB, DS, 8, 64)
    v_rows_sample = np.concatenate(cat("vrs")).reshape(DB, DS, 8, 64)
    logf_rows_sample = np.concatenate(cat("lfs")).reshape(DB, DS, 8)
    pool_state_sample = np.concatenate(cat("pss")).reshape(1, DB, 15, 512)
    return (y_prompt, y_sample, k_rows_prompt, v_rows_prompt, logf_rows_prompt, mem_k_prompt,
            mem_v_prompt, pool_state_prompt, k_rows_sample, v_rows_sample, logf_rows_sample,
            pool_state_sample)
```

```python
import os
import numpy as np
import concourse.bass as bass
import concourse.mybir as mybir
from concourse.bass_utils import run_bass_kernel_spmd

F32 = mybir.dt.float32
BF16 = mybir.dt.bfloat16
I32 = mybir.dt.int32
AF = mybir.ActivationFunctionType
ALU = mybir.AluOpType
AX = mybir.AxisListType

D = 1024
SEQ = 2048
NB = 8
DB = 128
DS = 8
BPC = DB // NB
NSAMP = BPC * DS
DFF = 2816
NJ = DFF // 128
NPAGES = 16
PAGE = 128
NPHYS = 2560
NMEM = 256
EPS = 1e-6
TILES = [(0, 512), (512, 512), (1024, 512), (1536, 512), (2048, 128)]
NT = len(TILES)

V_GMIX, V_GFFN, V_GMEM, V_GKV, V_GMQ, V_PSC, V_GFQ, V_GMK, V_GFK, V_BFG, NV = 0, 16, 32, 48, 56, 58, 62, 64, 320, 384, 392
C_ID, C_ONES, C_BD, C_UT, C_E127, C_GE, C_BT, C_INVC, C_MASKS, NCST = 0, 128, 256, 384, 512, 640, 768, 896, 960, 1088


class Buf:
    __slots__ = ("name", "space", "lo", "hi", "w", "r", "ov", "dma")

    def __init__(self, name, space, lo, hi):
        self.name, self.space, self.lo, self.hi = name, space, lo, hi
        self.w = None
        self.r = {}
        self.ov = None
        self.dma = False


class A:
    __slots__ = ("ap", "bufs")

    def __init__(self, ap, bufs):
        self.ap, self.bufs = ap, bufs


class TT:
    def __init__(self, h, buf):
        self.h, self.buf = h, buf

    def __getitem__(self, idx):
        return A(self.h[idx], [self.buf])

    def v(self, fn):
        return A(fn(self.h), [self.buf])


class Op:
    __slots__ = ("eng", "fn", "key", "pos", "signal", "waits", "is_dma", "val")

    def __init__(self, eng, fn, key, is_dma):
        self.eng, self.fn, self.key, self.is_dma = eng, fn, key, is_dma
        self.pos = 0
        self.signal = is_dma
        self.waits = {}
        self.val = 0


class K:
    ENGS = ("pe", "act", "dve", "pool", "sp")

    def __init__(self, nc):
        self.nc = nc
        self.q = {e: [] for e in self.ENGS}
        self.bufs = []
        self.chan_n = {}
        self.out_chans = set()
        self.sb_lo = (nc.sbuf_base + 63) // 64 * 64
        self.sb_hi = nc.sbuf_top
        self.sb_ptr = self.sb_lo
        self.banks = []
        self.free_banks = []

    def sb(self, name, shape, dt, off=None):
        nbytes = int(np.prod(shape[1:])) * mybir.dt.size(dt)
        if off is None:
            off = self.sb_ptr
            self.sb_ptr = (off + nbytes + 31) // 32 * 32
        assert off + nbytes <= self.sb_hi, (name, off, nbytes, self.sb_hi)
        h = self.nc.alloc_sbuf_tensor_at(name, list(shape), dt, offset=off)
        b = Buf(name, "sb", off, off + nbytes)
        self.bufs.append(b)
        for x in self.bufs:
            x.ov = None
        return TT(h, b)

    def make_banks(self):
        for i in range(8):
            h = self.nc.alloc_psum_tensor("bank%d" % i, [128, 512], F32)
            b = Buf("bank%d" % i, "ps", i, i + 1)
            self.bufs.append(b)
            self.banks.append(TT(h, b))
        self.free_banks = list(self.banks)

    def bank(self):
        assert self.free_banks, "out of PSUM banks"
        return self.free_banks.pop(0)

    def release(self, bk):
        self.free_banks.append(bk)

    def _ov(self, b):
        if b.ov is None:
            b.ov = [x for x in self.bufs if x.space == b.space and x.lo < b.hi and b.lo < x.hi]
        return b.ov

    @staticmethod
    def _dep(op, prod):
        if prod is None or prod is op:
            return
        if prod.eng == "pe" and op.eng == "pe" and not prod.is_dma:
            return
        cur = op.waits.get(prod.key)
        if cur is None or cur.pos < prod.pos:
            op.waits[prod.key] = prod

    def _add(self, eng, fn, R, W, is_dma=False, chan=None):
        if is_dma:
            key = chan
            op = Op(eng, fn, key, True)
            n = self.chan_n.get(key, 0) + 1
            self.chan_n[key] = n
            op.pos = n
        else:
            op = Op(eng, fn, eng, False)
            op.pos = len(self.q[eng]) + 1
        for b in R:
            for x in self._ov(b):
                self._dep(op, x.w)
                if b.space == "ps":
                    for rr in x.r.values():
                        if rr.eng != eng:
                            self._dep(op, rr)
        for b in W:
            for x in self._ov(b):
                self._dep(op, x.w)
                for rr in x.r.values():
                    self._dep(op, rr)
        rkey = op.key if is_dma else eng
        for b in R:
            b.r[rkey] = op
        for b in W:
            b.w = op
            b.r = {}
        self.q[eng].append(op)
        return op

    def op(self, eng, fn, R=(), W=()):
        rb = [b for a in R for b in a.bufs]
        wb = [b for a in W for b in a.bufs]
        return self._add(eng, fn, rb, wb)

    def mm(self, out, lhsT, rhs, start=True, stop=True, sgc=False):
        return self.op("pe", lambda e: e.matmul(out.ap, lhsT=lhsT.ap, rhs=rhs.ap, start=start, stop=stop,
                                                skip_group_check=sgc),
                       R=[lhsT, rhs], W=[out])

    def tr(self, out, in_, ident):
        return self.op("pe", lambda e: e.transpose(out.ap, in_.ap, ident.ap), R=[in_, ident], W=[out])

    def act(self, out, in_, func, scale=1.0, bias=None, eng="act"):
        R = [in_]
        kw = {}
        if isinstance(scale, A):
            R.append(scale)
            kw["scale"] = scale.ap
        else:
            kw["scale"] = float(scale)
        if isinstance(bias, A):
            R.append(bias)
            kw["bias"] = bias.ap
        elif bias is not None:
            kw["bias"] = float(bias)
        return self.op("act", lambda e: e.activation(out=out.ap, in_=in_.ap, func=func, **kw), R=R, W=[out])

    def tt(self, eng, out, in0, in1, op):
        return self.op(eng, lambda e: e.tensor_tensor(out=out.ap, in0=in0.ap, in1=in1.ap, op=op),
                       R=[in0, in1], W=[out])

    def ts(self, eng, out, in0, s1, op0, s2=None, op1=None):
        R = [in0]
        a1 = s1
        if isinstance(s1, A):
            R.append(s1)
            a1 = s1.ap
        a2 = s2
        if isinstance(s2, A):
            R.append(s2)
            a2 = s2.ap
        if op1 is None:
            return self.op(eng, lambda e: e.tensor_scalar(out=out.ap, in0=in0.ap, scalar1=a1, scalar2=None, op0=op0),
                           R=R, W=[out])
        return self.op(eng, lambda e: e.tensor_scalar(out=out.ap, in0=in0.ap, scalar1=a1, scalar2=a2, op0=op0, op1=op1),
                       R=R, W=[out])

    def stt(self, out, in0, scalar, in1, op0, op1):
        R = [in0, in1]
        sc = scalar
        if isinstance(scalar, A):
            R.append(scalar)
            sc = scalar.ap
        return self.op("dve", lambda e: e.scalar_tensor_tensor(out=out.ap, in0=in0.ap, scalar=sc, in1=in1.ap,
                                                               op0=op0, op1=op1), R=R, W=[out])

    def copy(self, eng, out, in_):
        if eng == "act":
            return self.op("act", lambda e: e.copy(out=out.ap, in_=in_.ap), R=[in_], W=[out])
        return self.op(eng, lambda e: e.tensor_copy(out=out.ap, in_=in_.ap), R=[in_], W=[out])

    def recip(self, out, in_):
        return self.op("dve", lambda e: e.reciprocal(out=out.ap, in_=in_.ap), R=[in_], W=[out])

    def red(self, out, in_, op=ALU.add, axis=AX.X):
        return self.op("dve", lambda e: e.tensor_reduce(out=out.ap, in_=in_.ap, op=op, axis=axis), R=[in_], W=[out])

    def memset(self, eng, out, val):
        return self.op(eng, lambda e: e.memset(out.ap, val), R=[], W=[out])

    def dma(self, q, out, in_, is_out=False, **kw):
        R, W = [], []
        o_ap, i_ap = out, in_
        chan = None
        if isinstance(out, A):
            W = list(out.bufs)
            o_ap = out.ap
            chan = out.bufs[0].name
        if isinstance(in_, A):
            R = list(in_.bufs)
            i_ap = in_.ap
            if chan is None:
                chan = in_.bufs[0].name
        if is_out:
            self.out_chans.add(chan)
        return self._add(q, lambda e: e.dma_start(out=o_ap, in_=i_ap, **kw), R, W, is_dma=True, chan=chan)

    def gather(self, out, in_ap, idx):
        chan = out.bufs[0].name
        return self._add("pool", lambda e: e.indirect_dma_start(
            out=out.ap, out_offset=None, in_=in_ap,
            in_offset=bass.IndirectOffsetOnAxis(ap=idx.ap, axis=0)),
            list(idx.bufs), list(out.bufs), is_dma=True, chan=chan)

    def emit(self):
        nc = self.nc
        for e in self.ENGS:
            for op in self.q[e]:
                for prod in op.waits.values():
                    prod.signal = True
        sems = {}
        for e in ("pe", "act", "dve", "pool"):
            sems[e] = nc.alloc_semaphore("s_" + e)
        for ch in self.chan_n:
            sems[ch] = nc.alloc_semaphore("d_" + ch)
        for e in self.ENGS:
            n = 0
            for op in self.q[e]:
                if op.is_dma:
                    op.val = 16 * op.pos
                elif op.signal:
                    n += 1
                    op.val = n
        engobj = {"pe": "tensor", "act": "scalar", "dve": "vector", "pool": "gpsimd", "sp": "sync"}
        out_chans = sorted(self.out_chans)
        chan_n = self.chan_n
        q = self.q
        stats = {}
        with nc.Block() as block:
            def body(ename):
                def f(e):
                    known = {}
                    nw = 0
                    for op in q[ename]:
                        for key, prod in op.waits.items():
                            v = prod.val
                            if known.get(key, 0) >= v:
                                continue
                            e.wait_ge(sems[key], v)
                            known[key] = v
                            nw += 1
                        inst = op.fn(e)
                        if op.signal:
                            inst.then_inc(sems[op.key], 16 if op.is_dma else 1)
                    if ename == "sp":
                        for ch in out_chans:
                            e.wait_ge(sems[ch], 16 * chan_n[ch])
                    stats[ename] = (len(q[ename]), nw)
                return f
            block.tensor(body("pe"))
            block.scalar(body("act"))
            block.vector(body("dve"))
            block.gpsimd(body("pool"))
            block.sync(body("sp"))
        if os.environ.get("KSTATS"):
            print("KSTATS", stats, "nsem", len(sems))


def build_program(stage=99, nphys=NPHYS):
    nc = bass.Bass("TRN2", target_bir_lowering=False)
    k = K(nc)

    def din(name, shape, dt=F32):
        return nc.dram_tensor(name, list(shape), dt, kind="ExternalInput").ap()

    def dout(name, shape, dt=F32):
        return nc.dram_tensor(name, list(shape), dt, kind="ExternalOutput").ap()

    xp = din("xp", [SEQ, D])
    xs = din("xs", [NSAMP, D])
    memp = din("memp", [NMEM, D])
    cmk = din("cmk", [2, BPC, NMEM, 512])
    cmv = din("cmv", [2, BPC, NMEM, 512])
    spool = din("spool", [BPC * 15, 512])
    ck = din("ck", [nphys * 32, 2048])
    cv = din("cv", [nphys * 32, 2048])
    clf = din("clf", [nphys * 8, 16 * 8])
    ptab = din("ptab", [128, BPC], I32)
    w_in = din("w_in", [2, D, D])
    w_out = din("w_out", [2, D, D])
    w_gu = din("w_gu", [2, D, 2 * DFF])
    w_down = din("w_down", [2, DFF, D])
    w_mkv = din("w_mkv", [2, D, D])
    w_pool = din("w_pool", [4, 128, 128])
    w_kv = din("w_kv", [D, D])
    w_fg = din("w_fg", [D, 8])
    vecs_d = din("vecs", [128, NV])
    cst_d = din("cst", [128, NCST])
    cint_d = din("cint", [128, 1], I32)

    y_p = dout("y_p", [SEQ, D])
    y_s = dout("y_s", [NSAMP, D])
    krp = dout("krp", [SEQ, 512])
    vrp = dout("vrp", [SEQ, 512])
    lfp = dout("lfp", [SEQ, 8])
    mkp = dout("mkp", [2, NMEM, 512])
    mvp = dout("mvp", [2, NMEM, 512])
    psp = dout("psp", [15, 512])
    krs = dout("krs", [NSAMP, 512])
    vrs = dout("vrs", [NSAMP, 512])
    lfs = dout("lfs", [NSAMP, 8])
    pss = dout("pss", [BPC * 15, 512])

    k.make_banks()

    cst = k.sb("cst", [128, NCST], F32)
    vecs = k.sb("vecs", [128, NV], F32)
    cint = k.sb("cint", [128, 1], I32)
    cb = k.sb("cb", [128, 4 * 128], BF16)
    masks_b = k.sb("masks_b", [128, 128], BF16)
    hT = [k.sb("hT%d" % i, [128, 8, n], F32) for i, (_, n) in enumerate(TILES)]
    xn = [k.sb("xn%d" % i, [128, 8, n], BF16) for i, (_, n) in enumerate(TILES)]
    mkT = [k.sb("mkT%d" % l, [128, 4, NMEM], BF16) for l in range(2)]
    mvb = [k.sb("mvb%d" % l, [128, 2, 512], BF16) for l in range(2)]
    rstds = [k.sb("rstd%d" % i, [128, 512], F32) for i in range(2)]
    lnbs = [k.sb("lnb%d" % i, [128, 512], F32) for i in range(2)]
    rs_i = [0]

    def rs_pair():
        rs_i[0] += 1
        return rstds[rs_i[0] % 2], lnbs[rs_i[0] % 2]
    sq = [k.sb("sq%d" % i, [128, 512], BF16) for i in range(2)]
    stg = [k.sb("stg%d" % i, [128, 1024], F32) for i in range(3)]
    small = k.sb("small", [128, 256], F32)
    KTs = k.sb("KTs", [128, 4, NSAMP], BF16)
    Vs = k.sb("Vs", [128, 512], BF16)
    lf_all = k.sb("lf_all", [128, 16, 8], F32)
    fcum = k.sb("fcum", [128, 16, 8], F32)
    fend = k.sb("fend", [128, 4, 8], F32)
    lfsum = k.sb("lfsum", [128, 8], F32)
    lf_s = k.sb("lf_s", [128, 8], F32)
    nfnew = k.sb("nfnew", [128, 8], F32)
    smalls = [k.sb("sm%d" % i, [128, 64], F32) for i in range(4)]
    R0 = k.sb_ptr
    RSIZE = k.sb_hi - R0

    ident = cst[:, C_ID:C_ID + 128]
    ones_f = cst[:, C_ONES:C_ONES + 128]
    ident_b = cb[:, 0:128]
    ones_b = cb[:, 128:256]
    bd_b = cb[:, 256:384]
    ut_b = cb[:, 384:512]

    stg_i = [0]

    def staging():
        s = stg[stg_i[0] % 3]
        stg_i[0] += 1
        return s

    sq_i = [0]

    def sqbuf():
        s = sq[sq_i[0] % 2]
        sq_i[0] += 1
        return s

    ev_i = [0]

    def ev_eng():
        ev_i[0] += 1
        return "act" if ev_i[0] % 2 else "dve"

    def evac(out, in_, eng=None):
        eng = eng or ev_eng()
        k.copy(eng, out, in_)

    def vcol(c):
        return vecs[:, c:c + 1]

    k.dma("sp", cst[:, :], cst_d[:, :])
    k.dma("sp", vecs[:, :], vecs_d[:, :])
    k.dma("sp", cint[:, :], cint_d[:, :])
    k.copy("dve", cb[:, 0:512], cst[:, 0:512])
    k.copy("dve", masks_b[:, :], cst[:, C_MASKS:C_MASKS + 128])

    def load_tok_major_T(src_rows, nrows, dst_fn):
        s = staging()
        k.dma("sp", s[0:nrows, :], src_rows)
        for half in range(2):
            bk = k.bank()
            for j in range(4):
                kk = half * 4 + j
                k.tr(bk[:, j * 128:j * 128 + nrows], s[0:nrows, kk * 128:(kk + 1) * 128], A(ident.ap[0:nrows, 0:nrows], ident.bufs))
            evac(dst_fn(half * 4), bk.v(lambda h: h[:, :].rearrange("p (j t) -> p j t", j=4)[:, :, 0:nrows]))
            k.release(bk)

    def rms_rstd(src_chunks, nch, n, dim, lhs_ones):
        rstd, lnb = rs_pair()
        bk = k.bank()
        for i in range(nch):
            s = sqbuf()
            k.act(s[:, 0:n], src_chunks(i), AF.Square)
            k.mm(bk[:, 0:n], lhs_ones, s[:, 0:n], start=(i == 0), stop=(i == nch - 1))
        k.act(lnb[:, 0:n], bk[:, 0:n], AF.Ln, scale=1.0 / dim, bias=EPS)
        k.act(rstd[:, 0:n], lnb[:, 0:n], AF.Exp, scale=-0.5)
        k.release(bk)
        return rstd

    def norm_tile(i, gcol, dst):
        n = TILES[i][1]
        rstd = rms_rstd(lambda kk: hT[i][:, kk, :], 8, n, float(D), ones_b)
        for kk in range(8):
            k.stt(dst[:, kk, 0:n], hT[i][:, kk, :], vcol(gcol + kk), rstd[:, 0:n], ALU.mult, ALU.mult)

    for i, (c0, n) in enumerate(TILES):
        for tcn in range(n // 128):
            if i < 4:
                src = xp[c0 + tcn * 128:c0 + (tcn + 1) * 128, :]
            else:
                src = xs[:, :]
            load_tok_major_T(src, 128, lambda k0, i=i, tcn=tcn: hT[i][:, k0:k0 + 4, tcn * 128:(tcn + 1) * 128])

    class Arena:
        def __init__(self, base=None, limit=None):
            self.p = R0 if base is None else base
            self.limit = k.sb_hi if limit is None else limit

        def alloc(self, name, shape, dt):
            nbytes = int(np.prod(shape[1:])) * mybir.dt.size(dt)
            assert self.p + nbytes <= self.limit, (name, self.p, nbytes, self.limit)
            t = k.sb(name, shape, dt, off=self.p)
            self.p = (self.p + nbytes + 31) // 32 * 32
            return t

    uid = [0]

    def uname(s):
        uid[0] += 1
        return "%s_%d" % (s, uid[0])

    def load_wsq(dst, w_ap):
        for half in range(2):
            k.dma("pool", dst[:, half * 4:(half + 1) * 4, :],
                  w_ap[half * 512:(half + 1) * 512, :].rearrange("(k p) n -> p k n", p=128))

    ar = Arena()
    wsq = [ar.alloc(uname("wsq"), [128, 8, D], BF16) for _ in range(2)]
    mT = ar.alloc(uname("mT"), [128, 8, NMEM], F32)
    mn = ar.alloc(uname("mn"), [128, 8, NMEM], BF16)
    ktok = ar.alloc(uname("ktok"), [128, 512], F32)
    ktmp = ar.alloc(uname("ktmp"), [128, 512], F32)
    kb16 = ar.alloc(uname("kb16"), [128, 512], BF16)

    for mc in range(2):
        load_tok_major_T(memp[mc * 128:(mc + 1) * 128, :], 128,
                         lambda k0, mc=mc: mT[:, k0:k0 + 4, mc * 128:(mc + 1) * 128])
    rstd = rms_rstd(lambda kk: mT[:, kk, :], 8, NMEM, float(D), ones_b)
    for l in range(2):
        load_wsq(wsq[l], w_mkv[l])
    for l in range(2):
        for kk in range(8):
            k.stt(mn[:, kk, :], mT[:, kk, :], vcol(V_GMEM + l * 8 + kk), rstd[:, 0:NMEM], ALU.mult, ALU.mult)
        for mc in range(2):
            bkK, bkV = k.bank(), k.bank()
            for kk in range(8):
                k.mm(bkK[:, :], mn[:, kk, mc * 128:(mc + 1) * 128], wsq[l][:, kk, 0:512], start=(kk == 0), stop=(kk == 7))
            for kk in range(8):
                k.mm(bkV[:, :], mn[:, kk, mc * 128:(mc + 1) * 128], wsq[l][:, kk, 512:1024], start=(kk == 0), stop=(kk == 7))
            sv = staging()
            k.copy("act", sv[:, 0:512], bkV[:, :])
            k.release(bkV)
            k.dma("sp", mvp[l, mc * 128:(mc + 1) * 128, :], sv[:, 0:512], is_out=True)
            k.copy("dve", mvb[l][:, mc, :], sv[:, 0:512])
            k.copy("act", ktok[:, :], bkK[:, :])
            k.release(bkK)
            k.tt("dve", ktmp[:, :], ktok[:, :], ktok[:, :], ALU.mult)
            k.red(small[:, 0:4], ktmp.v(lambda h: h[:, :].rearrange("p (h d) -> p h d", h=4)))
            k.act(small[:, 4:8], small[:, 0:4], AF.Ln, scale=1.0 / 128, bias=EPS)
            k.act(small[:, 8:12], small[:, 4:8], AF.Exp, scale=-0.5)
            k.tt("dve", ktmp.v(lambda h: h[:, :].rearrange("p (h d) -> p h d", h=4)),
                 ktok.v(lambda h: h[:, :].rearrange("p (h d) -> p h d", h=4)),
                 small.v(lambda h: h[:, 8:12].unsqueeze(2).to_broadcast([128, 4, 128])), ALU.mult)
            sk = staging()
            k.tt("dve", sk.v(lambda h: h[:, 0:512].rearrange("p (h d) -> p h d", h=4)),
                 ktmp.v(lambda h: h[:, :].rearrange("p (h d) -> p h d", h=4)),
                 vecs.v(lambda h: h[:, V_GMK + l * 128:V_GMK + (l + 1) * 128].unsqueeze(1).to_broadcast([128, 4, 128])), ALU.mult)
            k.dma("sp", mkp[l, mc * 128:(mc + 1) * 128, :], sk[:, 0:512], is_out=True)
            k.copy("act", kb16[:, :], sk[:, 0:512])
            bt = k.bank()
            btb = bt.v(lambda h: h[:, :].bitcast(BF16))
            for hh in range(4):
                k.tr(A(btb.ap[:, hh * 128:(hh + 1) * 128], btb.bufs), kb16[:, hh * 128:(hh + 1) * 128], ident_b)
            evac(mkT[l][:, :, mc * 128:(mc + 1) * 128],
                 A(btb.ap[:, 0:512].rearrange("p (h m) -> p h m", h=4), btb.bufs))
            k.release(bt)

    if stage <= 1:
        k.emit()
        return nc

    rsq128 = 1.0 / float(np.sqrt(128.0))

    def head_norm(bk, n, zm, dim, lhs_ones, gcol, dst):
        k.copy("act", zm[:, 0:n], bk[:, 0:n])
        k.release(bk)
        rstd = rms_rstd(lambda _: zm[:, 0:n], 1, n, float(dim), lhs_ones)
        k.stt(dst, zm[:, 0:n], vcol(gcol), rstd[:, 0:n], ALU.mult, ALU.mult)

    def mem_attn_prompt(l, n, qm, Pm, lnd, rden, cat):
        def stage_a(h):
            P = Pm[h % 2]
            for mc in range(2):
                bkS = k.bank()
                k.mm(bkS[:, 0:n], mkT[l][:, h, mc * 128:(mc + 1) * 128], qm[:, h, 0:n])
                k.act(P[:, mc, 0:n], bkS[:, 0:n], AF.Exp, scale=rsq128)
                k.release(bkS)

        def stage_b(h):
            P = Pm[h % 2]
            rden_, lnd_ = rs_pair()
            bkN, bkD = k.bank(), k.bank()
            for mc in range(2):
                k.mm(bkN[:, 0:n], mvb[l][:, mc, h * 128:(h + 1) * 128], P[:, mc, 0:n], start=(mc == 0), stop=(mc == 1))
            for mc in range(2):
                k.mm(bkD[:, 0:n], ones_b, P[:, mc, 0:n], start=(mc == 0), stop=(mc == 1))
            k.act(lnd_[:, 0:n], bkD[:, 0:n], AF.Ln)
            k.release(bkD)
            k.act(rden_[:, 0:n], lnd_[:, 0:n], AF.Exp, scale=-1.0)
            k.tt("dve", cat[:, 4 + h, 0:n], bkN[:, 0:n], rden_[:, 0:n], ALU.mult)
            k.release(bkN)

        stage_a(0)
        for h in range(4):
            if h + 1 < 4:
                stage_a(h + 1)
            stage_b(h)

    def mem_attn_sample(l, qm, cat, mks2, mksT, mvs2, Pms):
        def loads(b):
            k.dma("pool", mks2[b % 2][:, :, :], cmk[l, b].rearrange("(c p) f -> p c f", p=128))
            k.dma("pool", mvs2[b % 2][:, :, :], cmv[l, b].rearrange("(c p) f -> p c f", p=128))

        loads(0)
        for b in range(BPC):
            if b + 1 < BPC:
                loads(b + 1)
            mks, mvs = mks2[b % 2], mvs2[b % 2]
            bt = k.bank()
            btb = bt.v(lambda h: h[:, :].bitcast(BF16))
            for h in range(4):
                for mc in range(2):
                    o = h * 256 + mc * 128
                    k.tr(A(btb.ap[:, o:o + 128], btb.bufs), mks[:, mc, h * 128:(h + 1) * 128], ident_b)
            evac(mksT[:, :, :], A(btb.ap[:, 0:1024].rearrange("p (h m) -> p h m", h=4), btb.bufs))
            k.release(bt)
            bkS = k.bank()
            for mc in range(2):
                for h in range(4):
                    o = (mc * 4 + h) * 8
                    k.mm(bkS[:, o:o + 8], mksT[:, h, mc * 128:(mc + 1) * 128], qm[:, h, b * 8:(b + 1) * 8])
            k.act(Pms[:, 0:64], bkS[:, 0:64], AF.Exp, scale=rsq128)
            k.release(bkS)
            bkN = k.bank()
            first = True
            for h in range(4):
                for mc in range(2):
                    o = (mc * 4 + h) * 8
                    k.mm(bkN[:, h * 8:(h + 1) * 8], mvs[:, mc, h * 128:(h + 1) * 128], Pms[:, o:o + 8],
                         start=first, stop=(mc == 1), sgc=True)
                    first = False
            for mc in range(2):
                k.mm(bkN[:, 32:64], ones_b, Pms[:, mc * 32:(mc + 1) * 32], start=False, stop=(mc == 1), sgc=True)
            k.act(small[:, 64:96], bkN[:, 32:64], AF.Ln)
            k.act(small[:, 96:128], small[:, 64:96], AF.Exp, scale=-1.0)
            k.tt("dve", cat[:, 4:8, b * 8:(b + 1) * 8],
                 bkN.v(lambda h: h[:, 0:32].rearrange("p (h t) -> p h t", h=4)),
                 small.v(lambda h: h[:, 96:128].rearrange("p (h t) -> p h t", h=4)), ALU.mult)
            k.release(bkN)

    def out_proj(i, wo, cat):
        n = TILES[i][1]
        for c in range(8):
            bk = k.bank()
            for kk in range(8):
                k.mm(bk[:, 0:n], wo[:, kk, c * 128:(c + 1) * 128], cat[:, kk, 0:n], start=(kk == 0), stop=(kk == 7))
            k.tt("dve", hT[i][:, c, :], bk[:, 0:n], hT[i][:, c, :], ALU.add)
            k.release(bk)

    def mixer0():
        l = 0
        ar = Arena()
        wsq = [ar.alloc(uname("wsq"), [128, 8, D], BF16) for _ in range(2)]
        wpl = ar.alloc(uname("wpl"), [128, 4, 128], BF16)
        zte = ar.alloc(uname("zte"), [128, 4, 528], F32)
        sA = ar.alloc(uname("sA"), [128, 528], F32)
        sB = ar.alloc(uname("sB"), [128, 528], F32)
        yT = ar.alloc(uname("yT"), [128, 4, 512], BF16)
        zm = [ar.alloc(uname("zm"), [128, 512], F32) for _ in range(2)]
        qm = ar.alloc(uname("qm"), [128, 4, 512], BF16)
        Pm = [ar.alloc(uname("Pm"), [128, 2, 512], BF16) for _ in range(2)]
        lnd = rden = None
        ytmp = ar.alloc(uname("ytmp"), [128, 16], F32)
        a0 = Arena(xn[0].buf.lo, xn[0].buf.hi)
        zes = a0.alloc(uname("zes"), [128, 4, BPC, 24], F32)
        mks = a0.alloc(uname("mks"), [128, 2, 512], BF16)
        a1 = Arena(xn[1].buf.lo, xn[1].buf.hi)
        mksT = a1.alloc(uname("mksT"), [128, 4, NMEM], BF16)
        mvs = a1.alloc(uname("mvs"), [128, 2, 512], BF16)
        zst = a1.alloc(uname("zst"), [128, 4, 240], F32)
        Pms = a1.alloc(uname("Pms"), [128, 64], BF16)
        xr = [xn[2], xn[3]]
        mks2 = [mks, ar.alloc(uname("mksb"), [128, 2, 512], BF16)]
        mvs2 = [mvs, Arena(xn[4].buf.lo, xn[4].buf.hi).alloc(uname("mvsb"), [128, 2, 512], BF16)]

        load_wsq(wsq[0], w_in[l])
        load_wsq(wsq[1], w_out[l])
        k.dma("pool", wpl[:, :, :], w_pool.rearrange("g c d -> c g d"))
        k.memset("dve", zte[:, :, :], 0.0)
        k.memset("dve", zes[:, :, :, :], 0.0)
        s = staging()
        k.dma("sp", s[0:120, 0:512], spool[0:120, :])
        k.dma("sp", s[0:120, 512:1024], spool[120:240, :])
        for hb in range(2):
            bk = k.bank()
            for g in range(4):
                k.tr(bk[:, g * 128:g * 128 + 120], s[0:120, hb * 512 + g * 128:hb * 512 + (g + 1) * 128],
                     A(ident.ap[0:120, 0:120], ident.bufs))
            for g in range(4):
                evac(zes[:, g, hb * 8:(hb + 1) * 8, 1:16],
                     bk.v(lambda h, g=g: h[:, g * 128:g * 128 + 120].rearrange("p (b j) -> p b j", b=8)))
            k.release(bk)

        ksub = int(os.environ.get('KSUB', '99'))
        if ksub <= 1:
            return
        order0 = (4, 0, 1, 2, 3)
        norm_tile(order0[0], V_GMIX + l * 8, xr[0])
        for ti, i in enumerate(order0):
            n = TILES[i][1]
            samp = (i == 4)
            X = xr[ti % 2]
            def fin0(bk, c):
                if c < 4:
                    if samp:
                        evac(zes[:, c, :, 16:24], bk.v(lambda h: h[:, 0:128].rearrange("p (b t) -> p b t", b=BPC)))
                    else:
                        evac(zte[:, c, 16:528], bk[:, 0:512])
                    k.release(bk)
                else:
                    head_norm(bk, n, zm[c % 2], 128, ones_b, V_GMQ + l, qm[:, c - 4, 0:n])

            pend = []
            for c in range(8):
                bk = k.bank()
                for kk in range(8):
                    k.mm(bk[:, 0:n], wsq[0][:, kk, c * 128:(c + 1) * 128], X[:, kk, 0:n], start=(kk == 0), stop=(kk == 7))
                pend.append((bk, c))
                if len(pend) > 2:
                    fin0(*pend.pop(0))
            while pend:
                fin0(*pend.pop(0))
            if ksub <= 2:
                return
            cat = X
            for g in range(4):
                w = 2 << g
                eng = "pool" if g < 3 else "dve"
                if samp:
                    zz = lambda a, b_, g=g: zes[:, g, :, a:b_]
                    SA = lambda a, b_: sA.v(lambda h: h[:, 0:384].rearrange("p (b c) -> p b c", b=BPC)[:, :, a:b_])
                    SB = lambda a, b_: sB.v(lambda h: h[:, 0:384].rearrange("p (b c) -> p b c", b=BPC)[:, :, a:b_])
                    W_ = 24
                else:
                    zz = lambda a, b_, g=g: zte[:, g, a:b_]
                    SA = lambda a, b_: sA[:, a:b_]
                    SB = lambda a, b_: sB[:, a:b_]
                    W_ = 528
                k.tt(eng, SA(1, W_), zz(1, W_), zz(0, W_ - 1), ALU.add)
                S = SA
                if g >= 1:
                    k.tt(eng, SB(3, W_), SA(3, W_), SA(1, W_ - 2), ALU.add)
                    S = SB
                if g >= 2:
                    k.tt(eng, SA(7, W_), SB(7, W_), SB(3, W_ - 4), ALU.add)
                    S = SA
                if g >= 3:
                    k.tt(eng, SB(15, W_), SA(15, W_), SA(7, W_ - 8), ALU.add)
                    S = SB
                if samp:
                    k.stt(yT.v(lambda h, g=g: h[:, g, 0:128].rearrange("p (b t) -> p b t", b=BPC)),
                          S(16, 24), 1.0 / w, zz(16, 24), ALU.mult, ALU.subtract)
                else:
                    k.stt(yT[:, g, 0:512], S(16, 528), 1.0 / w, zz(16, 528), ALU.mult, ALU.subtract)
                    if i == 0:
                        k.tt("dve", ytmp[:, :], S(16, 32), cst[:, C_INVC + g * 16:C_INVC + (g + 1) * 16], ALU.mult)
                        k.tt("dve", yT[:, g, 0:16], ytmp[:, :], zz(16, 32), ALU.subtract)
                bk = k.bank()
                k.mm(bk[:, 0:n], wpl[:, g, :], yT[:, g, 0:n])
                k.act(cat[:, g, 0:n], bk[:, 0:n], AF.Identity, scale=vcol(V_PSC + g))
                k.release(bk)
            if ksub <= 3:
                return
            if samp:
                for g in range(4):
                    k.copy("dve", zst.v(lambda h, g=g: h[:, g, :].rearrange("p (b j) -> p b j", b=BPC)), zes[:, g, :, 9:24])
                for hb in range(2):
                    bk = k.bank()
                    for g in range(4):
                        k.tr(bk[0:120, g * 128:(g + 1) * 128], zst[:, g, hb * 120:(hb + 1) * 120], ident)
                    so = staging()
                    k.copy("act", so[0:120, 0:512], bk[0:120, :])
                    k.release(bk)
                    k.dma("sp", pss[hb * 120:(hb + 1) * 120, :], so[0:120, 0:512], is_out=True)
            else:
                if i == 3:
                    bk = k.bank()
                    for g in range(4):
                        k.copy("dve", zst[:, g, 0:15], zte[:, g, 513:528])
                        k.tr(bk[:, g * 128:(g + 1) * 128], zst[:, g, 0:128], ident)
                    so = staging()
                    k.copy("act", so[0:15, 0:512], bk[0:15, :])
                    k.release(bk)
                    k.dma("sp", psp[:, :], so[0:15, 0:512], is_out=True)
                else:
                    k.copy("dve", ytmp.v(lambda h: h[:, 0:16]), zte[:, 0, 512:528])
                    for g in range(4):
                        k.copy("dve", sA[:, 0:16], zte[:, g, 512:528])
                        k.copy("dve", zte[:, g, 0:16], sA[:, 0:16])
            if ksub <= 4:
                return
            if samp:
                mem_attn_sample(l, qm, cat, mks2, mksT, mvs2, Pms)
            else:
                mem_attn_prompt(l, n, qm, Pm, lnd, rden, cat)
            if ksub <= 5:
                return
            if ti + 1 < len(order0):
                norm_tile(order0[ti + 1], V_GMIX + l * 8, xr[(ti + 1) % 2])
            out_proj(i, wsq[1], cat)
            if ksub <= 6:
                return

    mixer0()
    if stage <= 2:
        k.emit()
        return nc

    sm_i = [0]

    def smbuf():
        sm_i[0] += 1
        return smalls[sm_i[0] % 4]

    def ffn(l):
        ar = Arena()
        aT = [ar.alloc(uname("aT"), [128, 4, n], BF16) for (_, n) in TILES]
        wgu = [ar.alloc(uname("wgu"), [128, 8, 2, 256], BF16) for _ in range(3)]
        wdn = [ar.alloc(uname("wdn"), [128, 4, D], BF16) for _ in range(2)]
        sg = [ar.alloc(uname("sg"), [128, 512], F32) for _ in range(2)]
        for i in range(NT):
            norm_tile(i, V_GFFN + l * 8, xn[i])
        groups = [(0, 4), (4, 4), (8, 4), (12, 4), (16, 4), (20, 2)]
        blk = 0
        cnt = 0
        for gi, (j0, nj) in enumerate(groups):
            wd = wdn[gi % 2]
            k.dma("pool", wd[:, 0:nj, :], w_down[l][j0 * 128:(j0 + nj) * 128, :].rearrange("(j p) n -> p j n", p=128))
            for jb in range(nj // 2):
                j = j0 + 2 * jb
                ws = wgu[blk % 3]
                blk += 1
                k.dma("pool", ws[:, :, 0, :], w_gu[l][:, j * 128:j * 128 + 256].rearrange("(k p) n -> p k n", p=128))
                k.dma("pool", ws[:, :, 1, :], w_gu[l][:, DFF + j * 128:DFF + j * 128 + 256].rearrange("(k p) n -> p k n", p=128))
                for i in range(NT):
                    n = TILES[i][1]
                    for jj in range(2):
                        bkG = k.bank()
                        for kk in range(8):
                            k.mm(bkG[:, 0:n], ws[:, kk, 0, jj * 128:(jj + 1) * 128], xn[i][:, kk, :], start=(kk == 0), stop=(kk == 7))
                        bkU = k.bank()
                        for kk in range(8):
                            k.mm(bkU[:, 0:n], ws[:, kk, 1, jj * 128:(jj + 1) * 128], xn[i][:, kk, :], start=(kk == 0), stop=(kk == 7))
                        s_ = sg[cnt % 2]
                        cnt += 1
                        k.act(s_[:, 0:n], bkG[:, 0:n], AF.Silu)
                        k.release(bkG)
                        k.tt("dve", aT[i][:, 2 * jb + jj, :], bkU[:, 0:n], s_[:, 0:n], ALU.mult)
                        k.release(bkU)
            for i in range(NT):
                n = TILES[i][1]
                for c in range(8):
                    bk = k.bank()
                    for jj in range(nj):
                        k.mm(bk[:, 0:n], wd[:, jj, c * 128:(c + 1) * 128], aT[i][:, jj, :], start=(jj == 0), stop=(jj == nj - 1))
                    k.tt("dve", hT[i][:, c, :], bk[:, 0:n], hT[i][:, c, :], ALU.add)
                    k.release(bk)

    ffn(0)

    axn = Arena(xn[0].buf.lo, xn[4].buf.hi)
    KTp = [axn.alloc(uname("KTp"), [128, 4, 512], BF16) for _ in range(4)]
    Vp = [axn.alloc(uname("Vp"), [128, 4, 8, 65], BF16) for _ in range(4)]

    def shared_kv():
        ar = Arena()
        wkv = ar.alloc(uname("wkv"), [128, 8, D], BF16)
        wfg = ar.alloc(uname("wfg"), [128, 8, 8], BF16)
        xk = ar.alloc(uname("xk"), [128, 8, 512], BF16)
        ktoks = [ar.alloc(uname("ktok"), [128, 512], F32) for _ in range(2)]
        ktmps = [ar.alloc(uname("ktmp"), [128, 512], F32) for _ in range(2)]
        kb16s = [ar.alloc(uname("kb16"), [128, 512], BF16) for _ in range(2)]
        chunk_i = [0]
        load_wsq(wkv, w_kv)
        k.dma("pool", wfg[:, :, :], w_fg.rearrange("(k p) h -> p k h", p=128))
        for i in range(4):
            k.memset("dve", Vp[i][:, :, :, 64:65], 1.0)
        h3 = lambda t, a=0, b_=512: t.v(lambda h: h[:, a:b_].rearrange("p (h d) -> p h d", h=8))
        for i in (4, 0, 1, 2, 3):
            n = TILES[i][1]
            samp = (i == 4)
            norm_tile(i, V_GKV, xk)
            for tcn in range(n // 128):
                gc = i * 4 + tcn
                r0 = TILES[i][0] + tcn * 128
                cs = slice(tcn * 128, (tcn + 1) * 128)
                chunk_i[0] += 1
                ktok, ktmp, kb16 = ktoks[chunk_i[0] % 2], ktmps[chunk_i[0] % 2], kb16s[chunk_i[0] % 2]
                bkK, bkV, bkF = k.bank(), k.bank(), k.bank()
                for kk in range(8):
                    k.mm(bkK[:, :], xk[:, kk, cs], wkv[:, kk, 0:512], start=(kk == 0), stop=(kk == 7))
                for kk in range(8):
                    k.mm(bkV[:, :], xk[:, kk, cs], wkv[:, kk, 512:1024], start=(kk == 0), stop=(kk == 7))
                for kk in range(8):
                    k.mm(bkF[:, 0:8], xk[:, kk, cs], wfg[:, kk, :], start=(kk == 0), stop=(kk == 7))
                sm = smbuf()
                sv = staging()
                k.copy("act", sv[:, 0:512], bkV[:, :])
                k.release(bkV)
                if samp:
                    k.dma("sp", vrs[:, :], sv[:, 0:512], is_out=True)
                    k.copy("dve", Vs[:, :], sv[:, 0:512])
                else:
                    k.dma("sp", vrp[r0:r0 + 128, :], sv[:, 0:512], is_out=True)
                    k.copy("dve", Vp[i][:, tcn, :, 0:64], h3(sv))
                k.copy("act", ktok[:, :], bkK[:, :])
                k.release(bkK)
                k.tt("dve", ktmp[:, :], ktok[:, :], ktok[:, :], ALU.mult)
                k.red(sm[:, 0:8], h3(ktmp))
                k.act(sm[:, 8:16], sm[:, 0:8], AF.Ln, scale=1.0 / 64, bias=EPS)
                k.act(sm[:, 16:24], sm[:, 8:16], AF.Exp, scale=-0.5)
                k.tt("dve", h3(ktmp), h3(ktok), sm.v(lambda h: h[:, 16:24].unsqueeze(2).to_broadcast([128, 8, 64])), ALU.mult)
                sk = staging()
                k.tt("dve", h3(sk), h3(ktmp), vecs.v(lambda h: h[:, V_GFK:V_GFK + 64].unsqueeze(1).to_broadcast([128, 8, 64])), ALU.mult)
                if samp:
                    k.dma("sp", krs[:, :], sk[:, 0:512], is_out=True)
                else:
                    k.dma("sp", krp[r0:r0 + 128, :], sk[:, 0:512], is_out=True)
                k.copy("act", kb16[:, :], sk[:, 0:512])
                bt = k.bank()
                btb = bt.v(lambda h: h[:, :].bitcast(BF16))
                for c in range(4):
                    k.tr(A(btb.ap[:, c * 128:(c + 1) * 128], btb.bufs), kb16[:, c * 128:(c + 1) * 128], ident_b)
                src = A(btb.ap[:, 0:512].rearrange("p (c t) -> p c t", c=4), btb.bufs)
                if samp:
                    evac(KTs[:, :, :], src)
                else:
                    evac(KTp[i][:, :, cs], src)
                k.release(bt)
                k.tt("dve", sm[:, 24:32], bkF[:, 0:8], vecs[:, V_BFG:V_BFG + 8], ALU.add)
                k.release(bkF)
                k.act(sm[:, 32:40], sm[:, 24:32], AF.Exp, scale=-1.0)
                k.act(sm[:, 40:48], sm[:, 32:40], AF.Ln, bias=1.0)
                if samp:
                    k.ts("dve", lf_s[:, :], sm[:, 40:48], -1.0, ALU.mult)
                    k.dma("sp", lfs[:, :], lf_s[:, :], is_out=True)
                    bkc = k.bank()
                    k.mm(bkc[:, 0:8], cst[:, C_BT:C_BT + 128], lf_s[:, :])
                    k.ts("dve", nfnew[:, :], bkc[:, 0:8], -1.0, ALU.mult)
                    k.release(bkc)
                else:
                    k.ts("dve", lf_all[:, gc, :], sm[:, 40:48], -1.0, ALU.mult)
                    k.dma("sp", lfp[r0:r0 + 128, :], lf_all[:, gc, :], is_out=True)
                    bkc = k.bank()
                    k.mm(bkc[:, 0:8], cst[:, C_UT:C_UT + 128], lf_all[:, gc, :], start=True, stop=(gc == 0))
                    if gc > 0:
                        k.mm(bkc[:, 0:8], ones_f, lfsum[:, :], start=False, stop=True)
                    k.copy("act", fcum[:, gc, :], bkc[:, 0:8])
                    k.release(bkc)
                    if gc == 0:
                        k.copy("dve", lfsum[:, :], lf_all[:, 0, :])
                    else:
                        k.tt("dve", lfsum[:, :], lfsum[:, :], lf_all[:, gc, :], ALU.add)
                    if tcn == 3:
                        bke = k.bank()
                        k.mm(bke[:, 0:8], cst[:, C_E127:C_E127 + 128], fcum[:, gc, :])
                        k.copy("act", fend[:, i, :], bke[:, 0:8])
                        k.release(bke)

    shared_kv()
    if stage <= 3:
        k.emit()
        return nc

    def mixer1():
        l = 1
        ar = Arena()
        wsq = [ar.alloc(uname("wsq"), [128, 8, D], BF16) for _ in range(2)]
        X = ar.alloc(uname("X1"), [128, 8, 512], BF16)
        zm = [ar.alloc(uname("zm"), [128, 512], F32) for _ in range(2)]
        qm = ar.alloc(uname("qm"), [128, 4, 512], BF16)
        Pm = [ar.alloc(uname("Pm"), [128, 2, 512], BF16) for _ in range(2)]
        lnd = rden = None
        biasJ = ar.alloc(uname("biasJ"), [128, 16, 8], F32)
        p0 = ar.p
        qT = ar.alloc(uname("qT"), [128, 4, 512], BF16)
        PT = [ar.alloc(uname("PT"), [128, 512], BF16) for _ in range(3)]
        otok = ar.alloc(uname("otok"), [128, 4, 8, 64], BF16)
        arw = Arena(wsq[0].buf.lo, wsq[0].buf.hi)
        KTg = arw.alloc(uname("KTg"), [128, 4, 8, 128], BF16)
        mks = arw.alloc(uname("mks"), [128, 2, 512], BF16)
        mksT = arw.alloc(uname("mksT"), [128, 4, NMEM], BF16)
        mvs = arw.alloc(uname("mvs"), [128, 2, 512], BF16)
        qbd = arw.alloc(uname("qbd"), [128, 4, BPC, 16], BF16)
        axs = Arena(xn[0].buf.lo, xn[4].buf.hi)
        Kg = [axs.alloc(uname("Kg"), [128, 4, 512], BF16) for _ in range(4)]
        Vg = [axs.alloc(uname("Vg"), [128, 4, 512], BF16) for _ in range(4)]
        ars = Arena(p0)
        qTs = ars.alloc(uname("qTs"), [128, 4, NSAMP], BF16)
        ptb = ars.alloc(uname("ptb"), [128, BPC], I32)
        itmp = ars.alloc(uname("itmp"), [128, BPC], I32)
        idx8 = ars.alloc(uname("idx8"), [128, BPC], I32)
        idx32 = ars.alloc(uname("idx32"), [128, BPC, 4], I32)
        rg4 = ars.alloc(uname("rg4"), [128, 1], I32)
        lfgs = [ars.alloc(uname("lfg"), [128, 16, 8], F32) for _ in range(2)]
        biasbs = [ars.alloc(uname("biasb"), [128, 16, 8], F32) for _ in range(2)]
        tmpS = ars.alloc(uname("tmpS"), [128, 512], F32)
        Pg = ars.alloc(uname("Pg"), [128, 512], BF16)
        Pn = ars.alloc(uname("Pn"), [128, 64], BF16)
        Pms = ars.alloc(uname("Pms"), [128, 64], BF16)
        mks2 = [mks, ars.alloc(uname("mksb"), [128, 2, 512], BF16)]
        mvs2 = [mvs, ars.alloc(uname("mvsb"), [128, 2, 512], BF16)]

        load_wsq(wsq[0], w_in[l])
        load_wsq(wsq[1], w_out[l])

        def in_proj(i, n, qdst):
            norm_tile(i, V_GMIX + l * 8, X)
            def fin1(bk, c):
                if c < 4:
                    head_norm(bk, n, zm[c % 2], 64, bd_b, V_GFQ, qdst[:, c, 0:n])
                else:
                    head_norm(bk, n, zm[c % 2], 128, ones_b, V_GMQ + l, qm[:, c - 4, 0:n])

            pend = []
            for c in range(8):
                bk = k.bank()
                for kk in range(8):
                    k.mm(bk[:, 0:n], wsq[0][:, kk, c * 128:(c + 1) * 128], X[:, kk, 0:n], start=(kk == 0), stop=(kk == 7))
                pend.append((bk, c))
                if len(pend) > 2:
                    fin1(*pend.pop(0))
            while pend:
                fin1(*pend.pop(0))

        ks1 = float(os.environ.get('KSUB1', '99'))
        for J in range(4):
            n = 512
            in_proj(J, n, qT)
            if ks1 <= 1:
                return
            cat = X
            nk = 4 * J + 4
            k.tt("dve", biasJ[:, 0:nk, :], fend.v(lambda h: h[:, J, :].unsqueeze(1).to_broadcast([128, nk, 8])),
                 fcum[:, 0:nk, :], ALU.subtract)
            for h in range(8):
                c, hp = h // 2, slice((h % 2) * 64, (h % 2) * 64 + 64)
                acc = k.bank()
                accv = acc.v(lambda hh: hh[:, 0:260].rearrange("p (q e) -> p q e", q=4))
                banks = {}

                def S_E(kc):
                    ti, tcn = kc // 4, kc % 4
                    nq0 = max(0, kc - 4 * J) * 128
                    bkS = k.bank()
                    k.mm(bkS[:, nq0:512], A(KTp[ti].h[hp, c, tcn * 128:(tcn + 1) * 128], [KTp[ti].buf]),
                         A(qT.h[hp, c, nq0:512], [qT.buf]))
                    P = PT[kc % 3]
                    k.act(P[:, nq0:512], bkS[:, nq0:512], AF.Exp, scale=0.125, bias=biasJ[:, kc, h:h + 1])
                    k.release(bkS)
                    if kc >= 4 * J:
                        k.tt("dve", P[:, nq0:nq0 + 128], P[:, nq0:nq0 + 128], ut_b, ALU.mult)

                S_E(0)
                if nk > 1:
                    S_E(1)
                first = True
                for kc in range(nk):
                    if kc + 2 < nk:
                        S_E(kc + 2)
                    ti, tcn = kc // 4, kc % 4
                    nq0 = max(0, kc - 4 * J) * 128
                    P = PT[kc % 3]
                    for tq in range(nq0 // 128, 4):
                        k.mm(A(accv.ap[:, tq, :], accv.bufs), P[:, tq * 128:(tq + 1) * 128], Vp[ti][:, tcn, h, :],
                             start=first, stop=(kc == 4 * J + tq), sgc=True)
                        first = False
                sm = smbuf()
                k.recip(sm.v(lambda hh: hh[:, 0:4].unsqueeze(2)), A(accv.ap[:, :, 64:65], accv.bufs))
                k.tt("dve", otok[:, :, h, :], A(accv.ap[:, :, 0:64], accv.bufs),
                     sm.v(lambda hh: hh[:, 0:4].unsqueeze(2).to_broadcast([128, 4, 64])), ALU.mult)
                k.release(acc)
            if ks1 <= 2:
                return
            for tp in range(2):
                bt = k.bank()
                btb = bt.v(lambda hh: hh[:, :].bitcast(BF16))
                for tq2 in range(2):
                    tq = tp * 2 + tq2
                    for c in range(4):
                        o = (tq2 * 4 + c) * 128
                        k.tr(A(btb.ap[:, o:o + 128], btb.bufs),
                             otok.v(lambda hh, tq=tq, c=c: hh[:, tq, 2 * c:2 * c + 2, :].rearrange("p a d -> p (a d)")), ident_b)
                for tq2 in range(2):
                    tq = tp * 2 + tq2
                    evac(cat[:, 0:4, tq * 128:(tq + 1) * 128],
                         A(btb.ap[:, tq2 * 512:(tq2 + 1) * 512].rearrange("p (c t) -> p c t", c=4), btb.bufs))
                k.release(bt)
            mem_attn_prompt(l, n, qm, Pm, lnd, rden, cat)
            out_proj(J, wsq[1], cat)
            if ks1 <= 3:
                return
        if ks1 <= 4:
            return

        n = NSAMP
        in_proj(4, n, qTs)
        cat = X
        k.memset("dve", qbd[:, :, :, :], 0.0)
        for c in range(4):
            k.copy("dve", A(qbd.h[0:64, c, :, 0:8], [qbd.buf]),
                   A(qTs.h[0:64, c, :].rearrange("p (b t) -> p b t", b=BPC), [qTs.buf]))
            k.copy("dve", A(qbd.h[64:128, c, :, 8:16], [qbd.buf]),
                   A(qTs.h[64:128, c, :].rearrange("p (b t) -> p b t", b=BPC), [qTs.buf]))
        k.dma("sp", ptb[:, :], ptab[:, :])
        k.ts("pool", itmp[:, :], ptb[:, :], 8, ALU.mult)
        k.tt("pool", idx8[:, :], itmp[:, :], cint.v(lambda hh: hh[:, 0:1].to_broadcast([128, BPC])), ALU.add)
        k.ts("pool", rg4[:, :], cint[:, :], 4, ALU.mult)
        k.ts("pool", itmp[:, :], ptb[:, :], 32, ALU.mult)
        k.tt("pool", idx32[:, :, 0], itmp[:, :], rg4.v(lambda hh: hh[:, 0:1].to_broadcast([128, BPC])), ALU.add)
        for j in range(1, 4):
            k.ts("pool", idx32[:, :, j], idx32[:, :, 0], j, ALU.add)
        if ks1 <= 5:
            return
        for b in range(BPC):
            lfg, biasb = lfgs[b % 2], biasbs[b % 2]
            k.gather(lfg.v(lambda hh: hh[:, :, :].rearrange("p r h -> p (r h)")), clf, idx8[:, b:b + 1])
            for r in range(1, 16):
                k.tt("dve", lfg[:, r, :], lfg[:, r, :], lfg[:, r - 1, :], ALU.add)
            bkb = k.bank()
            k.mm(bkb[:, 0:8], cst[:, C_GE:C_GE + 128], lfg[:, 15, :])
            k.tt("dve", biasb[:, :, :], bkb.v(lambda hh: hh[:, 0:8].unsqueeze(1).to_broadcast([128, 16, 8])),
                 lfg[:, :, :], ALU.subtract)
            k.release(bkb)
            if ks1 <= 6:
                return
            bkO = k.bank()
            first = [True, True]
            for hf in range(2):
                for jj in range(2):
                    j = hf * 2 + jj
                    k.gather(Kg[j].v(lambda hh: hh[:, :, :].rearrange("p r f -> p (r f)")), ck, idx32[:, b, j:j + 1])
                    k.gather(Vg[j].v(lambda hh: hh[:, :, :].rearrange("p r f -> p (r f)")), cv, idx32[:, b, j:j + 1])
                kgr = lambda r, hf=hf: Kg[hf * 2 + r // 4]
                vgr = lambda r, hf=hf: Vg[hf * 2 + r // 4]
                if ks1 <= 6.1:
                    return
                for c in range(4):
                    bt = k.bank()
                    btb = bt.v(lambda hh: hh[:, :].bitcast(BF16))
                    for r in range(8):
                        k.tr(A(btb.ap[:, r * 128:(r + 1) * 128], btb.bufs), kgr(r)[:, r % 4, c * 128:(c + 1) * 128], ident_b)
                    evac(KTg[:, c, :, :], A(btb.ap[:, 0:1024].rearrange("p (r s) -> p r s", r=8), btb.bufs))
                    k.release(bt)
                if ks1 <= 6.2:
                    return
                bkS = k.bank()
                for r in range(8):
                    for c in range(4):
                        o = (r * 4 + c) * 16
                        k.mm(bkS[:, o:o + 16], KTg[:, c, r, :], qbd[:, c, b, :])
                k.stt(tmpS.v(lambda hh: hh[:, :].rearrange("p (x t) -> p x t", t=8)),
                      bkS.v(lambda hh: hh[:, :].rearrange("p (x t) -> p x t", t=8)), 0.125,
                      biasb.v(lambda hh, hf=hf: hh[:, hf * 8:(hf + 1) * 8, :].rearrange("p r h -> p (r h)").unsqueeze(2).to_broadcast([128, 64, 8])),
                      ALU.mult, ALU.add)
                k.release(bkS)
                k.act(Pg[:, :], tmpS[:, :], AF.Exp)
                if ks1 <= 6.3:
                    return
                Pg5 = lambda r, c, hh_: A(Pg.h[:, (r * 8 + c * 2 + hh_) * 8:(r * 8 + c * 2 + hh_) * 8 + 8], [Pg.buf])
                for r in range(8):
                    for h in range(8):
                        c, hh_ = h // 2, h % 2
                        k.mm(A(bkO.h[hh_ * 64:(hh_ + 1) * 64, c * 8:(c + 1) * 8], [bkO.buf]),
                             vgr(r)[:, r % 4, h * 64:(h + 1) * 64], Pg5(r, c, hh_), start=first[hh_], stop=False, sgc=True)
                        first[hh_] = False
                    for hh_ in range(2):
                        k.mm(A(bkO.h[hh_ * 64:(hh_ + 1) * 64, 32:64].rearrange("p (c t) -> p c t", c=4), [bkO.buf]),
                             A(ones_b.ap[:, 0:64], ones_b.bufs),
                             A(Pg.h[:, r * 64:(r + 1) * 64].rearrange("p (c a t) -> p c a t", c=4, a=2)[:, :, hh_, :], [Pg.buf]),
                             start=False, stop=False, sgc=True)
            if ks1 <= 6.4:
                return
            bkS2 = k.bank()
            for c in range(4):
                k.mm(bkS2[:, c * 16:(c + 1) * 16], KTs[:, c, :], qbd[:, c, b, :])
            k.stt(tmpS.v(lambda hh: hh[:, 0:64].rearrange("p (x t) -> p x t", t=8)),
                  bkS2.v(lambda hh: hh[:, 0:64].rearrange("p (x t) -> p x t", t=8)), 0.125,
                  nfnew.v(lambda hh: hh[:, :].unsqueeze(2).to_broadcast([128, 8, 8])), ALU.mult, ALU.add)
            k.release(bkS2)
            k.act(Pn[:, :], tmpS[:, 0:64], AF.Exp)
            k.tt("dve", Pn.v(lambda hh: hh[:, :].rearrange("p (x t) -> p x t", t=8)),
                 Pn.v(lambda hh: hh[:, :].rearrange("p (x t) -> p x t", t=8)),
                 masks_b.v(lambda hh, b=b: hh[:, b * 8:(b + 1) * 8].unsqueeze(1).to_broadcast([128, 8, 8])), ALU.mult)
            for h in range(8):
                c, hh_ = h // 2, h % 2
                k.mm(A(bkO.h[hh_ * 64:(hh_ + 1) * 64, c * 8:(c + 1) * 8], [bkO.buf]),
                     Vs[:, h * 64:(h + 1) * 64], Pn[:, h * 8:(h + 1) * 8], start=False, stop=True, sgc=True)
            for hh_ in range(2):
                k.mm(A(bkO.h[hh_ * 64:(hh_ + 1) * 64, 32:64].rearrange("p (c t) -> p c t", c=4), [bkO.buf]),
                     A(ones_b.ap[:, 0:64], ones_b.bufs),
                     A(Pn.h[:, :].rearrange("p (c a t) -> p c a t", c=4, a=2)[:, :, hh_, :], [Pn.buf]),
                     start=False, stop=True, sgc=True)
            sm = smbuf()
            k.act(sm[:, 0:32], bkO[:, 32:64], AF.Ln)
            k.act(sm[:, 32:64], sm[:, 0:32], AF.Exp, scale=-1.0)
            k.tt("dve", cat[:, 0:4, b * 8:(b + 1) * 8],
                 bkO.v(lambda hh: hh[:, 0:32].rearrange("p (c t) -> p c t", c=4)),
                 sm.v(lambda hh: hh[:, 32:64].rearrange("p (c t) -> p c t", c=4)), ALU.mult)
            k.release(bkO)
            if ks1 <= 7:
                return
        mem_attn_sample(l, qm, cat, mks2, mksT, mvs2, Pms)
        out_proj(4, wsq[1], cat)

    mixer1()
    if float(os.environ.get('KSUB1', '99')) < 99:
        k.emit()
        return nc
    ffn(1)

    for i, (c0, n) in enumerate(TILES):
        for tcn in range(n // 128):
            so = staging()
            for half in range(2):
                bk = k.bank()
                for j in range(4):
                    k.tr(bk[:, j * 128:(j + 1) * 128], hT[i][:, half * 4 + j, tcn * 128:(tcn + 1) * 128], ident)
                evac(so[:, half * 512:(half + 1) * 512], bk[:, :])
                k.release(bk)
            if i < 4:
                k.dma("sp", y_p[c0 + tcn * 128:c0 + (tcn + 1) * 128, :], so[:, :], is_out=True)
            else:
                k.dma("sp", y_s[:, :], so[:, :], is_out=True)
    k.emit()
    return nc


def _consts():
    c = np.zeros((128, NCST), np.float32)
    p = np.arange(128)
    c[:, C_ID:C_ID + 128] = np.eye(128, dtype=np.float32)
    c[:, C_ONES:C_ONES + 128] = 1.0
    c[:, C_BD:C_BD + 128] = (p[:, None] // 64 == p[None, :] // 64)
    c[:, C_UT:C_UT + 128] = (p[:, None] <= p[None, :])
    c[127, C_E127:C_E127 + 128] = 1.0
    c[:, C_GE:C_GE + 128] = (p[:, None] >= p[None, :])
    c[:, C_BT:C_BT + 128] = (p[:, None] // 8 == p[None, :] // 8) & (p[:, None] <= p[None, :])
    for g, w in enumerate((2, 4, 8, 16)):
        t = np.arange(16)
        c[:, C_INVC + g * 16:C_INVC + (g + 1) * 16] = 1.0 / np.minimum(w, t + 1)
    m = np.zeros((128, 16, 8), np.float32)
    for b in range(16):
        for t in range(8):
            m[:, b, t] = (p // 8 == b) & (p % 8 <= t)
    c[:, C_MASKS:C_MASKS + 128] = m.reshape(128, 128)
    ci = (p % 8).astype(np.int32).reshape(128, 1)
    return c, ci


def _vecs(g_mix, g_ffn, g_mem, g_kv, g_mem_q, g_mem_k, pool_scale, g_fox_k, g_fox_q, b_fg):
    v = np.zeros((128, NV), np.float32)
    fm = lambda a: np.ascontiguousarray(a.reshape(8, 128).T)
    for l in range(2):
        v[:, V_GMIX + l * 8:V_GMIX + (l + 1) * 8] = fm(g_mix[l])
        v[:, V_GFFN + l * 8:V_GFFN + (l + 1) * 8] = fm(g_ffn[l])
        v[:, V_GMEM + l * 8:V_GMEM + (l + 1) * 8] = fm(g_mem[l])
        v[:, V_GMQ + l] = g_mem_q[l]
        v[:, V_GMK + l * 128:V_GMK + (l + 1) * 128] = np.broadcast_to(g_mem_k[l][None, :], (128, 128))
    v[:, V_GKV:V_GKV + 8] = fm(g_kv)
    v[:, V_PSC:V_PSC + 4] = pool_scale[0].reshape(4, 128).T
    v[:, V_GFQ] = np.tile(g_fox_q[0], 2)
    v[:, V_GFK:V_GFK + 64] = np.broadcast_to(g_fox_k[None, :], (128, 64))
    v[:, V_BFG:V_BFG + 8] = np.broadcast_to(b_fg[None, :], (128, 8))
    return v


_NC_CACHE = {}


def kernel(x_prompt, x_sample, cache_mem_k, cache_mem_v, state_pool, cache_k, cache_v, cache_logf,
           page_table, mem_prompt, g_mix, w_in, w_out, g_ffn, w_gu, w_down, g_mem, w_mem_kv,
           g_mem_q, g_mem_k, w_pool, pool_scale, g_kv, w_kv, g_fox_k, w_fg, b_fg, g_fox_q,
           _stage=99):
    f = lambda a: np.ascontiguousarray(np.asarray(a, dtype=np.float32))
    x_prompt, x_sample, mem_prompt = f(x_prompt), f(x_sample), f(mem_prompt)
    cache_mem_k, cache_mem_v, state_pool = f(cache_mem_k), f(cache_mem_v), f(state_pool)
    cache_k, cache_v, cache_logf = f(cache_k), f(cache_v), f(cache_logf)
    page_table = np.ascontiguousarray(np.asarray(page_table, dtype=np.int32))
    nphys = cache_k.shape[0]
    key = (_stage, nphys)
    if key not in _NC_CACHE:
        _NC_CACHE[key] = build_program(_stage, nphys)
    nc = _NC_CACHE[key]
    cst, cint = _consts()
    vecs = _vecs(f(g_mix), f(g_ffn), f(g_mem), f(g_kv), f(g_mem_q), f(g_mem_k), f(pool_scale),
                 f(g_fox_k), f(g_fox_q), f(b_fg))
    ck = cache_k.reshape(nphys * 32, 2048)
    cv = cache_v.reshape(nphys * 32, 2048)
    clf = cache_logf.reshape(nphys * 8, 16 * 8)
    shared = {
        "ck": ck, "cv": cv, "clf": clf,
        "w_in": f(w_in), "w_out": f(w_out), "w_gu": f(w_gu), "w_down": f(w_down), "w_mkv": f(w_mem_kv),
        "w_pool": f(w_pool).reshape(4, 128, 128), "w_kv": f(w_kv), "w_fg": f(w_fg),
        "vecs": vecs, "cst": cst, "cint": cint,
    }
    in_maps = []
    for c in range(NB):
        m = dict(shared)
        bs = slice(c * BPC, (c + 1) * BPC)
        m["xp"] = x_prompt[c]
        m["xs"] = x_sample[bs].reshape(NSAMP, D)
        m["memp"] = mem_prompt[c]
        m["cmk"] = cache_mem_k[:, bs].reshape(2, BPC, NMEM, 512)
        m["cmv"] = cache_mem_v[:, bs].reshape(2, BPC, NMEM, 512)
        m["spool"] = state_pool[0, bs].reshape(BPC * 15, 512)
        m["ptab"] = np.ascontiguousarray(np.repeat(page_table[bs].T, 8, axis=0))
        in_maps.append(m)
    ncores = int(os.environ.get("KCORES", NB))
    res = run_bass_kernel_spmd(nc, in_maps[:ncores], core_ids=list(range(ncores)))
    r = list(res.results)
    while len(r) < NB:
        r.append({kk: np.zeros_like(np.asarray(v)) for kk, v in r[0].items()})
    cat = lambda name: [np.asarray(r[c][name]) for c in range(NB)]
    y_prompt = np.stack(cat("y_p")).reshape(NB, SEQ, D)
    y_sample = np.concatenate(cat("y_s")).reshape(DB, DS, D)
    k_rows_prompt = np.stack(cat("krp")).reshape(NB, SEQ // PAGE, PAGE, 8, 64)
    v_rows_prompt = np.stack(cat("vrp")).reshape(NB, SEQ // PAGE, PAGE, 8, 64)
    logf_rows_prompt = np.stack(cat("lfp")).reshape(NB, SEQ // PAGE, PAGE, 8)
    mem_k_prompt = np.stack(cat("mkp"), axis=1).reshape(2, NB, NMEM, 4, 128)
    mem_v_prompt = np.stack(cat("mvp"), axis=1).reshape(2, NB, NMEM, 4, 128)
    pool_state_prompt = np.stack(cat("psp")).reshape(1, NB, 15, 512)
    k_rows_sample = np.concatenate(cat("krs")).reshape(DB, DS, 8, 64)
    v_rows_sample = np.concatenate(cat("vrs")).reshape(DB, DS, 8, 64)
    logf_rows_sample = np.concatenate(cat("lfs")).reshape(DB, DS, 8)
    pool_state_sample = np.concatenate(cat("pss")).reshape(1, DB, 15, 512)
    return (y_prompt, y_sample, k_rows_prompt, v_rows_prompt, logf_rows_prompt, mem_k_prompt,
            mem_v_prompt, pool_state_prompt, k_rows_sample, v_rows_sample, logf_rows_sample,
            pool_state_sample)
```
